# Optimizing a Trainium2 kernel written in Bass

```python
import math
import jax, jax.numpy as jnp
from jax import lax
import numpy as np

D_MODEL = 2048
BATCH = 2
SEQ = 16384
DEPTH = 4

N_MIXERS = 3
N_A = (DEPTH + 2) // 3
N_B = (DEPTH + 1) // 3
N_C = DEPTH // 3
D_FF = 5632
EPS = 1e-6
NA_HEAD_DIM = 32
NA_HEADS = D_MODEL // NA_HEAD_DIM
GRID_W = 64
WIN_H = 8
WIN_W = 16
CHUNK = 128
D_SGU = 2 * D_MODEL
SGU_GROUPS = 16
SGU_GROUP_DIM = D_SGU // SGU_GROUPS
POOL_WINDOWS = (2, 4, 8, 16)
POOL_GROUPS = len(POOL_WINDOWS)
POOL_GROUP_DIM = D_MODEL // POOL_GROUPS

kernel_name = "hybrid_natten_sgu_pool_macaron"


def rmsnorm(x, g):
    xf = x.astype(jnp.float32)
    y = xf * lax.rsqrt(jnp.mean(xf * xf, axis=-1, keepdims=True) + EPS)
    return (y * g.astype(jnp.float32)).astype(x.dtype)


def swiglu(h, w_gate, w_up, w_down):
    return (jax.nn.silu(h @ w_gate) * (h @ w_up)) @ w_down


def neighbourhood_attention(h, w_qkv, q_gain, k_gain, rpb, w_o):
    B, T, D = h.shape
    rows = T // GRID_W
    kh = min(WIN_H, rows)
    qkv = h @ w_qkv
    q, k, v = jnp.split(qkv, 3, axis=-1)
    shp = (B, rows, GRID_W, NA_HEADS, NA_HEAD_DIM)
    q = rmsnorm(q.reshape(shp), q_gain) * (NA_HEAD_DIM ** -0.5)
    k = rmsnorm(k.reshape(shp), k_gain)
    v = v.reshape(shp)
    cols = np.arange(GRID_W)
    col_start = np.clip(cols - WIN_W // 2, 0, GRID_W - WIN_W)
    col_idx = col_start[:, None] + np.arange(WIN_W)[None, :]
    dc_idx = col_idx - cols[:, None] + (WIN_W - 1)

    def row_block(r):
        rs = jnp.clip(r - kh // 2, 0, rows - kh)
        k_rows = lax.dynamic_slice_in_dim(k, rs, kh, axis=1)
        v_rows = lax.dynamic_slice_in_dim(v, rs, kh, axis=1)
        k_win = k_rows[:, :, col_idx]
        v_win = v_rows[:, :, col_idx]
        q_r = lax.dynamic_index_in_dim(q, r, axis=1, keepdims=False)
        s = jnp.einsum('bchd,bicjhd->bhcij', q_r, k_win).astype(jnp.float32)
        dr_idx = rs + jnp.arange(kh) - r + (WIN_H - 1)
        bias = rpb[:, dr_idx[:, None, None], dc_idx[None, :, :]]
        s = s + jnp.transpose(bias, (0, 2, 1, 3))[None].astype(jnp.float32)
        p = jax.nn.softmax(s.reshape(B, NA_HEADS, GRID_W, kh * WIN_W), axis=-1)
        p = p.reshape(B, NA_HEADS, GRID_W, kh, WIN_W).astype(v.dtype)
        return jnp.einsum('bhcij,bicjhd->bchd', p, v_win)

    out = lax.map(row_block, jnp.arange(rows))
    out = jnp.transpose(out, (1, 0, 2, 3, 4)).reshape(B, T, D)
    return out @ w_o


def spatial_gating(h, w_in, v_gain, w_s, b_s, w_out):
    B, T, D = h.shape
    z = jax.nn.gelu(h @ w_in, approximate=False)
    u, v = jnp.split(z, 2, axis=-1)
    v = rmsnorm(v, v_gain)
    v = v.reshape(B, T // CHUNK, CHUNK, SGU_GROUPS, SGU_GROUP_DIM)
    v = jnp.einsum('gij,bnjgd->bnigd', w_s, v) + jnp.transpose(b_s)[None, None, :, :, None]
    return (u * v.reshape(B, T, D_SGU)) @ w_out


def multiscale_pool(h, w_grp, scale):
    B, T, D = h.shape
    hf = h.astype(jnp.float32)
    cs = jnp.concatenate([jnp.zeros((B, 1, D), jnp.float32), jnp.cumsum(hf, axis=1)], axis=1)
    t = jnp.arange(T)
    parts = []
    for g, w in enumerate(POOL_WINDOWS):
        lo = jnp.clip(t - w // 2, 0, T)
        hi = jnp.clip(t + w // 2, 0, T)
        sl = slice(g * POOL_GROUP_DIM, (g + 1) * POOL_GROUP_DIM)
        cg = cs[..., sl]
        cnt = (hi - lo).astype(jnp.float32)[None, :, None]
        parts.append((cg[:, hi] - cg[:, lo]) / cnt - hf[..., sl])
    p = jnp.concatenate(parts, axis=-1).astype(h.dtype).reshape(B, T, POOL_GROUPS, POOL_GROUP_DIM)
    y = jnp.einsum('btgc,gcd->btgd', p, w_grp).reshape(B, T, D)
    return y * scale


def _w(key, shape, fan_in):
    return jax.random.normal(key, shape, jnp.float32) * (fan_in ** -0.5)


def _gain(key, shape):
    return 1.0 + 0.02 * jax.random.normal(key, shape, jnp.float32)


def setup_inputs(seed: int = 0) -> dict:
    key = jax.random.key(seed)
    ks = jax.random.split(key, 24)
    D = D_MODEL
    return {
        "x": jax.random.normal(ks[0], (BATCH, SEQ, D), jnp.float32),
        "ffn1_norm": _gain(ks[1], (DEPTH, D)),
        "ffn1_w_gate": _w(ks[2], (DEPTH, D, D_FF), D),
        "ffn1_w_up": _w(ks[3], (DEPTH, D, D_FF), D),
        "ffn1_w_down": _w(ks[4], (DEPTH, D_FF, D), D_FF),
        "mix_norm": _gain(ks[5], (DEPTH, D)),
        "ffn2_norm": _gain(ks[6], (DEPTH, D)),
        "ffn2_w_gate": _w(ks[7], (DEPTH, D, D_FF), D),
        "ffn2_w_up": _w(ks[8], (DEPTH, D, D_FF), D),
        "ffn2_w_down": _w(ks[9], (DEPTH, D_FF, D), D_FF),
        "out_norm": _gain(ks[10], (DEPTH, D)),
        "na_w_qkv": _w(ks[11], (N_A, D, 3 * D), D),
        "na_q_gain": _gain(ks[12], (N_A, NA_HEAD_DIM)),
        "na_k_gain": _gain(ks[13], (N_A, NA_HEAD_DIM)),
        "na_rpb": 0.1 * jax.random.normal(ks[14], (N_A, NA_HEADS, 2 * WIN_H - 1, 2 * WIN_W - 1), jnp.float32),
        "na_w_o": _w(ks[15], (N_A, D, D), D),
        "sgu_w_in": _w(ks[16], (N_B, D, 2 * D_SGU), D),
        "sgu_v_gain": _gain(ks[17], (N_B, D_SGU)),
        "sgu_w_s": _w(ks[18], (N_B, SGU_GROUPS, CHUNK, CHUNK), CHUNK),
        "sgu_b_s": _gain(ks[19], (N_B, SGU_GROUPS, CHUNK)),
        "sgu_w_out": _w(ks[20], (N_B, D_SGU, D), D_SGU),
        "pool_w": _w(ks[21], (N_C, POOL_GROUPS, POOL_GROUP_DIM, POOL_GROUP_DIM), POOL_GROUP_DIM),
        "pool_scale": _gain(ks[22], (N_C, D)),
    }


def reference(x, ffn1_norm, ffn1_w_gate, ffn1_w_up, ffn1_w_down, mix_norm,
              ffn2_norm, ffn2_w_gate, ffn2_w_up, ffn2_w_down, out_norm,
              na_w_qkv, na_q_gain, na_k_gain, na_rpb, na_w_o,
              sgu_w_in, sgu_v_gain, sgu_w_s, sgu_b_s, sgu_w_out,
              pool_w, pool_scale):
    for i in range(DEPTH):
        x = x + 0.5 * swiglu(rmsnorm(x, ffn1_norm[i]), ffn1_w_gate[i], ffn1_w_up[i], ffn1_w_down[i])
        h = rmsnorm(x, mix_norm[i])
        kind, j = i % N_MIXERS, i // N_MIXERS
        if kind == 0:
            y = neighbourhood_attention(h, na_w_qkv[j], na_q_gain[j], na_k_gain[j], na_rpb[j], na_w_o[j])
        elif kind == 1:
            y = spatial_gating(h, sgu_w_in[j], sgu_v_gain[j], sgu_w_s[j], sgu_b_s[j], sgu_w_out[j])
        else:
            y = multiscale_pool(h, pool_w[j], pool_scale[j])
        x = x + y
        x = x + 0.5 * swiglu(rmsnorm(x, ffn2_norm[i]), ffn2_w_gate[i], ffn2_w_up[i], ffn2_w_down[i])
        x = rmsnorm(x, out_norm[i])
    return x
```

```python
import contextlib
import numpy as np
import ml_dtypes
import concourse.bass as bass
import concourse.mybir as mybir
from concourse.bass_utils import run_bass_kernel_spmd

F32 = mybir.dt.float32
BF16 = mybir.dt.bfloat16
AF = mybir.ActivationFunctionType
ALU = mybir.AluOpType
AX = mybir.AxisListType

D = 2048
DFF = 5632
KC = D // 128
FC = DFF // 128
EPS = 1e-6
NSUB_MAX = 7


class _Sem:
    __slots__ = ("h", "cnt", "name")

    def __init__(self, name):
        self.h = None
        self.cnt = 0
        self.name = name


class _Eng:
    def __init__(self, name, ndma=0):
        self.name = name
        self.ops = []
        self.sem = _Sem("c_" + name)
        self.waited = {}
        self.dma_sems = [_Sem("d_%s%d" % (name, i)) for i in range(ndma)]
        self.rr = 0


class Prog:
    def __init__(self, nc, ndma=10):
        self.nc = nc
        self.E = {
            "sp": _Eng("sp", 32),
            "act": _Eng("act", 2),
            "pool": _Eng("pool", 12),
            "dve": _Eng("dve"),
            "pe": _Eng("pe"),
        }
        self.lastw = {}
        self.readers = {}
        self.const = set()
        self.stack = contextlib.ExitStack()
        for s in self.all_sems():
            s.h = self.stack.enter_context(nc.semaphore(s.name))

    def all_sems(self):
        out = []
        for e in self.E.values():
            out.append(e.sem)
            out.extend(e.dma_sems)
        return out

    def op(self, eng, fn, reads=(), writes=(), dma=False):
        e = self.E[eng]
        deps = []
        for k in reads:
            t = self.lastw.get(k)
            if t is not None:
                deps.append(t)
        for k in writes:
            t = self.lastw.get(k)
            if t is not None:
                deps.append(t)
            r = self.readers.get(k)
            if r:
                deps.extend(r.values())
        if dma:
            ds = e.dma_sems[e.rr]
            e.rr = (e.rr + 1) % len(e.dma_sems)
            if ds.cnt > 0:
                deps.append((ds, ds.cnt))
            ds.cnt += 16
            tok = (ds, ds.cnt)
        else:
            e.sem.cnt += 1
            tok = (e.sem, e.sem.cnt)
        newmax = {}
        for (so, v) in deps:
            if e.waited.get(so, 0) < v and newmax.get(so, 0) < v:
                newmax[so] = v
        waits = list(newmax.items())
        for so, v in waits:
            e.waited[so] = v
        e.ops.append((fn, waits, tok, 16 if dma else 1))
        for k in reads:
            if k in self.const:
                continue
            self.readers.setdefault(k, {})[tok[0]] = tok
        for k in writes:
            self.lastw[k] = tok
            self.readers[k] = {}
        return tok

    def wait_all(self, eng):
        e = self.E[eng]
        waits = []
        for s in self.all_sems():
            if s.cnt > 0 and e.waited.get(s, 0) < s.cnt:
                waits.append((s, s.cnt))
                e.waited[s] = s.cnt
        e.ops.append((None, waits, None, 0))

    def flush(self):
        self.wait_all("sp")
        with self.nc.Block() as block:
            def run(e):
                ops = e.ops

                def body(eng):
                    for fn, waits, tok, amt in ops:
                        for so, v in waits:
                            eng.wait_ge(so.h, v)
                        if fn is None:
                            continue
                        ins = fn(eng)
                        if tok is not None:
                            ins.then_inc(tok[0].h, amt)
                return body

            reg = {"sp": block.sync, "act": block.scalar, "pool": block.gpsimd,
                   "dve": block.vector, "pe": block.tensor}
            for n, e in self.E.items():
                if e.ops:
                    reg[n](run(e))
        for e in self.E.values():
            e.ops = []

    def close(self):
        self.stack.close()


def split_tiles(lo, hi, mx):
    n = hi - lo
    nt = -(-n // mx)
    base, rem = divmod(n, nt)
    out = []
    s = lo
    for i in range(nt):
        k = base + (1 if i < rem else 0)
        out.append((s, k))
        s += k
    return out


def cast_cols(P, src, dst, key, K, ncols, cw):
    s = src.rearrange("(kc p) f -> p kc f", p=128)
    ng = ncols // cw
    per = max(1, 4096 // (K // 128 * 128) * 1)
    for g in range(ng):
        P.op("pool", lambda e, g=g: e.dma_start(out=dst[g], in_=s[:, :, g * cw:(g + 1) * cw]),
             writes=[(key, g)], dma=True)


def ffn_phase(P, nc, ps, C, tag, Xin, Xout, st_lo, st_hi, Wg, Wu, Wd, wkeys, g_pre, g_ffn):
    tiles = split_tiles(st_lo, st_hi, NSUB_MAX)
    NTM = NSUB_MAX * 128
    kg, ku, kd = wkeys
    xin_name = Xin.tensor.name
    xout_name = Xout.tensor.name
    with contextlib.ExitStack() as st:
        def sb(name, shape, dt):
            return st.enter_context(nc.sbuf_tensor(tag + name, shape, dt))
        xnT = sb("xnT", [128, KC, NTM], BF16)
        hT = sb("hT", [128, FC, NTM], BF16)
        wgu = sb("wgu", [128, 3, 2, KC, 128], BF16)
        wd = sb("wd", [128, 3, 4, 512], BF16)
        xt = sb("xt", [128, 2, D], F32)
        hn = sb("hn", [128, 2, D], BF16)
        junk = sb("junk", [128, D], BF16)
        gbf = sb("gbf", [128, D], F32)
        gbp = sb("gbp", [128, D], F32) if g_pre is not None else None
        sil = sb("sil", [128, 2, 512], BF16)
        xr = sb("xr", [128, 3, 512], F32)
        xo = sb("xo", [128, 3, 512], F32)
        stat = sb("stat", [128, 2, 8], F32)

        P.op("sp", lambda e: e.dma_start(out=gbf[:], in_=g_ffn.partition_broadcast(128)),
             writes=[tag + "gbf"], dma=True)
        if g_pre is not None:
            P.op("sp", lambda e: e.dma_start(out=gbp[:], in_=g_pre.partition_broadcast(128)),
                 writes=[tag + "gbp"], dma=True)
        P.const.add(tag + "gbf")
        P.const.add(tag + "gbp")

        cnt = {"x": 0, "wgu": 0, "wd": 0, "r": 0, "sil": 0}

        def prologue(t0, nsub):
            for ts in range(nsub):
                s_ = t0 + ts
                sl = cnt["x"] % 2
                cnt["x"] += 1
                rows = slice(s_ * 128, (s_ + 1) * 128)
                kx, khn, kst = (tag + "xt", sl), (tag + "hn", sl), (tag + "stat", sl)
                P.op("sp", lambda e, sl=sl, rows=rows: e.dma_start(out=xt[:, sl, :], in_=Xin[rows, :]),
                     reads=[(xin_name, s_, j) for j in range(4)], writes=[kx], dma=True)
                if g_pre is not None:
                    P.op("act", lambda e, sl=sl: e.activation(out=junk[:], in_=xt[:, sl, :], func=AF.Square,
                                                               accum_out=stat[:, sl, 0:1]),
                         reads=[kx], writes=[tag + "junk", kst])
                    P.op("act", lambda e, sl=sl: e.activation(out=stat[:, sl, 1:2], in_=stat[:, sl, 0:1], func=AF.Sqrt,
                                                               bias=C["eps"][:], scale=1.0 / D),
                         reads=[kst], writes=[kst])
                    P.op("dve", lambda e, sl=sl: e.reciprocal(out=stat[:, sl, 2:3], in_=stat[:, sl, 1:2]),
                         reads=[kst], writes=[kst])
                    P.op("dve", lambda e, sl=sl: e.scalar_tensor_tensor(
                        out=xt[:, sl, :], in0=xt[:, sl, :], scalar=stat[:, sl, 2:3], in1=gbp[:],
                        op0=ALU.mult, op1=ALU.mult), reads=[kx, kst, tag + "gbp"], writes=[kx])
                    P.op("sp", lambda e, sl=sl, rows=rows: e.dma_start(out=Xout[rows, :], in_=xt[:, sl, :]),
                         reads=[kx], writes=[(xout_name, s_, j) for j in range(4)], dma=True)
                P.op("act", lambda e, sl=sl: e.activation(out=junk[:], in_=xt[:, sl, :], func=AF.Square,
                                                           accum_out=stat[:, sl, 3:4]),
                     reads=[kx], writes=[tag + "junk", kst])
                P.op("act", lambda e, sl=sl: e.activation(out=stat[:, sl, 4:5], in_=stat[:, sl, 3:4], func=AF.Sqrt,
                                                           bias=C["eps"][:], scale=1.0 / D),
                     reads=[kst], writes=[kst])
                P.op("dve", lambda e, sl=sl: e.reciprocal(out=stat[:, sl, 5:6], in_=stat[:, sl, 4:5]),
                     reads=[kst], writes=[kst])
                P.op("dve", lambda e, sl=sl: e.scalar_tensor_tensor(
                    out=hn[:, sl, :], in0=xt[:, sl, :], scalar=stat[:, sl, 5:6], in1=gbf[:],
                    op0=ALU.mult, op1=ALU.mult), reads=[kx, kst, tag + "gbf"], writes=[khn])
                for half in range(2):
                    def tr(e, sl=sl, half=half):
                        pv = ps[:, 7, :].bitcast(BF16)
                        ins = None
                        for j in range(8):
                            kc = half * 8 + j
                            ins = e.transpose(out=pv[:, j * 128:(j + 1) * 128], in_=hn[:, sl, kc * 128:(kc + 1) * 128],
                                              identity=C["ident"][:])
                        return ins
                    P.op("pe", tr, reads=[khn, "ident"], writes=[("ps", 7)])

                    def ev(e, half=half, ts=ts):
                        pv = ps[:, 7, :].bitcast(BF16).rearrange("p (j t) -> p j t", j=8)
                        return e.tensor_copy(out=xnT[:, half * 8:(half + 1) * 8, ts * 128:(ts + 1) * 128], in_=pv)
                    P.op("dve" if half == 0 else "act", (ev if half == 0 else
                         (lambda e, half=half, ts=ts: e.copy(
                             out=xnT[:, half * 8:(half + 1) * 8, ts * 128:(ts + 1) * 128],
                             in_=ps[:, 7, :].bitcast(BF16).rearrange("p (j t) -> p j t", j=8)))),
                         reads=[("ps", 7)], writes=[(tag + "xnT", ts)])

        def load_wgu(fc):
            sl = cnt["wgu"] % 3
            cnt["wgu"] += 1
            P.op("sp", lambda e: e.dma_start(out=wgu[:, sl, 0], in_=Wg[fc]), reads=[(kg, fc)],
                 writes=[(tag + "wg", sl)], dma=True)
            P.op("sp", lambda e: e.dma_start(out=wgu[:, sl, 1], in_=Wu[fc]), reads=[(ku, fc)],
                 writes=[(tag + "wu", sl)], dma=True)
            return sl

        def load_wd(dg, fg):
            sl = cnt["wd"] % 3
            cnt["wd"] += 1
            P.op("sp", lambda e: e.dma_start(out=wd[:, sl], in_=Wd[dg, :, fg * 4:(fg + 1) * 4, :]),
                 reads=[(kd, dg)], writes=[(tag + "wd", sl)], dma=True)
            return sl

        def gateup(nsub):
            NT = nsub * 128
            halves = [(0, min(512, NT))] + ([(512, NT)] if NT > 512 else [])
            pend = [load_wgu(fc) for fc in range(min(3, FC))]
            for fc in range(FC):
                sl = pend.pop(0)
                par = fc % 2
                for gu in range(2):
                    for hi, (a, b) in enumerate(halves):
                        bank = 4 * par + 2 * gu + hi
                        def mm(e, sl=sl, gu=gu, a=a, b=b, bank=bank):
                            ins = None
                            for kc in range(KC):
                                ins = e.matmul(ps[:, bank, 0:b - a], lhsT=wgu[:, sl, gu, kc, :], rhs=xnT[:, kc, a:b],
                                               start=(kc == 0), stop=(kc == KC - 1))
                            return ins
                        P.op("pe", mm, reads=[(tag + ("wg" if gu == 0 else "wu"), sl)] +
                             [(tag + "xnT", t) for t in range(a // 128, b // 128)], writes=[("ps", bank)])
                for hi, (a, b) in enumerate(halves):
                    ss = cnt["sil"] % 2
                    cnt["sil"] += 1
                    bg, bu = 4 * par + hi, 4 * par + 2 + hi
                    P.op("act", lambda e, ss=ss, a=a, b=b, bg=bg: e.activation(
                        out=sil[:, ss, 0:b - a], in_=ps[:, bg, 0:b - a], func=AF.Silu),
                        reads=[("ps", bg)], writes=[(tag + "sil", ss)])
                    P.op("dve", lambda e, ss=ss, a=a, b=b, bu=bu, fc=fc: e.tensor_tensor(
                        out=hT[:, fc, a:b], in0=ps[:, bu, 0:b - a], in1=sil[:, ss, 0:b - a], op=ALU.mult),
                        reads=[("ps", bu), (tag + "sil", ss)], writes=[(tag + "hT", fc)])
                if fc + 3 < FC:
                    pend.append(load_wgu(fc + 3))

        def down(t0, nsub, mid_hook):
            Xres = Xout if g_pre is not None else Xin
            xres_name = Xres.tensor.name
            NFG = FC // 4
            seq = [(dg, fg) for dg in range(4) for fg in range(NFG)]
            pend = [load_wd(*seq[i]) for i in range(3)]
            for i, (dg, fg) in enumerate(seq):
                sl = pend.pop(0)

                def mm(e, sl=sl, fg=fg):
                    ins = None
                    for fl in range(4):
                        fc = fg * 4 + fl
                        for ts in range(nsub):
                            ins = e.matmul(ps[:, ts, :], lhsT=hT[:, fc, ts * 128:(ts + 1) * 128], rhs=wd[:, sl, fl, :],
                                           start=(fc == 0), stop=(fc == FC - 1))
                    return ins
                P.op("pe", mm, reads=[(tag + "wd", sl)] + [(tag + "hT", fg * 4 + fl) for fl in range(4)],
                     writes=[("ps", ts) for ts in range(nsub)])
                if i + 3 < len(seq):
                    pend.append(load_wd(*seq[i + 3]))
                if fg == NFG - 1:
                    cols = slice(dg * 512, (dg + 1) * 512)
                    for ts in range(nsub):
                        s_ = t0 + ts
                        rows = slice(s_ * 128, (s_ + 1) * 128)
                        rs = cnt["r"] % 3
                        cnt["r"] += 1
                        P.op("sp", lambda e, rs=rs, rows=rows, cols=cols: e.dma_start(out=xr[:, rs, :], in_=Xres[rows, cols]),
                             reads=[(xres_name, s_, dg)], writes=[(tag + "xr", rs)], dma=True)
                        P.op("dve", lambda e, rs=rs, ts=ts: e.scalar_tensor_tensor(
                            out=xo[:, rs, :], in0=ps[:, ts, :], scalar=0.5, in1=xr[:, rs, :], op0=ALU.mult, op1=ALU.add),
                            reads=[("ps", ts), (tag + "xr", rs)], writes=[(tag + "xo", rs)])
                        P.op("sp", lambda e, rs=rs, rows=rows, cols=cols: e.dma_start(out=Xout[rows, cols], in_=xo[:, rs, :]),
                             reads=[(tag + "xo", rs)], writes=[(xout_name, s_, dg)], dma=True)
                    if dg == 1 and mid_hook is not None:
                        mid_hook()

        for ti, (t0, nsub) in enumerate(tiles):
            if ti == 0:
                prologue(t0, nsub)
            gateup(nsub)
            nxt = tiles[ti + 1] if ti + 1 < len(tiles) else None
            down(t0, nsub, (lambda nxt=nxt: prologue(*nxt)) if nxt is not None else None)
        P.flush()


class NormCtx:
    def __init__(self, P, nc, st, C, tag, gvec, out_dt, nslot=2):
        self.P, self.C, self.tag = P, C, tag
        self.xt = st.enter_context(nc.sbuf_tensor(tag + "nxt", [128, nslot, D], F32))
        self.hn = st.enter_context(nc.sbuf_tensor(tag + "nhn", [128, nslot, D], out_dt))
        self.junk = st.enter_context(nc.sbuf_tensor(tag + "njunk", [128, D], BF16))
        self.gb = st.enter_context(nc.sbuf_tensor(tag + "ngb", [128, D], F32))
        self.stat = st.enter_context(nc.sbuf_tensor(tag + "nstat", [128, nslot, 4], F32))
        self.n = 0
        self.nslot = nslot
        gb = self.gb
        P.op("sp", lambda e: e.dma_start(out=gb[:], in_=gvec.partition_broadcast(128)), writes=[tag + "ngb"], dma=True)
        P.const.add(tag + "ngb")

    def run(self, X, s_, valid=None):
        P, C, tag = self.P, self.C, self.tag
        sl = self.n % self.nslot
        self.n += 1
        xt, hn, junk, gb, stat = self.xt, self.hn, self.junk, self.gb, self.stat
        xname = X.tensor.name
        rows = slice(s_ * 128, (s_ + 1) * 128)
        kx, khn, kst = (tag + "nxt", sl), (tag + "nhn", sl), (tag + "nstat", sl)
        P.op("sp", lambda e: e.dma_start(out=xt[:, sl, :], in_=X[rows, :]),
             reads=[(xname, s_, j) for j in range(4)], writes=[kx], dma=True)
        P.op("act", lambda e: e.activation(out=junk[:], in_=xt[:, sl, :], func=AF.Square, accum_out=stat[:, sl, 0:1]),
             reads=[kx], writes=[tag + "njunk", kst])
        P.op("act", lambda e: e.activation(out=stat[:, sl, 1:2], in_=stat[:, sl, 0:1], func=AF.Sqrt,
                                           bias=C["eps"][:], scale=1.0 / D), reads=[kst], writes=[kst])
        P.op("dve", lambda e: e.reciprocal(out=stat[:, sl, 2:3], in_=stat[:, sl, 1:2]), reads=[kst], writes=[kst])
        if valid is not None:
            vap, vkey = valid
            P.op("dve", lambda e: e.tensor_tensor(out=stat[:, sl, 2:3], in0=stat[:, sl, 2:3], in1=vap, op=ALU.mult),
                 reads=[kst, vkey], writes=[kst])
        P.op("dve", lambda e: e.scalar_tensor_tensor(out=hn[:, sl, :], in0=xt[:, sl, :], scalar=stat[:, sl, 2:3],
                                                     in1=gb[:], op0=ALU.mult, op1=ALU.mult),
             reads=[kx, kst, tag + "ngb"], writes=[khn])
        return sl, khn


def transpose_to(P, ps, C, src_ap_fn, src_key, nchunk, dst_fn, dst_keys, bank=7, fp32=False):
    per = 4 if fp32 else 8
    ident = C["identf"] if fp32 else C["ident"]
    ikey = "identf" if fp32 else "ident"
    r = 0
    for c0 in range(0, nchunk, per):
        n = min(per, nchunk - c0)

        def tr(e, c0=c0, n=n):
            pv = ps[:, bank, :] if fp32 else ps[:, bank, :].bitcast(BF16)
            ins = None
            for j in range(n):
                ins = e.transpose(out=pv[:, j * 128:(j + 1) * 128], in_=src_ap_fn(c0 + j), identity=ident[:])
            return ins
        P.op("pe", tr, reads=[src_key, ikey], writes=[("ps", bank)])

        def ev(e, c0=c0, n=n, r=r):
            pv = ps[:, bank, :] if fp32 else ps[:, bank, :].bitcast(BF16)
            src = pv[:, 0:n * 128].rearrange("p (j t) -> p j t", j=n)
            if r % 2 == 0:
                return e.tensor_copy(out=dst_fn(c0, n), in_=src)
            return e.copy(out=dst_fn(c0, n), in_=src)
        P.op("dve" if r % 2 == 0 else "act", ev, reads=[("ps", bank)], writes=dst_keys)
        r += 1


def final_phase(P, nc, C, tag, X, out, st_lo, st_hi, gvec):
    with contextlib.ExitStack() as st:
        N = NormCtx(P, nc, st, C, tag, gvec, F32, nslot=3)
        for s_ in range(st_lo, st_hi):
            sl, khn = N.run(X, s_)
            o = s_ - st_lo
            P.op("sp", lambda e, sl=sl, o=o: e.dma_start(out=out[o * 128:(o + 1) * 128, :], in_=N.hn[:, sl, :]),
                 reads=[khn], writes=[("out", o)], dma=True)
        P.flush()


def pool_phase(P, nc, ps, C, tag, X, HT, st_lo, st_hi, gvec, Wp, wkey, scale_vec, valid_d, invcnt_d, NTOK):
    with contextlib.ExitStack() as st:
        N = NormCtx(P, nc, st, C, tag + "A", gvec, F32, nslot=2)
        hts = st.enter_context(nc.sbuf_tensor(tag + "hts", [128, 2, KC, 128], F32))
        vt = st.enter_context(nc.sbuf_tensor(tag + "vt", [128, 2, 1], F32))
        zt = st.enter_context(nc.sbuf_tensor(tag + "zt", [128, KC, 8], F32))
        P.op("dve", lambda e: e.memset(zt[:], 0.0), writes=[tag + "zt"])
        P.op("sp", lambda e: e.dma_start(out=HT[:, :, st_lo * 128:st_lo * 128 + 8].rearrange("c p t -> p c t"), in_=zt[:]),
             reads=[tag + "zt"], writes=[(tag + "HTpad", 0)], dma=True)
        P.op("sp", lambda e: e.dma_start(out=HT[:, :, st_hi * 128 + 8:st_hi * 128 + 16].rearrange("c p t -> p c t"), in_=zt[:]),
             reads=[tag + "zt"], writes=[(tag + "HTpad", 1)], dma=True)
        for i, s_ in enumerate(range(st_lo, st_hi)):
            hs = i % 2
            P.op("sp", lambda e, hs=hs, s_=s_: e.dma_start(out=vt[:, hs, :], in_=valid_d[s_ * 128:(s_ + 1) * 128, :]),
                 writes=[(tag + "vt", hs)], dma=True)
            sl, khn = N.run(X, s_, valid=(vt[:, hs, :], (tag + "vt", hs)))
            transpose_to(P, ps, C, lambda c, sl=sl: N.hn[:, sl, c * 128:(c + 1) * 128], khn, KC,
                         lambda c0, n, hs=hs: hts[:, hs, c0:c0 + n, :], [(tag + "hts", hs)], bank=7, fp32=True)
            P.op("sp", lambda e, hs=hs, s_=s_: e.dma_start(
                out=HT[:, :, 8 + s_ * 128:8 + (s_ + 1) * 128].rearrange("c p t -> p c t"), in_=hts[:, hs]),
                reads=[(tag + "hts", hs)], writes=[(tag + "HT", s_)], dma=True)
        P.flush()
    TS = 4
    tiles = split_tiles(st_lo, st_hi, TS)
    xname = X.tensor.name
    with contextlib.ExitStack() as st:
        def sb(name, shape, dt):
            return st.enter_context(nc.sbuf_tensor(tag + name, shape, dt))
        W = TS * 128
        hg = sb("hg", [128, 2, 4, W + 16], F32)
        ca = sb("ca", [128, 4, W + 16], F32)
        cb = sb("cb", [128, 4, W + 16], F32)
        ic = sb("ic", [128, 2, W], F32)
        pT = sb("pT", [128, 2, 4, W], BF16)
        wp = sb("wp", [128, 4, 4, 512], BF16)
        sc = sb("sc", [128, D], F32)
        xr = sb("xr", [128, 3, 512], F32)
        xo = sb("xo", [128, 3, 512], F32)
        tm = sb("tm", [128, 2, 512], F32)
        P.op("sp", lambda e: e.dma_start(out=sc[:], in_=scale_vec.partition_broadcast(128)), writes=[tag + "sc"], dma=True)
        for g in range(4):
            P.op("sp", lambda e, g=g: e.dma_start(out=wp[:, g], in_=Wp[g]), reads=[(wkey, g)], writes=[tag + "wp"], dma=True)
        P.const.add(tag + "sc")
        k = 0
        r = 0
        for (t0, nsub) in tiles:
            n = nsub * 128
            tok0 = t0 * 128
            for g in range(4):
                w = 2 << g
                sl = k % 2
                k += 1
                khg, kic, kpT = (tag + "hg", sl), (tag + "ic", sl), (tag + "pT", sl)
                P.op("sp", lambda e, sl=sl, g=g, tok0=tok0, n=n: e.dma_start(
                    out=hg[:, sl, :, 0:n + 16], in_=HT[4 * g:4 * g + 4, :, tok0:tok0 + n + 16].rearrange("c p t -> p c t")),
                    reads=[(tag + "HT", s_) for s_ in range(max(st_lo, t0 - 1), min(st_hi, t0 + nsub + 1))] +
                    [(tag + "HTpad", 0), (tag + "HTpad", 1)], writes=[khg], dma=True)
                P.op("sp", lambda e, sl=sl, g=g, tok0=tok0, n=n: e.dma_start(
                    out=ic[:, sl, 0:n], in_=invcnt_d[g:g + 1, tok0:tok0 + n].partition_broadcast(128)),
                    writes=[kic], dma=True)
                L = n + 16
                cur, curk = (lambda a, b, sl=sl: hg[:, sl, :, a:b]), khg
                bufs = [(ca, tag + "ca"), (cb, tag + "cb")]
                step = 1
                for lv in range(g + 1):
                    dst, dk = bufs[lv % 2]
                    Ln = L - (2 * step - 1)
                    P.op("dve",
                         lambda e, cur=cur, dst=dst, step=step, Ln=Ln: e.tensor_tensor(
                             out=dst[:, :, 0:Ln], in0=cur(0, Ln), in1=cur(step, step + Ln), op=ALU.add),
                         reads=[curk], writes=[dk])
                    cur, curk = (lambda a, b, dst=dst: dst[:, :, a:b]), dk
                    step *= 2
                off = 8 - w // 2
                dst, dk = bufs[(g + 1) % 2]
                P.op("dve", lambda e, cur=cur, dst=dst, off=off, n=n, sl=sl: e.tensor_tensor(
                    out=dst[:, :, 0:n], in0=cur(off, off + n),
                    in1=ic[:, sl, 0:n].unsqueeze(1).broadcast_to([128, 4, n]), op=ALU.mult),
                    reads=[curk, kic], writes=[dk])
                P.op("dve", lambda e, dst=dst, n=n, sl=sl: e.tensor_tensor(
                    out=pT[:, sl, :, 0:n], in0=dst[:, :, 0:n], in1=hg[:, sl, :, 8:8 + n], op=ALU.subtract),
                    reads=[dk, khg], writes=[kpT])
                for ts in range(nsub):
                    s_ = t0 + ts
                    bank = (r % 6)
                    rs = r % 3
                    tsl = r % 2
                    r += 1

                    def mm(e, sl=sl, ts=ts, g=g, bank=bank):
                        ins = None
                        for cc in range(4):
                            ins = e.matmul(ps[:, bank, :], lhsT=pT[:, sl, cc, ts * 128:(ts + 1) * 128], rhs=wp[:, g, cc, :],
                                           start=(cc == 0), stop=(cc == 3))
                        return ins
                    P.op("pe", mm, reads=[kpT, tag + "wp"], writes=[("ps", bank)])
                    rows = slice(s_ * 128, (s_ + 1) * 128)
                    cols = slice(g * 512, (g + 1) * 512)
                    P.op("sp", lambda e, rs=rs, rows=rows, cols=cols: e.dma_start(out=xr[:, rs, :], in_=X[rows, cols]),
                         reads=[(xname, s_, g)], writes=[(tag + "xr", rs)], dma=True)
                    P.op("dve", lambda e, tsl=tsl, bank=bank, cols=cols: e.tensor_tensor(
                        out=tm[:, tsl, :], in0=ps[:, bank, :], in1=sc[:, cols], op=ALU.mult),
                        reads=[("ps", bank), tag + "sc"], writes=[(tag + "tm", tsl)])
                    P.op("dve", lambda e, tsl=tsl, rs=rs: e.tensor_tensor(
                        out=xo[:, rs, :], in0=tm[:, tsl, :], in1=xr[:, rs, :], op=ALU.add),
                        reads=[(tag + "tm", tsl), (tag + "xr", rs)], writes=[(tag + "xo", rs)])
                    P.op("sp", lambda e, rs=rs, rows=rows, cols=cols: e.dma_start(out=X[rows, cols], in_=xo[:, rs, :]),
                         reads=[(tag + "xo", rs)], writes=[(xname, s_, g)], dma=True)
        P.flush()


def emit_out_proj(P, ps, tag, X, t0, nsub, aT, aT_keys, nk, w_ap_fn, w_keys_fn, xr, xo, cnt, banks):
    xname = X.tensor.name
    for dg in range(4):
        cols = slice(dg * 512, (dg + 1) * 512)
        for ts in range(nsub):
            s_ = t0 + ts
            bank = banks[cnt["b"] % len(banks)]
            cnt["b"] += 1
            rs = cnt["r"] % 3
            cnt["r"] += 1

            def mm(e, ts=ts, dg=dg, bank=bank):
                ins = None
                for k in range(nk):
                    ins = e.matmul(ps[:, bank, :], lhsT=aT[:, k, ts * 128:(ts + 1) * 128], rhs=w_ap_fn(dg, k),
                                   start=(k == 0), stop=(k == nk - 1))
                return ins
            P.op("pe", mm, reads=list(aT_keys) + list(w_keys_fn(dg)), writes=[("ps", bank)])
            rows = slice(s_ * 128, (s_ + 1) * 128)
            P.op("sp", lambda e, rs=rs, rows=rows, cols=cols: e.dma_start(out=xr[:, rs, :], in_=X[rows, cols]),
                 reads=[(xname, s_, dg)], writes=[(tag + "xr", rs)], dma=True)
            P.op("dve", lambda e, rs=rs, bank=bank: e.tensor_tensor(out=xo[:, rs, :], in0=ps[:, bank, :], in1=xr[:, rs, :],
                                                                     op=ALU.add),
                 reads=[("ps", bank), (tag + "xr", rs)], writes=[(tag + "xo", rs)])
            P.op("sp", lambda e, rs=rs, rows=rows, cols=cols: e.dma_start(out=X[rows, cols], in_=xo[:, rs, :]),
                 reads=[(tag + "xo", rs)], writes=[(xname, s_, dg)], dma=True)


def sgu_phase(P, nc, ps, C, tag, X, st_lo, st_hi, gvec, Win, kin, Wout, kout, vgain, w_s, b_s):
    TS = 3
    DS = 4096
    tiles = split_tiles(st_lo, st_hi, TS)
    with contextlib.ExitStack() as st:
        def sb(name, shape, dt):
            return st.enter_context(nc.sbuf_tensor(tag + name, shape, dt))
        N = NormCtx(P, nc, st, C, tag, gvec, BF16, nslot=2)
        xnT = sb("xnT", [128, KC, TS * 128], BF16)
        wsl = sb("wsl", [128, 2, 16, 512], BF16)
        U = sb("U", [128, TS, DS], BF16)
        V = sb("V", [128, TS, DS], BF16)
        vg = sb("vg", [128, DS], F32)
        G = sb("G", [128, DS], BF16)
        GT = sb("GT", [128, 32, TS * 128], BF16)
        wsT = sb("wsT", [128, 16, 128], BF16)
        bs = sb("bs", [128, 16], F32)
        xr = sb("xr", [128, 3, 512], F32)
        xo = sb("xo", [128, 3, 512], F32)
        vst = sb("vst", [128, 4], F32)
        P.op("sp", lambda e: e.dma_start(out=vg[:], in_=vgain.partition_broadcast(128)), writes=[tag + "vg"], dma=True)
        P.op("sp", lambda e: e.dma_start(out=bs[:], in_=b_s.rearrange("g i -> i g"), allow_slow_non_contiguous=True),
             writes=[tag + "bs"], dma=True)
        wstage = N.xt
        P.op("sp", lambda e: e.dma_start(out=wstage[:, 0, :].rearrange("p (g j) -> p g j", g=16),
                                         in_=w_s.rearrange("g i j -> i g j")), writes=[(tag + "nxt", 0)], dma=True)
        transpose_to(P, ps, C, lambda c: wstage[:, 0, c * 128:(c + 1) * 128], (tag + "nxt", 0), 16,
                     lambda c0, n: wsT[:, c0:c0 + n, :], [tag + "wsT"], bank=7, fp32=True)
        for k in ("vg", "bs", "wsT"):
            P.const.add(tag + k)
        cnt = {"w": 0, "b": 0, "r": 0}

        def loadw(src, keys):
            sl = cnt["w"] % 2
            cnt["w"] += 1
            P.op("sp", lambda e: e.dma_start(out=wsl[:, sl], in_=src), reads=keys, writes=[(tag + "wsl", sl)], dma=True)
            return sl

        for (t0, nsub) in tiles:
            for ts in range(nsub):
                sl, khn = N.run(X, t0 + ts)
                transpose_to(P, ps, C, lambda c, sl=sl: N.hn[:, sl, c * 128:(c + 1) * 128], khn, KC,
                             lambda c0, n, ts=ts: xnT[:, c0:c0 + n, ts * 128:(ts + 1) * 128], [(tag + "xnT", ts)], bank=7)
            nxt = loadw(Win[0], [(kin, 0)])
            for cg in range(16):
                sl = nxt
                if cg + 1 < 16:
                    nxt = loadw(Win[cg + 1], [(kin, cg + 1)])
                for ts in range(nsub):
                    bank = cnt["b"] % 6
                    cnt["b"] += 1

                    def mm(e, sl=sl, ts=ts, bank=bank):
                        ins = None
                        for kc in range(KC):
                            ins = e.matmul(ps[:, bank, :], lhsT=xnT[:, kc, ts * 128:(ts + 1) * 128], rhs=wsl[:, sl, kc, :],
                                           start=(kc == 0), stop=(kc == KC - 1))
                        return ins
                    P.op("pe", mm, reads=[(tag + "xnT", ts), (tag + "wsl", sl)], writes=[("ps", bank)])
                    dst = U if cg < 8 else V
                    dk = (tag + ("U" if cg < 8 else "V"), ts, cg % 8)
                    c0 = (cg % 8) * 512
                    P.op("act", lambda e, dst=dst, ts=ts, c0=c0, bank=bank: e.activation(
                        out=dst[:, ts, c0:c0 + 512], in_=ps[:, bank, :], func=AF.Gelu), reads=[("ps", bank)], writes=[dk])
            for ts in range(nsub):
                vk = [(tag + "V", ts, j) for j in range(8)]
                P.op("act", lambda e, ts=ts: e.activation(out=G[:], in_=V[:, ts, :], func=AF.Square, accum_out=vst[:, 0:1]),
                     reads=vk, writes=[tag + "G", tag + "vst"])
                P.op("act", lambda e: e.activation(out=vst[:, 1:2], in_=vst[:, 0:1], func=AF.Sqrt, bias=C["eps"][:],
                                                   scale=1.0 / DS), reads=[tag + "vst"], writes=[tag + "vst"])
                P.op("dve", lambda e: e.reciprocal(out=vst[:, 2:3], in_=vst[:, 1:2]), reads=[tag + "vst"], writes=[tag + "vst"])
                P.op("dve", lambda e, ts=ts: e.scalar_tensor_tensor(out=V[:, ts, :], in0=V[:, ts, :], scalar=vst[:, 2:3],
                                                                    in1=vg[:], op0=ALU.mult, op1=ALU.mult),
                     reads=vk + [tag + "vst", tag + "vg"], writes=vk)
            for ts in range(nsub):
                vk = [(tag + "V", ts, j) for j in range(8)]
                uk = [(tag + "U", ts, j) for j in range(8)]
                for gp in range(8):
                    bank = cnt["b"] % 6
                    cnt["b"] += 1

                    def mm(e, ts=ts, gp=gp, bank=bank):
                        ins = None
                        for q in range(2):
                            g = 2 * gp + q
                            ins = e.matmul(ps[:, bank, q * 256:(q + 1) * 256], lhsT=wsT[:, g, :],
                                           rhs=V[:, ts, g * 256:(g + 1) * 256], start=True, stop=True)
                        return ins
                    P.op("pe", mm, reads=vk + [tag + "wsT"], writes=[("ps", bank)])
                    for q in range(2):
                        g = 2 * gp + q
                        P.op("dve", lambda e, ts=ts, g=g, q=q, bank=bank: e.scalar_tensor_tensor(
                            out=G[:, g * 256:(g + 1) * 256], in0=ps[:, bank, q * 256:(q + 1) * 256], scalar=bs[:, g:g + 1],
                            in1=U[:, ts, g * 256:(g + 1) * 256], op0=ALU.add, op1=ALU.mult),
                            reads=[("ps", bank), tag + "bs"] + uk, writes=[tag + "G"])
                transpose_to(P, ps, C, lambda c: G[:, c * 128:(c + 1) * 128], tag + "G", 32,
                             lambda c0, n, ts=ts: GT[:, c0:c0 + n, ts * 128:(ts + 1) * 128], [(tag + "GT", ts)], bank=7)
            xname = X.tensor.name
            nxt = loadw(Wout[0, :, 0:16, :], [(kout, 0)])
            for dg in range(4):
                cols = slice(dg * 512, (dg + 1) * 512)
                for half in range(2):
                    sl = nxt
                    nh = dg * 2 + half + 1
                    if nh < 8:
                        nxt = loadw(Wout[nh // 2, :, (nh % 2) * 16:(nh % 2) * 16 + 16, :], [(kout, nh // 2)])
                    for ts in range(nsub):
                        bank = ts

                        def mm(e, sl=sl, ts=ts, half=half, bank=bank):
                            ins = None
                            for k in range(16):
                                fc = half * 16 + k
                                ins = e.matmul(ps[:, bank, :], lhsT=GT[:, fc, ts * 128:(ts + 1) * 128], rhs=wsl[:, sl, k, :],
                                               start=(fc == 0), stop=(fc == 31))
                            return ins
                        P.op("pe", mm, reads=[(tag + "GT", ts), (tag + "wsl", sl)], writes=[("ps", bank)])
                for ts in range(nsub):
                    s_ = t0 + ts
                    rs = cnt["r"] % 3
                    cnt["r"] += 1
                    rows = slice(s_ * 128, (s_ + 1) * 128)
                    P.op("sp", lambda e, rs=rs, rows=rows, cols=cols: e.dma_start(out=xr[:, rs, :], in_=X[rows, cols]),
                         reads=[(xname, s_, dg)], writes=[(tag + "xr", rs)], dma=True)
                    P.op("dve", lambda e, rs=rs, ts=ts: e.tensor_tensor(out=xo[:, rs, :], in0=ps[:, ts, :], in1=xr[:, rs, :],
                                                                         op=ALU.add),
                         reads=[("ps", ts), (tag + "xr", rs)], writes=[(tag + "xo", rs)])
                    P.op("sp", lambda e, rs=rs, rows=rows, cols=cols: e.dma_start(out=X[rows, cols], in_=xo[:, rs, :]),
                         reads=[(tag + "xo", rs)], writes=[(xname, s_, dg)], dma=True)
        P.flush()


NH = 64
HD = 32


def na_qkv_phase(P, nc, ps, C, tag, X, st_lo, st_hi, gvec, Wqk, kqk, Wv, kv, qgain, kgain, QT, VX):
    TS = 4
    tiles = split_tiles(st_lo, st_hi, TS)
    with contextlib.ExitStack() as st:
        def sb(name, shape, dt):
            return st.enter_context(nc.sbuf_tensor(tag + name, shape, dt))
        N = NormCtx(P, nc, st, C, tag, gvec, BF16, nslot=2)
        xnT = sb("xnT", [128, KC, TS * 128], BF16)
        wq = sb("wq", [128, 3, KC, 128], BF16)
        wv = sb("wv", [128, 2, KC, 512], BF16)
        sq = sb("sq", [128, 2, 512], BF16)
        lnt = sb("lnt", [128, 2, 512], F32)
        rstd = sb("rstd", [128, 2, 512], F32)
        qo = sb("qo", [128, 3, 512], BF16)
        vx = sb("vx", [128, TS, NH, HD + 1], BF16)
        gqk = sb("gqk", [128, 2], F32)
        for hl in range(4):
            P.op("sp", lambda e, hl=hl: e.dma_start(out=gqk[32 * hl:32 * hl + 32, 0:1], in_=qgain), writes=[tag + "gqk"], dma=True)
            P.op("sp", lambda e, hl=hl: e.dma_start(out=gqk[32 * hl:32 * hl + 32, 1:2], in_=kgain), writes=[tag + "gqk"], dma=True)
        P.op("dve", lambda e: e.tensor_scalar(out=gqk[:, 0:1], in0=gqk[:, 0:1], scalar1=float(HD) ** -0.5, scalar2=None,
                                              op0=ALU.mult), reads=[tag + "gqk"], writes=[tag + "gqk"])
        P.op("dve", lambda e: e.memset(vx[:, :, :, HD:HD + 1], 1.0), writes=[(tag + "vx", ts) for ts in range(TS)])
        P.const.add(tag + "gqk")
        cnt = {"wq": 0, "wv": 0, "s": 0, "q": 0, "b": 0, "ev": 0}

        def load_wq(fo):
            sl = cnt["wq"] % 3
            cnt["wq"] += 1
            P.op("sp", lambda e: e.dma_start(out=wq[:, sl], in_=Wqk[fo]), reads=[(kqk, fo)], writes=[(tag + "wq", sl)], dma=True)
            return sl

        def load_wv(cg):
            sl = cnt["wv"] % 2
            cnt["wv"] += 1
            P.op("sp", lambda e: e.dma_start(out=wv[:, sl], in_=Wv[cg]), reads=[(kv, cg)], writes=[(tag + "wv", sl)], dma=True)
            return sl

        for (t0, nsub) in tiles:
            n = nsub * 128
            tok0 = t0 * 128
            for ts in range(nsub):
                sl, khn = N.run(X, t0 + ts)
                transpose_to(P, ps, C, lambda c, sl=sl: N.hn[:, sl, c * 128:(c + 1) * 128], khn, KC,
                             lambda c0, n_, ts=ts: xnT[:, c0:c0 + n_, ts * 128:(ts + 1) * 128], [(tag + "xnT", ts)], bank=7)
            xk = [(tag + "xnT", ts) for ts in range(nsub)]
            pend = [load_wq(fo) for fo in range(3)]
            for fo in range(32):
                sl = pend.pop(0)
                bA = cnt["b"] % 3
                bB = 3 + cnt["b"] % 2
                cnt["b"] += 1
                ss = cnt["s"] % 2
                cnt["s"] += 1
                qs = cnt["q"] % 3
                cnt["q"] += 1

                def mm(e, sl=sl, bA=bA, n=n):
                    ins = None
                    for kc in range(KC):
                        ins = e.matmul(ps[:, bA, 0:n], lhsT=wq[:, sl, kc, :], rhs=xnT[:, kc, 0:n], start=(kc == 0), stop=(kc == KC - 1))
                    return ins
                P.op("pe", mm, reads=xk + [(tag + "wq", sl)], writes=[("ps", bA)])
                if fo + 3 < 32:
                    pend.append(load_wq(fo + 3))
                P.op("act", lambda e, ss=ss, bA=bA, n=n: e.activation(out=sq[:, ss, 0:n], in_=ps[:, bA, 0:n], func=AF.Square),
                     reads=[("ps", bA)], writes=[(tag + "sq", ss)])
                P.op("pe", lambda e, ss=ss, bB=bB, n=n: e.matmul(ps[:, bB, 0:n], lhsT=C["bd"][:], rhs=sq[:, ss, 0:n],
                                                                  start=True, stop=True),
                     reads=[(tag + "sq", ss), "bd"], writes=[("ps", bB)])
                P.op("act", lambda e, ss=ss, bB=bB, n=n: e.activation(out=lnt[:, ss, 0:n], in_=ps[:, bB, 0:n], func=AF.Ln,
                                                                       bias=C["eps"][:], scale=1.0),
                     reads=[("ps", bB)], writes=[(tag + "lnt", ss)])
                P.op("act", lambda e, ss=ss, n=n: e.activation(out=rstd[:, ss, 0:n], in_=lnt[:, ss, 0:n], func=AF.Exp, scale=-0.5),
                     reads=[(tag + "lnt", ss)], writes=[(tag + "rstd", ss)])
                gi = 0 if fo < 16 else 1
                P.op("dve", lambda e, ss=ss, qs=qs, bA=bA, n=n, gi=gi: e.scalar_tensor_tensor(
                    out=qo[:, qs, 0:n], in0=ps[:, bA, 0:n], scalar=gqk[:, gi:gi + 1], in1=rstd[:, ss, 0:n],
                    op0=ALU.mult, op1=ALU.mult), reads=[("ps", bA), (tag + "rstd", ss), tag + "gqk"], writes=[(tag + "qo", qs)])
                P.op("sp", lambda e, qs=qs, fo=fo, tok0=tok0, n=n: e.dma_start(out=QT[fo, :, tok0:tok0 + n], in_=qo[:, qs, 0:n]),
                     reads=[(tag + "qo", qs)], writes=[(tag + "QT", fo, t0)], dma=True)
            nxt = load_wv(0)
            for cg in range(4):
                sl = nxt
                if cg + 1 < 4:
                    nxt = load_wv(cg + 1)
                for ts in range(nsub):
                    bank = 5 + cnt["ev"] % 2
                    ev = cnt["ev"]
                    cnt["ev"] += 1

                    def mm(e, sl=sl, ts=ts, bank=bank):
                        ins = None
                        for kc in range(KC):
                            ins = e.matmul(ps[:, bank, :], lhsT=xnT[:, kc, ts * 128:(ts + 1) * 128], rhs=wv[:, sl, kc, :],
                                           start=(kc == 0), stop=(kc == KC - 1))
                        return ins
                    P.op("pe", mm, reads=[(tag + "xnT", ts), (tag + "wv", sl)], writes=[("ps", bank)])
                    if ev % 2 == 0:
                        P.op("dve", lambda e, ts=ts, cg=cg, bank=bank: e.tensor_copy(
                            out=vx[:, ts, 16 * cg:16 * cg + 16, 0:HD], in_=ps[:, bank, :].rearrange("p (h d) -> p h d", h=16)),
                            reads=[("ps", bank)], writes=[(tag + "vx", ts)])
                    else:
                        P.op("act", lambda e, ts=ts, cg=cg, bank=bank: e.copy(
                            out=vx[:, ts, 16 * cg:16 * cg + 16, 0:HD], in_=ps[:, bank, :].rearrange("p (h d) -> p h d", h=16)),
                            reads=[("ps", bank)], writes=[(tag + "vx", ts)])
            for ts in range(nsub):
                s_ = t0 + ts
                P.op("sp", lambda e, ts=ts, s_=s_: e.dma_start(out=VX[s_ * 128:(s_ + 1) * 128, :],
                                                                  in_=vx[:, ts].rearrange("p h d -> p (h d)")),
                     reads=[(tag + "vx", ts)], writes=[(tag + "VX", s_)], dma=True)
        P.flush()


def na_attn_phase(P, nc, ps, C, tag, NB, q_lo, q_hi, QT, VX, AO, TB, RVd, qkv_tag, qkv_tiles):
    NTOK = NB * 128
    with contextlib.ExitStack() as st:
        def sb(name, shape, dt):
            return st.enter_context(nc.sbuf_tensor(tag + name, shape, dt))
        Kc = sb("Kc", [128, 2, NTOK], BF16)
        Qc = sb("Qc", [128, 2, NTOK], BF16)
        Vc = sb("Vc", [128, 2, NB, 4, HD + 1], BF16)
        Gt = sb("Gt", [128, 2, 4, 896], F32)
        RVc = sb("RVc", [128, NB, 14], BF16)
        RVx = sb("RVx", [128, 2, 14, 64], BF16)
        Sg = sb("Sg", [128, 2, 896], F32)
        PT = sb("PT", [128, 2, 896], BF16)
        rec = sb("rec", [128, 2, 4], F32)
        aot = sb("aot", [128, 2, 4, HD], BF16)
        P.op("dve", lambda e: e.memset(RVc[:], 0.0), writes=[tag + "RVc"])
        for hl in range(4):
            P.op("sp", lambda e, hl=hl: e.dma_start(out=RVc[32 * hl:32 * hl + 2, :, :], in_=RVd.rearrange("b r c -> r b c")),
                 writes=[tag + "RVc"], dma=True)
        P.const.add(tag + "RVc")
        qt_keys_all = [(qkv_tag + "QT", fo, t0) for fo in range(32) for (t0, _) in qkv_tiles]
        cnt = {"s": 0, "x": 0, "o": 0}
        for c in range(16):
            cs = c % 2
            qk_keys = [(qkv_tag + "QT", fo, t0) for fo in (c, 16 + c) for (t0, _) in qkv_tiles]
            P.op("sp", lambda e, c=c, cs=cs: e.dma_start(out=Kc[:, cs, :], in_=QT[16 + c]), reads=qk_keys,
                 writes=[(tag + "Kc", cs)], dma=True)
            P.op("sp", lambda e, c=c, cs=cs: e.dma_start(out=Qc[:, cs, :], in_=QT[c]), reads=qk_keys,
                 writes=[(tag + "Qc", cs)], dma=True)
            P.op("sp", lambda e, c=c, cs=cs: e.dma_start(
                out=Vc[:, cs], in_=VX.rearrange("(j p) (h d) -> p j h d", p=128, d=HD + 1)[:, :, 4 * c:4 * c + 4, :]),
                reads=[(qkv_tag + "VX", s_) for s_ in range(NB)], writes=[(tag + "Vc", cs)], dma=True)
            P.op("sp", lambda e, c=c, cs=cs: e.dma_start(out=Gt[:, cs], in_=TB[4 * c:4 * c + 4].rearrange("h p f -> p h f")),
                 writes=[(tag + "Gt", cs)], dma=True)
            for i in range(q_lo, q_hi):
                jlo, jhi = max(0, i - 3), min(NB - 1, i + 3)
                a0, a1 = (jlo - (i - 3)) * 128, (jhi - (i - 3) + 1) * 128
                xs = cnt["x"] % 2
                cnt["x"] += 1
                P.op("act", lambda e, xs=xs, i=i: e.copy(
                    out=RVx[:, xs], in_=RVc[:, i, :].unsqueeze(2).broadcast_to([128, 14, 64])),
                    reads=[tag + "RVc"], writes=[(tag + "RVx", xs)])
                ob = 4 + cnt["o"] % 2
                osl = cnt["o"] % 2
                cnt["o"] += 1
                for hl in range(4):
                    ssl = cnt["s"] % 2
                    cnt["s"] += 1
                    b0 = 2 * ssl
                    pb = 32 * hl

                    def mm_s(e, xs=xs, cs=cs, i=i, jlo=jlo, jhi=jhi, a0=a0, a1=a1, b0=b0, pb=pb):
                        Sv = ps[:, b0:b0 + 2, :].rearrange("p a c -> p (a c)")
                        rv = RVx[:, xs].rearrange("p a b -> p (a b)")
                        ins = None
                        for (lo, hi) in ((a0, min(a1, 512)), (max(a0, 512), a1)):
                            if hi > lo:
                                ins = e.matmul(Sv[:, lo:hi], lhsT=C["ind4"][pb:pb + 2, :], rhs=rv[pb:pb + 2, lo:hi],
                                               start=True, stop=False, tile_position=(pb, 0), skip_group_check=True)
                        for j in range(jlo, jhi + 1):
                            o = j - (i - 3)
                            ins = e.matmul(Sv[:, o * 128:(o + 1) * 128], lhsT=Kc[pb:pb + 32, cs, j * 128:(j + 1) * 128],
                                           rhs=Qc[pb:pb + 32, cs, i * 128:(i + 1) * 128], start=False, stop=True,
                                           tile_position=(pb, 0), skip_group_check=True)
                        return ins
                    P.op("pe", mm_s, reads=[(tag + "RVx", xs), (tag + "Kc", cs), (tag + "Qc", cs), "ind4"],
                         writes=[("ps", b0), ("ps", b0 + 1)])
                    P.op("dve", lambda e, ssl=ssl, cs=cs, hl=hl, a0=a0, a1=a1, b0=b0: e.tensor_tensor(
                        out=Sg[:, ssl, a0:a1], in0=ps[:, b0:b0 + 2, :].rearrange("p a c -> p (a c)")[:, a0:a1],
                        in1=Gt[:, cs, hl, a0:a1], op=ALU.add),
                        reads=[("ps", b0), ("ps", b0 + 1), (tag + "Gt", cs)], writes=[(tag + "Sg", ssl)])
                    P.op("act", lambda e, ssl=ssl, a0=a0, a1=a1: e.activation(out=PT[:, ssl, a0:a1], in_=Sg[:, ssl, a0:a1],
                                                                              func=AF.Exp),
                         reads=[(tag + "Sg", ssl)], writes=[(tag + "PT", ssl)])

                    def mm_o(e, ssl=ssl, cs=cs, i=i, jlo=jlo, jhi=jhi, hl=hl, ob=ob):
                        ins = None
                        for j in range(jlo, jhi + 1):
                            o = j - (i - 3)
                            ins = e.matmul(ps[:, ob, hl * (HD + 1):(hl + 1) * (HD + 1)], lhsT=PT[:, ssl, o * 128:(o + 1) * 128],
                                           rhs=Vc[:, cs, j, hl, :], start=(j == jlo), stop=(j == jhi))
                        return ins
                    P.op("pe", mm_o, reads=[(tag + "PT", ssl), (tag + "Vc", cs)], writes=[("ps", ob)])
                Ov_fn = lambda ob=ob: ps[:, ob, 0:4 * (HD + 1)].rearrange("p (h d) -> p h d", h=4)
                P.op("dve", lambda e, osl=osl, Ov_fn=Ov_fn: e.tensor_scalar(
                    out=rec[:, osl, :], in0=Ov_fn()[:, :, HD], scalar1=1e-30, scalar2=None, op0=ALU.add),
                    reads=[("ps", ob)], writes=[(tag + "rec", osl)])
                P.op("dve", lambda e, osl=osl: e.reciprocal(out=rec[:, osl, :], in_=rec[:, osl, :]),
                     reads=[(tag + "rec", osl)], writes=[(tag + "rec", osl)])
                P.op("dve", lambda e, osl=osl, Ov_fn=Ov_fn: e.tensor_tensor(
                    out=aot[:, osl], in0=Ov_fn()[:, :, 0:HD], in1=rec[:, osl, :].unsqueeze(2).broadcast_to([128, 4, HD]),
                    op=ALU.mult), reads=[("ps", ob), (tag + "rec", osl)], writes=[(tag + "aot", osl)])
                P.op("sp", lambda e, osl=osl, i=i, c=c: e.dma_start(
                    out=AO[i * 128:(i + 1) * 128, c * 128:(c + 1) * 128], in_=aot[:, osl].rearrange("p h d -> p (h d)")),
                    reads=[(tag + "aot", osl)], writes=[(tag + "AO", i, c)], dma=True)
        P.flush()


def na_out_phase(P, nc, ps, C, tag, X, st_lo, st_hi, AO, at_tag, Wo, ko):
    TS = 4
    tiles = split_tiles(st_lo, st_hi, TS)
    with contextlib.ExitStack() as st:
        def sb(name, shape, dt):
            return st.enter_context(nc.sbuf_tensor(tag + name, shape, dt))
        wo = sb("wo", [128, 4, KC, 512], BF16)
        at = sb("at", [128, 2, D], BF16)
        aT = sb("aT", [128, KC, TS * 128], BF16)
        xr = sb("xr", [128, 3, 512], F32)
        xo = sb("xo", [128, 3, 512], F32)
        for dg in range(4):
            P.op("sp", lambda e, dg=dg: e.dma_start(out=wo[:, dg], in_=Wo[dg]), reads=[(ko, dg)], writes=[(tag + "wo", dg)], dma=True)
        cnt = {"b": 0, "r": 0, "a": 0}
        for (t0, nsub) in tiles:
            for ts in range(nsub):
                s_ = t0 + ts
                sl = cnt["a"] % 2
                cnt["a"] += 1
                P.op("sp", lambda e, sl=sl, s_=s_: e.dma_start(out=at[:, sl, :], in_=AO[s_ * 128:(s_ + 1) * 128, :]),
                     reads=[(at_tag + "AO", s_, c) for c in range(16)], writes=[(tag + "at", sl)], dma=True)
                transpose_to(P, ps, C, lambda c, sl=sl: at[:, sl, c * 128:(c + 1) * 128], (tag + "at", sl), KC,
                             lambda c0, n, ts=ts: aT[:, c0:c0 + n, ts * 128:(ts + 1) * 128], [(tag + "aT", ts)], bank=7)
            emit_out_proj(P, ps, tag, X, t0, nsub, aT, [(tag + "aT", ts) for ts in range(nsub)], KC,
                          lambda dg, k: wo[:, dg, k, :], lambda dg: [(tag + "wo", dg)], xr, xo, cnt, [0, 1, 2, 3, 4, 5])
        P.flush()


NEG = -30000.0
GRID_W = 64
ROWS = 256


def build_tb(rpb):
    kr = np.arange(2)[:, None, None, None, None]
    kc = np.arange(64)[None, :, None, None, None]
    o = np.arange(7)[None, None, :, None, None]
    qr = np.arange(2)[None, None, None, :, None]
    qc = np.arange(64)[None, None, None, None, :]
    dr = 2 * (o - 3) + kr - qr + 7
    dc = kc - qc + 15
    cs = np.clip(qc - 8, 0, GRID_W - 16)
    cvalid = (kc >= cs) & (kc < cs + 16)
    dr_b, dc_b, cv_b = np.broadcast_arrays(dr, dc, cvalid)
    dc_c = np.clip(dc_b, 0, 30)
    g = rpb[:, dr_b, dc_c]
    g = np.where(cv_b[None], g, np.float32(NEG)).astype(np.float32)
    return np.ascontiguousarray(g.reshape(64, 128, 896))


def build_rv(NB, row0):
    rv = np.full((NB, 2, 7, 2), NEG, np.float32)
    for i in range(NB):
        for o in range(7):
            for kr in range(2):
                for qr in range(2):
                    Rq = row0 + 2 * i + qr
                    Rk = row0 + 2 * (i + o - 3) + kr
                    if 0 <= Rq < ROWS and 0 <= Rk < ROWS:
                        rs = min(max(Rq - 4, 0), ROWS - 8)
                        if rs <= Rk < rs + 8:
                            rv[i, kr, o, qr] = 0.0
    return rv.reshape(NB, 2, 14).astype(ml_dtypes.bfloat16)


def build_consts():
    ident = np.eye(128, dtype=np.float32)
    p = np.arange(128)
    bd = np.where((p[:, None] // 32) == (p[None, :] // 32), np.float32(1.0 / 32), np.float32(0.0))
    ind4 = np.zeros((128, 128), np.float32)
    for hl in range(4):
        for r in range(2):
            ind4[32 * hl + r, :] = (p // 64 == r)
    return dict(ident=ident.astype(ml_dtypes.bfloat16), identf=ident, bd=bd.astype(ml_dtypes.bfloat16),
                ind4=ind4.astype(ml_dtypes.bfloat16))


DEPTH = 4
NB = 41
NTOK = NB * 128
OWN_LO, OWN_HI = 5, 37
SEQ = 16384
BATCH = 2


def build_program():
    nc = bass.Bass("TRN2", target_bir_lowering=False)

    def din(name, shape, dt=F32):
        return nc.dram_tensor(name, list(shape), dt, kind="ExternalInput").ap()

    def dscr(name, shape, dt):
        return nc.dram_tensor(name, list(shape), dt, kind="Internal").ap()

    x_loc = din("x_loc", [NTOK, D])
    I = {}
    for nm in ("ffn1_norm", "mix_norm", "ffn2_norm", "out_norm"):
        I[nm] = din(nm, [DEPTH, D])
    for nm in ("ffn1_w_gate", "ffn1_w_up", "ffn2_w_gate", "ffn2_w_up"):
        I[nm] = din(nm, [DEPTH, D, DFF])
    for nm in ("ffn1_w_down", "ffn2_w_down"):
        I[nm] = din(nm, [DEPTH, DFF, D])
    I["na_w_qkv"] = din("na_w_qkv", [2, D, 3 * D])
    I["na_q_gain"] = din("na_q_gain", [2, HD])
    I["na_k_gain"] = din("na_k_gain", [2, HD])
    I["na_w_o"] = din("na_w_o", [2, D, D])
    I["sgu_w_in"] = din("sgu_w_in", [1, D, 8192])
    I["sgu_v_gain"] = din("sgu_v_gain", [1, 4096])
    I["sgu_w_s"] = din("sgu_w_s", [1, 16, 128, 128])
    I["sgu_b_s"] = din("sgu_b_s", [1, 16, 128])
    I["sgu_w_out"] = din("sgu_w_out", [1, 4096, D])
    I["pool_w"] = din("pool_w", [1, 4, 512, 512])
    I["pool_scale"] = din("pool_scale", [1, D])
    TB = [din("tb0", [NH, 128, 896]), din("tb1", [NH, 128, 896])]
    RVd = din("rv", [NB, 2, 14], BF16)
    valid_d = din("valid", [NTOK, 1])
    invcnt_d = din("invcnt", [4, NTOK])
    cin = {k: din("c_" + k, [128, 128], F32 if k == "identf" else BF16) for k in ("ident", "identf", "bd", "ind4")}
    out = nc.dram_tensor("out", [(OWN_HI - OWN_LO) * 128, D], F32, kind="ExternalOutput").ap()

    X = dscr("X", [NTOK, D], F32)
    QT = dscr("QT", [32, 128, NTOK], BF16)
    VX = dscr("VX", [NTOK, NH * (HD + 1)], BF16)
    AO = dscr("AO", [NTOK, D], BF16)
    HT = dscr("HT", [KC, 128, NTOK + 16], F32)
    W = {}
    for l in range(DEPTH):
        for f in (1, 2):
            W["g%d%d" % (f, l)] = dscr("wg%d%d" % (f, l), [FC, 128, KC, 128], BF16)
            W["u%d%d" % (f, l)] = dscr("wu%d%d" % (f, l), [FC, 128, KC, 128], BF16)
            W["d%d%d" % (f, l)] = dscr("wd%d%d" % (f, l), [4, 128, FC, 512], BF16)
    for j in range(2):
        W["qk%d" % j] = dscr("wqk%d" % j, [32, 128, KC, 128], BF16)
        W["v%d" % j] = dscr("wv%d" % j, [4, 128, KC, 512], BF16)
        W["o%d" % j] = dscr("wo%d" % j, [4, 128, KC, 512], BF16)
    W["sin"] = dscr("wsin", [16, 128, KC, 512], BF16)
    W["sout"] = dscr("wsout", [4, 128, 32, 512], BF16)
    W["pw"] = dscr("wpw", [4, 128, 4, 512], BF16)

    P = Prog(nc)

    def cast_ffn(f, l):
        def go():
            cast_cols(P, I["ffn%d_w_gate" % f][l], W["g%d%d" % (f, l)], "g%d%d" % (f, l), D, DFF, 128)
            cast_cols(P, I["ffn%d_w_up" % f][l], W["u%d%d" % (f, l)], "u%d%d" % (f, l), D, DFF, 128)
            cast_cols(P, I["ffn%d_w_down" % f][l], W["d%d%d" % (f, l)], "d%d%d" % (f, l), DFF, D, 512)
        return go

    def cast_na(j):
        def go():
            cast_cols(P, I["na_w_qkv"][j][:, 0:2 * D], W["qk%d" % j], "qk%d" % j, D, 2 * D, 128)
            cast_cols(P, I["na_w_qkv"][j][:, 2 * D:3 * D], W["v%d" % j], "v%d" % j, D, D, 512)
            cast_cols(P, I["na_w_o"][j], W["o%d" % j], "o%d" % j, D, D, 512)
        return go

    def cast_sgu():
        cast_cols(P, I["sgu_w_in"][0], W["sin"], "sin", D, 8192, 512)
        cast_cols(P, I["sgu_w_out"][0], W["sout"], "sout", 4096, D, 512)

    def cast_pool():
        for g in range(4):
            P.op("pool", lambda e, g=g: e.dma_start(out=W["pw"][g], in_=I["pool_w"][0][g].rearrange("(cc p) d -> p cc d", p=128)),
                 writes=[("pw", g)], dma=True)

    with contextlib.ExitStack() as st:
        ps = st.enter_context(nc.psum_tensor("ps", [128, 8, 512], F32))
        C = {}
        for k in ("ident", "identf", "bd", "ind4"):
            C[k] = st.enter_context(nc.sbuf_tensor(k + "_sb", [128, 128], F32 if k == "identf" else BF16))
            P.op("sp", lambda e, k=k: e.dma_start(out=C[k][:], in_=cin[k]), writes=[k], dma=True)
            P.const.add(k)
        C["eps"] = st.enter_context(nc.sbuf_tensor("eps_sb", [128, 1], F32))
        P.op("dve", lambda e: e.memset(C["eps"][:], EPS), writes=["eps"])
        P.const.add("eps")
        cast_ffn(1, 0)()
        P.flush()

        def vec(nm, l):
            return I[nm][l:l + 1, :]

        def ffn(f, l, lo, hi, Xin, g_pre, nxt_cast):
            if nxt_cast is not None:
                nxt_cast()
            ffn_phase(P, nc, ps, C, "f%d%d" % (f, l), Xin, X, lo, hi, W["g%d%d" % (f, l)], W["u%d%d" % (f, l)],
                      W["d%d%d" % (f, l)], ("g%d%d" % (f, l), "u%d%d" % (f, l), "d%d%d" % (f, l)), g_pre,
                      vec("ffn%d_norm" % f, l))

        def na(j, l, kv_lo, kv_hi, q_lo, q_hi, nxt_cast):
            nxt_cast()
            qt = "q%d" % j
            na_qkv_phase(P, nc, ps, C, qt, X, kv_lo, kv_hi, vec("mix_norm", l), W["qk%d" % j], "qk%d" % j, W["v%d" % j],
                         "v%d" % j, I["na_q_gain"][j:j + 1, :].rearrange("o d -> d o"),
                         I["na_k_gain"][j:j + 1, :].rearrange("o d -> d o"), QT, VX)
            na_attn_phase(P, nc, ps, C, "a%d" % j, NB, q_lo, q_hi, QT, VX, AO, TB[j], RVd, qt, split_tiles(kv_lo, kv_hi, 4))
            na_out_phase(P, nc, ps, C, "o%d" % j, X, q_lo, q_hi, AO, "a%d" % j, W["o%d" % j], "o%d" % j)

        ffn(1, 0, 0, NB, x_loc, None, cast_na(0))
        na(0, 0, 0, NB, 2, 39, cast_ffn(2, 0))
        ffn(2, 0, 2, 39, X, None, cast_ffn(1, 1))
        ffn(1, 1, 2, 39, X, vec("out_norm", 0), cast_sgu)
        cast_ffn(2, 1)()
        sgu_phase(P, nc, ps, C, "sg", X, 2, 39, vec("mix_norm", 1), W["sin"], "sin", W["sout"], "sout",
                  I["sgu_v_gain"], I["sgu_w_s"][0], I["sgu_b_s"][0])
        ffn(2, 1, 2, 39, X, None, cast_ffn(1, 2))
        ffn(1, 2, 2, 39, X, vec("out_norm", 1), cast_pool)
        cast_ffn(2, 2)()
        pool_phase(P, nc, ps, C, "pl", X, HT, 2, 39, vec("mix_norm", 2), W["pw"], "pw", I["pool_scale"], valid_d, invcnt_d, NTOK)
        ffn(2, 2, 3, 39, X, None, cast_ffn(1, 3))
        ffn(1, 3, 3, 39, X, vec("out_norm", 2), cast_na(1))
        na(1, 3, 3, 39, OWN_LO, OWN_HI, cast_ffn(2, 3))
        ffn(2, 3, OWN_LO, OWN_HI, X, None, None)
        final_phase(P, nc, C, "fin", X, out, OWN_LO, OWN_HI, vec("out_norm", 3))
    P.close()
    return nc


def host_inputs(inputs, core):
    b, q = divmod(core, 4)
    row0 = 64 * q - 10
    x = np.asarray(inputs["x"], dtype=np.float32)
    x_loc = np.zeros((NTOK, D), np.float32)
    t_lo, t_hi = row0 * 64, row0 * 64 + NTOK
    a, c = max(t_lo, 0), min(t_hi, SEQ)
    x_loc[a - t_lo:c - t_lo] = x[b, a:c]
    t = np.arange(NTOK) + t_lo
    valid = ((t >= 0) & (t < SEQ)).astype(np.float32).reshape(NTOK, 1)
    invcnt = np.ones((4, NTOK), np.float32)
    for g, w in enumerate((2, 4, 8, 16)):
        lo = np.clip(t - w // 2, 0, SEQ)
        hi = np.clip(t + w // 2, 0, SEQ)
        cnt = np.maximum(hi - lo, 1).astype(np.float32)
        invcnt[g] = np.float32(1.0) / cnt
    m = {"x_loc": x_loc, "valid": valid, "invcnt": invcnt, "rv": build_rv(NB, row0)}
    return m


_NC_CACHE = {}


def _run(inputs, cores):
    if "nc" not in _NC_CACHE:
        _NC_CACHE["nc"] = build_program()
    nc = _NC_CACHE["nc"]
    shared = {}
    for k, v in inputs.items():
        if k != "x":
            shared[k] = np.ascontiguousarray(np.asarray(v, dtype=np.float32))
    rpb = shared.pop("na_rpb")
    shared["tb0"] = build_tb(rpb[0])
    shared["tb1"] = build_tb(rpb[1])
    for k, v in build_consts().items():
        shared["c_" + k] = v
    in_maps = []
    for c in cores:
        m = dict(shared)
        m.update(host_inputs(inputs, c))
        in_maps.append(m)
    res = run_bass_kernel_spmd(nc, in_maps, core_ids=list(range(len(cores))))
    return [np.asarray(r["out"]) for r in res.results]


def kernel(**inputs):
    outs = _run(inputs, list(range(8)))
    full = np.zeros((BATCH, SEQ, D), np.float32)
    for c, o in enumerate(outs):
        b, q = divmod(c, 4)
        full[b, q * 4096:(q + 1) * 4096] = o
    return full
```

```python
import contextlib
import numpy as np
import ml_dtypes
import concourse.bass as bass
import concourse.mybir as mybir
from concourse.bass_utils import run_bass_kernel_spmd

F32 = mybir.dt.float32
BF16 = mybir.dt.bfloat16
AF = mybir.ActivationFunctionType
ALU = mybir.AluOpType
AX = mybir.AxisListType

D = 2048
DFF = 5632
KC = D // 128
FC = DFF // 128
EPS = 1e-6
NSUB_MAX = 7


class _Sem:
    __slots__ = ("h", "cnt", "name")

    def __init__(self, name):
        self.h = None
        self.cnt = 0
        self.name = name


class _Eng:
    def __init__(self, name, ndma=0):
        self.name = name
        self.ops = []
        self.sem = _Sem("c_" + name)
        self.waited = {}
        self.dma_sems = [_Sem("d_%s%d" % (name, i)) for i in range(ndma)]
        self.rr = 0


class Prog:
    def __init__(self, nc, ndma=10):
        self.nc = nc
        self.E = {
            "sp": _Eng("sp", 32),
            "act": _Eng("act", 2),
            "pool": _Eng("pool", 12),
            "dve": _Eng("dve"),
            "pe": _Eng("pe"),
        }
        self.lastw = {}
        self.readers = {}
        self.const = set()
        self.stack = contextlib.ExitStack()
        for s in self.all_sems():
            s.h = self.stack.enter_context(nc.semaphore(s.name))

    def all_sems(self):
        out = []
        for e in self.E.values():
            out.append(e.sem)
            out.extend(e.dma_sems)
        return out

    def op(self, eng, fn, reads=(), writes=(), dma=False):
        e = self.E[eng]
        deps = []
        for k in reads:
            t = self.lastw.get(k)
            if t is not None:
                deps.append(t)
        for k in writes:
            t = self.lastw.get(k)
            if t is not None:
                deps.append(t)
            r = self.readers.get(k)
            if r:
                deps.extend(r.values())
        if dma:
            ds = e.dma_sems[e.rr]
            e.rr = (e.rr + 1) % len(e.dma_sems)
            if ds.cnt > 0:
                deps.append((ds, ds.cnt))
            ds.cnt += 16
            tok = (ds, ds.cnt)
        else:
            e.sem.cnt += 1
            tok = (e.sem, e.sem.cnt)
        newmax = {}
        for (so, v) in deps:
            if e.waited.get(so, 0) < v and newmax.get(so, 0) < v:
                newmax[so] = v
        waits = list(newmax.items())
        for so, v in waits:
            e.waited[so] = v
        e.ops.append((fn, waits, tok, 16 if dma else 1))
        for k in reads:
            if k in self.const:
                continue
            self.readers.setdefault(k, {})[tok[0]] = tok
        for k in writes:
            self.lastw[k] = tok
            self.readers[k] = {}
        return tok

    def wait_all(self, eng):
        e = self.E[eng]
        waits = []
        for s in self.all_sems():
            if s.cnt > 0 and e.waited.get(s, 0) < s.cnt:
                waits.append((s, s.cnt))
                e.waited[s] = s.cnt
        e.ops.append((None, waits, None, 0))

    def flush(self):
        self.wait_all("sp")
        with self.nc.Block() as block:
            def run(e):
                ops = e.ops

                def body(eng):
                    for fn, waits, tok, amt in ops:
                        for so, v in waits:
                            eng.wait_ge(so.h, v)
                        if fn is None:
                            continue
                        ins = fn(eng)
                        if tok is not None:
                            ins.then_inc(tok[0].h, amt)
                return body

            reg = {"sp": block.sync, "act": block.scalar, "pool": block.gpsimd,
                   "dve": block.vector, "pe": block.tensor}
            for n, e in self.E.items():
                if e.ops:
                    reg[n](run(e))
        for e in self.E.values():
            e.ops = []

    def close(self):
        self.stack.close()


def split_tiles(lo, hi, mx):
    n = hi - lo
    nt = -(-n // mx)
    base, rem = divmod(n, nt)
    out = []
    s = lo
    for i in range(nt):
        k = base + (1 if i < rem else 0)
        out.append((s, k))
        s += k
    return out


def cast_cols(P, src, dst, key, K, ncols, cw):
    s = src.rearrange("(kc p) f -> p kc f", p=128)
    ng = ncols // cw
    per = max(1, 4096 // (K // 128 * 128) * 1)
    for g in range(ng):
        P.op("pool", lambda e, g=g: e.dma_start(out=dst[g], in_=s[:, :, g * cw:(g + 1) * cw]),
             writes=[(key, g)], dma=True)


def ffn_phase(P, nc, ps, C, tag, Xin, Xout, st_lo, st_hi, Wg, Wu, Wd, wkeys, g_pre, g_ffn):
    tiles = split_tiles(st_lo, st_hi, NSUB_MAX)
    NTM = NSUB_MAX * 128
    kg, ku, kd = wkeys
    xin_name = Xin.tensor.name
    xout_name = Xout.tensor.name
    with contextlib.ExitStack() as st:
        def sb(name, shape, dt):
            return st.enter_context(nc.sbuf_tensor(tag + name, shape, dt))
        xnT = sb("xnT", [128, KC, NTM], BF16)
        hT = sb("hT", [128, FC, NTM], BF16)
        wgu = sb("wgu", [128, 3, 2, KC, 128], BF16)
        wd = sb("wd", [128, 3, 4, 512], BF16)
        xt = sb("xt", [128, 2, D], F32)
        hn = sb("hn", [128, 2, D], BF16)
        junk = sb("junk", [128, D], BF16)
        gbf = sb("gbf", [128, D], F32)
        gbp = sb("gbp", [128, D], F32) if g_pre is not None else None
        sil = sb("sil", [128, 2, 512], BF16)
        xr = sb("xr", [128, NSUB_MAX, 512], F32)
        stat = sb("stat", [128, 2, 8], F32)

        P.op("sp", lambda e: e.dma_start(out=gbf[:], in_=g_ffn.partition_broadcast(128)),
             writes=[tag + "gbf"], dma=True)
        if g_pre is not None:
            P.op("sp", lambda e: e.dma_start(out=gbp[:], in_=g_pre.partition_broadcast(128)),
                 writes=[tag + "gbp"], dma=True)
        P.const.add(tag + "gbf")
        P.const.add(tag + "gbp")

        cnt = {"x": 0, "wgu": 0, "wd": 0, "r": 0, "sil": 0}

        def pro1(s_):
            sl = cnt["x"] % 2
            cnt["x"] += 1
            rows = slice(s_ * 128, (s_ + 1) * 128)
            kx, khn, kst = (tag + "xt", sl), (tag + "hn", sl), (tag + "stat", sl)
            P.op("sp", lambda e: e.dma_start(out=xt[:, sl, :], in_=Xin[rows, :]),
                 reads=[(xin_name, s_, j) for j in range(4)], writes=[kx], dma=True)
            if g_pre is not None:
                P.op("act", lambda e: e.activation(out=junk[:], in_=xt[:, sl, :], func=AF.Square, accum_out=stat[:, sl, 0:1]),
                     reads=[kx], writes=[tag + "junk", kst])
                P.op("act", lambda e: e.activation(out=stat[:, sl, 1:2], in_=stat[:, sl, 0:1], func=AF.Sqrt,
                                                   bias=C["eps"][:], scale=1.0 / D), reads=[kst], writes=[kst])
                P.op("dve", lambda e: e.reciprocal(out=stat[:, sl, 2:3], in_=stat[:, sl, 1:2]), reads=[kst], writes=[kst])
                P.op("dve", lambda e: e.scalar_tensor_tensor(out=xt[:, sl, :], in0=xt[:, sl, :], scalar=stat[:, sl, 2:3],
                                                             in1=gbp[:], op0=ALU.mult, op1=ALU.mult),
                     reads=[kx, kst, tag + "gbp"], writes=[kx])
                P.op("sp", lambda e: e.dma_start(out=Xout[rows, :], in_=xt[:, sl, :]),
                     reads=[kx], writes=[(xout_name, s_, j) for j in range(4)], dma=True)
            P.op("act", lambda e: e.activation(out=junk[:], in_=xt[:, sl, :], func=AF.Square, accum_out=stat[:, sl, 3:4]),
                 reads=[kx], writes=[tag + "junk", kst])
            P.op("act", lambda e: e.activation(out=stat[:, sl, 4:5], in_=stat[:, sl, 3:4], func=AF.Sqrt,
                                               bias=C["eps"][:], scale=1.0 / D), reads=[kst], writes=[kst])
            P.op("dve", lambda e: e.reciprocal(out=stat[:, sl, 5:6], in_=stat[:, sl, 4:5]), reads=[kst], writes=[kst])
            P.op("dve", lambda e: e.scalar_tensor_tensor(out=hn[:, sl, :], in0=xt[:, sl, :], scalar=stat[:, sl, 5:6],
                                                         in1=gbf[:], op0=ALU.mult, op1=ALU.mult),
                 reads=[kx, kst, tag + "gbf"], writes=[khn])
            return sl, khn

        def pro2(sl, khn, ts, banks=(7,)):
            transpose_to(P, ps, C, lambda c: hn[:, sl, c * 128:(c + 1) * 128], khn, KC,
                         lambda c0, n: xnT[:, c0:c0 + n, ts * 128:(ts + 1) * 128], [(tag + "xnT", ts)], banks=banks)

        def prologue(t0, nsub):
            for ts in range(nsub):
                sl, khn = pro1(t0 + ts)
                pro2(sl, khn, ts, banks=(6, 7))

        def load_wgu(fc):
            sl = cnt["wgu"] % 3
            cnt["wgu"] += 1
            P.op("sp", lambda e: e.dma_start(out=wgu[:, sl, 0], in_=Wg[fc]), reads=[(kg, fc)],
                 writes=[(tag + "wg", sl)], dma=True)
            P.op("sp", lambda e: e.dma_start(out=wgu[:, sl, 1], in_=Wu[fc]), reads=[(ku, fc)],
                 writes=[(tag + "wu", sl)], dma=True)
            return sl

        def load_wd(dg, fg):
            sl = cnt["wd"] % 3
            cnt["wd"] += 1
            P.op("sp", lambda e: e.dma_start(out=wd[:, sl], in_=Wd[dg, :, fg * 4:(fg + 1) * 4, :]),
                 reads=[(kd, dg)], writes=[(tag + "wd", sl)], dma=True)
            return sl

        def pk(bank):
            return [("ps", bank)]

        def gateup(nsub):
            NT = nsub * 128
            halves = [(0, min(512, NT))] + ([(512, NT)] if NT > 512 else [])
            pend = [load_wgu(fc) for fc in range(min(3, FC))]
            for fc in range(FC):
                sl = pend.pop(0)
                par = fc % 2
                for gu in range(2):
                    for hi, (a, b) in enumerate(halves):
                        bank = 4 * par + 2 * gu + hi
                        def mm(e, sl=sl, gu=gu, a=a, b=b, bank=bank):
                            ins = None
                            for kc in range(KC):
                                ins = e.matmul(ps[:, bank, 0:b - a], lhsT=wgu[:, sl, gu, kc, :], rhs=xnT[:, kc, a:b],
                                               start=(kc == 0), stop=(kc == KC - 1))
                            return ins
                        P.op("pe", mm, reads=[(tag + ("wg" if gu == 0 else "wu"), sl)] +
                             [(tag + "xnT", t) for t in range(a // 128, b // 128)], writes=pk(bank))
                for hi, (a, b) in enumerate(halves):
                    ss = cnt["sil"] % 2
                    cnt["sil"] += 1
                    bg, bu = 4 * par + hi, 4 * par + 2 + hi
                    P.op("act", lambda e, ss=ss, a=a, b=b, bg=bg: e.activation(
                        out=sil[:, ss, 0:b - a], in_=ps[:, bg, 0:b - a], func=AF.Silu),
                        reads=pk(bg), writes=[(tag + "sil", ss)])
                    P.op("dve", lambda e, ss=ss, a=a, b=b, bu=bu, fc=fc: e.tensor_tensor(
                        out=hT[:, fc, a:b], in0=ps[:, bu, 0:b - a], in1=sil[:, ss, 0:b - a], op=ALU.mult),
                        reads=pk(bu) + [(tag + "sil", ss)], writes=[(tag + "hT", fc)])
                if fc + 3 < FC:
                    pend.append(load_wgu(fc + 3))

        def down(t0, nsub, nxt):
            Xres = Xout if g_pre is not None else Xin
            xres_name = Xres.tensor.name
            NFG = FC // 4
            seq = [(dg, fg) for dg in range(4) for fg in range(NFG)]
            pend = [load_wd(*seq[i]) for i in range(3)]
            st1 = {}
            for i, (dg, fg) in enumerate(seq):
                cols = slice(dg * 512, (dg + 1) * 512)
                if fg == 0:
                    for ts in range(nsub):
                        s_ = t0 + ts
                        rows = slice(s_ * 128, (s_ + 1) * 128)
                        P.op("sp", lambda e, ts=ts, rows=rows, cols=cols: e.dma_start(out=xr[:, ts, :], in_=Xres[rows, cols]),
                             reads=[(xres_name, s_, dg)], writes=[(tag + "xr", ts)], dma=True)
                sl = pend.pop(0)

                def mm(e, sl=sl, fg=fg, tss=None):
                    ins = None
                    for ts in tss:
                        for fl in range(4):
                            fc = fg * 4 + fl
                            ins = e.matmul(ps[:, ts, :], lhsT=hT[:, fc, ts * 128:(ts + 1) * 128], rhs=wd[:, sl, fl, :],
                                           start=(fc == 0), stop=(fc == FC - 1))
                    return ins
                rk = [(tag + "wd", sl)] + [(tag + "hT", fg * 4 + fl) for fl in range(4)]
                if fg == 0:
                    for ts in range(nsub):
                        P.op("pe", lambda e, mm=mm, ts=ts: mm(e, tss=[ts]), reads=rk, writes=[("ps", ts)])
                else:
                    P.op("pe", lambda e, mm=mm: mm(e, tss=list(range(nsub))), reads=rk,
                         writes=[("ps", ts) for ts in range(nsub)])
                if i + 3 < len(seq):
                    pend.append(load_wd(*seq[i + 3]))
                if fg == NFG - 1:
                    for ts in range(nsub):
                        s_ = t0 + ts
                        rows = slice(s_ * 128, (s_ + 1) * 128)
                        P.op("dve", lambda e, ts=ts: e.scalar_tensor_tensor(
                            out=xr[:, ts, :], in0=ps[:, ts, :], scalar=0.5, in1=xr[:, ts, :], op0=ALU.mult, op1=ALU.add),
                            reads=[("ps", ts), (tag + "xr", ts)], writes=[(tag + "xr", ts)])
                        P.op("sp", lambda e, ts=ts, rows=rows, cols=cols: e.dma_start(out=Xout[rows, cols], in_=xr[:, ts, :]),
                             reads=[(tag + "xr", ts)], writes=[(xout_name, s_, dg)], dma=True)
                if nxt is not None:
                    n0, nn = nxt
                    if i % 3 == 1 and i // 3 <= nn:
                        ts = i // 3
                        if ts >= 1:
                            pro2(st1[ts - 1][0], st1[ts - 1][1], ts - 1, banks=(6, 7) if nsub <= 6 else (7,))
                        if ts < nn:
                            st1[ts] = pro1(n0 + ts)

        for ti, (t0, nsub) in enumerate(tiles):
            if ti == 0:
                prologue(t0, nsub)
            gateup(nsub)
            nxt = tiles[ti + 1] if ti + 1 < len(tiles) else None
            down(t0, nsub, nxt)
        P.flush()


class NormCtx:
    def __init__(self, P, nc, st, C, tag, gvec, out_dt, nslot=2):
        self.P, self.C, self.tag = P, C, tag
        self.xt = st.enter_context(nc.sbuf_tensor(tag + "nxt", [128, nslot, D], F32))
        self.hn = st.enter_context(nc.sbuf_tensor(tag + "nhn", [128, nslot, D], out_dt))
        self.junk = st.enter_context(nc.sbuf_tensor(tag + "njunk", [128, D], BF16))
        self.gb = st.enter_context(nc.sbuf_tensor(tag + "ngb", [128, D], F32))
        self.stat = st.enter_context(nc.sbuf_tensor(tag + "nstat", [128, nslot, 4], F32))
        self.n = 0
        self.nslot = nslot
        gb = self.gb
        P.op("sp", lambda e: e.dma_start(out=gb[:], in_=gvec.partition_broadcast(128)), writes=[tag + "ngb"], dma=True)
        P.const.add(tag + "ngb")

    def run(self, X, s_, valid=None):
        P, C, tag = self.P, self.C, self.tag
        sl = self.n % self.nslot
        self.n += 1
        xt, hn, junk, gb, stat = self.xt, self.hn, self.junk, self.gb, self.stat
        xname = X.tensor.name
        rows = slice(s_ * 128, (s_ + 1) * 128)
        kx, khn, kst = (tag + "nxt", sl), (tag + "nhn", sl), (tag + "nstat", sl)
        P.op("sp", lambda e: e.dma_start(out=xt[:, sl, :], in_=X[rows, :]),
             reads=[(xname, s_, j) for j in range(4)], writes=[kx], dma=True)
        P.op("act", lambda e: e.activation(out=junk[:], in_=xt[:, sl, :], func=AF.Square, accum_out=stat[:, sl, 0:1]),
             reads=[kx], writes=[tag + "njunk", kst])
        P.op("act", lambda e: e.activation(out=stat[:, sl, 1:2], in_=stat[:, sl, 0:1], func=AF.Sqrt,
                                           bias=C["eps"][:], scale=1.0 / D), reads=[kst], writes=[kst])
        P.op("dve", lambda e: e.reciprocal(out=stat[:, sl, 2:3], in_=stat[:, sl, 1:2]), reads=[kst], writes=[kst])
        if valid is not None:
            vap, vkey = valid
            P.op("dve", lambda e: e.tensor_tensor(out=stat[:, sl, 2:3], in0=stat[:, sl, 2:3], in1=vap, op=ALU.mult),
                 reads=[kst, vkey], writes=[kst])
        P.op("dve", lambda e: e.scalar_tensor_tensor(out=hn[:, sl, :], in0=xt[:, sl, :], scalar=stat[:, sl, 2:3],
                                                     in1=gb[:], op0=ALU.mult, op1=ALU.mult),
             reads=[kx, kst, tag + "ngb"], writes=[khn])
        return sl, khn


def transpose_to(P, ps, C, src_ap_fn, src_key, nchunk, dst_fn, dst_keys, bank=7, fp32=False, banks=None):
    per = 4 if fp32 else 8
    ident = C["identf"] if fp32 else C["ident"]
    ikey = "identf" if fp32 else "ident"
    banks = banks or (bank,)
    for c0 in range(0, nchunk, per):
        n = min(per, nchunk - c0)
        r = C.setdefault("trr", 0)
        C["trr"] = r + 1
        bk = banks[r % len(banks)]

        def view(bk=bk):
            return ps[:, bk, :] if fp32 else ps[:, bk, :].bitcast(BF16)

        def tr(e, c0=c0, n=n, view=view):
            pv = view()
            ins = None
            for j in range(n):
                ins = e.transpose(out=pv[:, j * 128:(j + 1) * 128], in_=src_ap_fn(c0 + j), identity=ident[:])
            return ins
        P.op("pe", tr, reads=[src_key, ikey], writes=[("ps", bk)])

        def ev(e, c0=c0, n=n, r=r, view=view):
            src = view()[:, 0:n * 128].rearrange("p (j t) -> p j t", j=n)
            if r % 2 == 0:
                return e.tensor_copy(out=dst_fn(c0, n), in_=src)
            return e.copy(out=dst_fn(c0, n), in_=src)
        P.op("dve" if r % 2 == 0 else "act", ev, reads=[("ps", bk)], writes=dst_keys)


def final_phase(P, nc, C, tag, X, out, st_lo, st_hi, gvec):
    with contextlib.ExitStack() as st:
        N = NormCtx(P, nc, st, C, tag, gvec, F32, nslot=3)
        for s_ in range(st_lo, st_hi):
            sl, khn = N.run(X, s_)
            o = s_ - st_lo
            P.op("sp", lambda e, sl=sl, o=o: e.dma_start(out=out[o * 128:(o + 1) * 128, :], in_=N.hn[:, sl, :]),
                 reads=[khn], writes=[("out", o)], dma=True)
        P.flush()


def pool_phase(P, nc, ps, C, tag, X, HT, st_lo, st_hi, gvec, Wp, wkey, scale_vec, valid_d, invcnt_d, NTOK):
    with contextlib.ExitStack() as st:
        N = NormCtx(P, nc, st, C, tag + "A", gvec, F32, nslot=2)
        hts = st.enter_context(nc.sbuf_tensor(tag + "hts", [128, 2, KC, 128], F32))
        vt = st.enter_context(nc.sbuf_tensor(tag + "vt", [128, 2, 1], F32))
        zt = st.enter_context(nc.sbuf_tensor(tag + "zt", [128, KC, 8], F32))
        P.op("dve", lambda e: e.memset(zt[:], 0.0), writes=[tag + "zt"])
        P.op("sp", lambda e: e.dma_start(out=HT[:, :, st_lo * 128:st_lo * 128 + 8].rearrange("c p t -> p c t"), in_=zt[:]),
             reads=[tag + "zt"], writes=[(tag + "HTpad", 0)], dma=True)
        P.op("sp", lambda e: e.dma_start(out=HT[:, :, st_hi * 128 + 8:st_hi * 128 + 16].rearrange("c p t -> p c t"), in_=zt[:]),
             reads=[tag + "zt"], writes=[(tag + "HTpad", 1)], dma=True)
        for i, s_ in enumerate(range(st_lo, st_hi)):
            hs = i % 2
            P.op("sp", lambda e, hs=hs, s_=s_: e.dma_start(out=vt[:, hs, :], in_=valid_d[s_ * 128:(s_ + 1) * 128, :]),
                 writes=[(tag + "vt", hs)], dma=True)
            sl, khn = N.run(X, s_, valid=(vt[:, hs, :], (tag + "vt", hs)))
            transpose_to(P, ps, C, lambda c, sl=sl: N.hn[:, sl, c * 128:(c + 1) * 128], khn, KC,
                         lambda c0, n, hs=hs: hts[:, hs, c0:c0 + n, :], [(tag + "hts", hs)], banks=(6, 7), fp32=True)
            P.op("sp", lambda e, hs=hs, s_=s_: e.dma_start(
                out=HT[:, :, 8 + s_ * 128:8 + (s_ + 1) * 128].rearrange("c p t -> p c t"), in_=hts[:, hs]),
                reads=[(tag + "hts", hs)], writes=[(tag + "HT", s_)], dma=True)
        P.flush()
    TS = 4
    tiles = split_tiles(st_lo, st_hi, TS)
    xname = X.tensor.name
    with contextlib.ExitStack() as st:
        def sb(name, shape, dt):
            return st.enter_context(nc.sbuf_tensor(tag + name, shape, dt))
        W = TS * 128
        hg = sb("hg", [128, 2, 4, W + 16], F32)
        ca = sb("ca", [128, 4, W + 16], F32)
        cb = sb("cb", [128, 4, W + 16], F32)
        ic = sb("ic", [128, 2, W], F32)
        pT = sb("pT", [128, 2, 4, W], BF16)
        wp = sb("wp", [128, 4, 4, 512], BF16)
        sc = sb("sc", [128, D], F32)
        xr = sb("xr", [128, 3, 512], F32)
        xo = sb("xo", [128, 3, 512], F32)
        tm = sb("tm", [128, 2, 512], F32)
        P.op("sp", lambda e: e.dma_start(out=sc[:], in_=scale_vec.partition_broadcast(128)), writes=[tag + "sc"], dma=True)
        for g in range(4):
            P.op("sp", lambda e, g=g: e.dma_start(out=wp[:, g], in_=Wp[g]), reads=[(wkey, g)], writes=[tag + "wp"], dma=True)
        P.const.add(tag + "sc")
        k = 0
        r = 0
        for (t0, nsub) in tiles:
            n = nsub * 128
            tok0 = t0 * 128
            for g in range(4):
                w = 2 << g
                sl = k % 2
                k += 1
                khg, kic, kpT = (tag + "hg", sl), (tag + "ic", sl), (tag + "pT", sl)
                P.op("sp", lambda e, sl=sl, g=g, tok0=tok0, n=n: e.dma_start(
                    out=hg[:, sl, :, 0:n + 16], in_=HT[4 * g:4 * g + 4, :, tok0:tok0 + n + 16].rearrange("c p t -> p c t")),
                    reads=[(tag + "HT", s_) for s_ in range(max(st_lo, t0 - 1), min(st_hi, t0 + nsub + 1))] +
                    [(tag + "HTpad", 0), (tag + "HTpad", 1)], writes=[khg], dma=True)
                P.op("sp", lambda e, sl=sl, g=g, tok0=tok0, n=n: e.dma_start(
                    out=ic[:, sl, 0:n], in_=invcnt_d[g:g + 1, tok0:tok0 + n].partition_broadcast(128)),
                    writes=[kic], dma=True)
                L = n + 16
                cur, curk = (lambda a, b, sl=sl: hg[:, sl, :, a:b]), khg
                bufs = [(ca, tag + "ca"), (cb, tag + "cb")]
                step = 1
                for lv in range(g + 1):
                    dst, dk = bufs[lv % 2]
                    Ln = L - (2 * step - 1)
                    P.op("dve",
                         lambda e, cur=cur, dst=dst, step=step, Ln=Ln: e.tensor_tensor(
                             out=dst[:, :, 0:Ln], in0=cur(0, Ln), in1=cur(step, step + Ln), op=ALU.add),
                         reads=[curk], writes=[dk])
                    cur, curk = (lambda a, b, dst=dst: dst[:, :, a:b]), dk
                    step *= 2
                off = 8 - w // 2
                dst, dk = bufs[(g + 1) % 2]
                P.op("dve", lambda e, cur=cur, dst=dst, off=off, n=n, sl=sl: e.tensor_tensor(
                    out=dst[:, :, 0:n], in0=cur(off, off + n),
                    in1=ic[:, sl, 0:n].unsqueeze(1).broadcast_to([128, 4, n]), op=ALU.mult),
                    reads=[curk, kic], writes=[dk])
                P.op("dve", lambda e, dst=dst, n=n, sl=sl: e.tensor_tensor(
                    out=pT[:, sl, :, 0:n], in0=dst[:, :, 0:n], in1=hg[:, sl, :, 8:8 + n], op=ALU.subtract),
                    reads=[dk, khg], writes=[kpT])
                for ts in range(nsub):
                    s_ = t0 + ts
                    bank = (r % 6)
                    rs = r % 3
                    tsl = r % 2
                    r += 1

                    def mm(e, sl=sl, ts=ts, g=g, bank=bank):
                        ins = None
                        for cc in range(4):
                            ins = e.matmul(ps[:, bank, :], lhsT=pT[:, sl, cc, ts * 128:(ts + 1) * 128], rhs=wp[:, g, cc, :],
                                           start=(cc == 0), stop=(cc == 3))
                        return ins
                    P.op("pe", mm, reads=[kpT, tag + "wp"], writes=[("ps", bank)])
                    rows = slice(s_ * 128, (s_ + 1) * 128)
                    cols = slice(g * 512, (g + 1) * 512)
                    P.op("sp", lambda e, rs=rs, rows=rows, cols=cols: e.dma_start(out=xr[:, rs, :], in_=X[rows, cols]),
                         reads=[(xname, s_, g)], writes=[(tag + "xr", rs)], dma=True)
                    P.op("dve", lambda e, tsl=tsl, bank=bank, cols=cols: e.tensor_tensor(
                        out=tm[:, tsl, :], in0=ps[:, bank, :], in1=sc[:, cols], op=ALU.mult),
                        reads=[("ps", bank), tag + "sc"], writes=[(tag + "tm", tsl)])
                    P.op("dve", lambda e, tsl=tsl, rs=rs: e.tensor_tensor(
                        out=xo[:, rs, :], in0=tm[:, tsl, :], in1=xr[:, rs, :], op=ALU.add),
                        reads=[(tag + "tm", tsl), (tag + "xr", rs)], writes=[(tag + "xo", rs)])
                    P.op("sp", lambda e, rs=rs, rows=rows, cols=cols: e.dma_start(out=X[rows, cols], in_=xo[:, rs, :]),
                         reads=[(tag + "xo", rs)], writes=[(xname, s_, g)], dma=True)
        P.flush()


def emit_out_proj(P, ps, tag, X, t0, nsub, aT, aT_keys, nk, w_ap_fn, w_keys_fn, xr, xo, cnt, banks):
    xname = X.tensor.name
    for dg in range(4):
        cols = slice(dg * 512, (dg + 1) * 512)
        for ts in range(nsub):
            s_ = t0 + ts
            bank = banks[cnt["b"] % len(banks)]
            cnt["b"] += 1
            rs = cnt["r"] % 6
            cnt["r"] += 1
            rows = slice(s_ * 128, (s_ + 1) * 128)
            P.op("sp", lambda e, rs=rs, rows=rows, cols=cols: e.dma_start(out=xr[:, rs, :], in_=X[rows, cols]),
                 reads=[(xname, s_, dg)], writes=[(tag + "xr", rs)], dma=True)

            def mm(e, ts=ts, dg=dg, bank=bank):
                ins = None
                for k in range(nk):
                    ins = e.matmul(ps[:, bank, :], lhsT=aT[:, k, ts * 128:(ts + 1) * 128], rhs=w_ap_fn(dg, k),
                                   start=(k == 0), stop=(k == nk - 1))
                return ins
            P.op("pe", mm, reads=list(aT_keys) + list(w_keys_fn(dg)), writes=[("ps", bank)])
            P.op("dve", lambda e, rs=rs, bank=bank: e.tensor_tensor(out=xr[:, rs, :], in0=ps[:, bank, :], in1=xr[:, rs, :],
                                                                     op=ALU.add),
                 reads=[("ps", bank), (tag + "xr", rs)], writes=[(tag + "xr", rs)])
            P.op("sp", lambda e, rs=rs, rows=rows, cols=cols: e.dma_start(out=X[rows, cols], in_=xr[:, rs, :]),
                 reads=[(tag + "xr", rs)], writes=[(xname, s_, dg)], dma=True)


def sgu_phase(P, nc, ps, C, tag, X, st_lo, st_hi, gvec, Win, kin, Wout, kout, vgain, w_s, b_s):
    TS = 3
    DS = 4096
    tiles = split_tiles(st_lo, st_hi, TS)
    with contextlib.ExitStack() as st:
        def sb(name, shape, dt):
            return st.enter_context(nc.sbuf_tensor(tag + name, shape, dt))
        N = NormCtx(P, nc, st, C, tag, gvec, BF16, nslot=2)
        xnT = sb("xnT", [128, KC, TS * 128], BF16)
        wsl = sb("wsl", [128, 2, 16, 512], BF16)
        U = sb("U", [128, TS, DS], BF16)
        V = sb("V", [128, TS, DS], BF16)
        vg = sb("vg", [128, DS], F32)
        G = sb("G", [128, DS], BF16)
        GT = sb("GT", [128, 32, TS * 128], BF16)
        wsT = sb("wsT", [128, 16, 128], BF16)
        bs = sb("bs", [128, 16], F32)
        xr = sb("xr", [128, 3, 512], F32)
        xo = sb("xo", [128, 3, 512], F32)
        vst = sb("vst", [128, 4], F32)
        P.op("sp", lambda e: e.dma_start(out=vg[:], in_=vgain.partition_broadcast(128)), writes=[tag + "vg"], dma=True)
        P.op("sp", lambda e: e.dma_start(out=bs[:], in_=b_s.rearrange("g i -> i g"), allow_slow_non_contiguous=True),
             writes=[tag + "bs"], dma=True)
        wstage = N.xt
        P.op("sp", lambda e: e.dma_start(out=wstage[:, 0, :].rearrange("p (g j) -> p g j", g=16),
                                         in_=w_s.rearrange("g i j -> i g j")), writes=[(tag + "nxt", 0)], dma=True)
        transpose_to(P, ps, C, lambda c: wstage[:, 0, c * 128:(c + 1) * 128], (tag + "nxt", 0), 16,
                     lambda c0, n: wsT[:, c0:c0 + n, :], [tag + "wsT"], bank=7, fp32=True)
        for k in ("vg", "bs", "wsT"):
            P.const.add(tag + k)
        cnt = {"w": 0, "b": 0, "r": 0}

        def loadw(src, keys):
            sl = cnt["w"] % 2
            cnt["w"] += 1
            P.op("sp", lambda e: e.dma_start(out=wsl[:, sl], in_=src), reads=keys, writes=[(tag + "wsl", sl)], dma=True)
            return sl

        for (t0, nsub) in tiles:
            for ts in range(nsub):
                sl, khn = N.run(X, t0 + ts)
                transpose_to(P, ps, C, lambda c, sl=sl: N.hn[:, sl, c * 128:(c + 1) * 128], khn, KC,
                             lambda c0, n, ts=ts: xnT[:, c0:c0 + n, ts * 128:(ts + 1) * 128], [(tag + "xnT", ts)], banks=(6, 7))
            nxt = loadw(Win[0], [(kin, 0)])
            for cg in range(16):
                sl = nxt
                if cg + 1 < 16:
                    nxt = loadw(Win[cg + 1], [(kin, cg + 1)])
                for ts in range(nsub):
                    bank = cnt["b"] % 6
                    cnt["b"] += 1

                    def mm(e, sl=sl, ts=ts, bank=bank):
                        ins = None
                        for kc in range(KC):
                            ins = e.matmul(ps[:, bank, :], lhsT=xnT[:, kc, ts * 128:(ts + 1) * 128], rhs=wsl[:, sl, kc, :],
                                           start=(kc == 0), stop=(kc == KC - 1))
                        return ins
                    P.op("pe", mm, reads=[(tag + "xnT", ts), (tag + "wsl", sl)], writes=[("ps", bank)])
                    dst = U if cg < 8 else V
                    dk = (tag + ("U" if cg < 8 else "V"), ts, cg % 8)
                    c0 = (cg % 8) * 512
                    P.op("act", lambda e, dst=dst, ts=ts, c0=c0, bank=bank: e.activation(
                        out=dst[:, ts, c0:c0 + 512], in_=ps[:, bank, :], func=AF.Gelu), reads=[("ps", bank)], writes=[dk])
            for ts in range(nsub):
                vk = [(tag + "V", ts, j) for j in range(8)]
                P.op("act", lambda e, ts=ts: e.activation(out=G[:], in_=V[:, ts, :], func=AF.Square, accum_out=vst[:, 0:1]),
                     reads=vk, writes=[tag + "G", tag + "vst"])
                P.op("act", lambda e: e.activation(out=vst[:, 1:2], in_=vst[:, 0:1], func=AF.Sqrt, bias=C["eps"][:],
                                                   scale=1.0 / DS), reads=[tag + "vst"], writes=[tag + "vst"])
                P.op("dve", lambda e: e.reciprocal(out=vst[:, 2:3], in_=vst[:, 1:2]), reads=[tag + "vst"], writes=[tag + "vst"])
                P.op("dve", lambda e, ts=ts: e.scalar_tensor_tensor(out=V[:, ts, :], in0=V[:, ts, :], scalar=vst[:, 2:3],
                                                                    in1=vg[:], op0=ALU.mult, op1=ALU.mult),
                     reads=vk + [tag + "vst", tag + "vg"], writes=vk)
            for ts in range(nsub):
                vk = [(tag + "V", ts, j) for j in range(8)]
                uk = [(tag + "U", ts, j) for j in range(8)]
                for gp in range(8):
                    bank = cnt["b"] % 6
                    cnt["b"] += 1

                    def mm(e, ts=ts, gp=gp, bank=bank):
                        ins = None
                        for q in range(2):
                            g = 2 * gp + q
                            ins = e.matmul(ps[:, bank, q * 256:(q + 1) * 256], lhsT=wsT[:, g, :],
                                           rhs=V[:, ts, g * 256:(g + 1) * 256], start=True, stop=True)
                        return ins
                    P.op("pe", mm, reads=vk + [tag + "wsT"], writes=[("ps", bank)])
                    for q in range(2):
                        g = 2 * gp + q
                        P.op("dve", lambda e, ts=ts, g=g, q=q, bank=bank: e.scalar_tensor_tensor(
                            out=G[:, g * 256:(g + 1) * 256], in0=ps[:, bank, q * 256:(q + 1) * 256], scalar=bs[:, g:g + 1],
                            in1=U[:, ts, g * 256:(g + 1) * 256], op0=ALU.add, op1=ALU.mult),
                            reads=[("ps", bank), tag + "bs"] + uk, writes=[tag + "G"])
                transpose_to(P, ps, C, lambda c: G[:, c * 128:(c + 1) * 128], tag + "G", 32,
                             lambda c0, n, ts=ts: GT[:, c0:c0 + n, ts * 128:(ts + 1) * 128], [(tag + "GT", ts)], banks=(6, 7))
            xname = X.tensor.name
            nxt = loadw(Wout[0, :, 0:16, :], [(kout, 0)])
            for dg in range(4):
                cols = slice(dg * 512, (dg + 1) * 512)
                for half in range(2):
                    sl = nxt
                    nh = dg * 2 + half + 1
                    if nh < 8:
                        nxt = loadw(Wout[nh // 2, :, (nh % 2) * 16:(nh % 2) * 16 + 16, :], [(kout, nh // 2)])
                    for ts in range(nsub):
                        bank = ts

                        def mm(e, sl=sl, ts=ts, half=half, bank=bank):
                            ins = None
                            for k in range(16):
                                fc = half * 16 + k
                                ins = e.matmul(ps[:, bank, :], lhsT=GT[:, fc, ts * 128:(ts + 1) * 128], rhs=wsl[:, sl, k, :],
                                               start=(fc == 0), stop=(fc == 31))
                            return ins
                        P.op("pe", mm, reads=[(tag + "GT", ts), (tag + "wsl", sl)], writes=[("ps", bank)])
                for ts in range(nsub):
                    s_ = t0 + ts
                    rs = cnt["r"] % 3
                    cnt["r"] += 1
                    rows = slice(s_ * 128, (s_ + 1) * 128)
                    P.op("sp", lambda e, rs=rs, rows=rows, cols=cols: e.dma_start(out=xr[:, rs, :], in_=X[rows, cols]),
                         reads=[(xname, s_, dg)], writes=[(tag + "xr", rs)], dma=True)
                    P.op("dve", lambda e, rs=rs, ts=ts: e.tensor_tensor(out=xo[:, rs, :], in0=ps[:, ts, :], in1=xr[:, rs, :],
                                                                         op=ALU.add),
                         reads=[("ps", ts), (tag + "xr", rs)], writes=[(tag + "xo", rs)])
                    P.op("sp", lambda e, rs=rs, rows=rows, cols=cols: e.dma_start(out=X[rows, cols], in_=xo[:, rs, :]),
                         reads=[(tag + "xo", rs)], writes=[(xname, s_, dg)], dma=True)
        P.flush()


NH = 64
HD = 32


def na_qkv_phase(P, nc, ps, C, tag, X, st_lo, st_hi, gvec, Wqk, kqk, Wv, kv, qgain, kgain, QT, VX):
    TS = 4
    tiles = split_tiles(st_lo, st_hi, TS)
    with contextlib.ExitStack() as st:
        def sb(name, shape, dt):
            return st.enter_context(nc.sbuf_tensor(tag + name, shape, dt))
        N = NormCtx(P, nc, st, C, tag, gvec, BF16, nslot=2)
        xnT = sb("xnT", [128, KC, TS * 128], BF16)
        wq = sb("wq", [128, 3, KC, 128], BF16)
        wv = sb("wv", [128, 2, KC, 512], BF16)
        sq = sb("sq", [128, 2, 512], BF16)
        lnt = sb("lnt", [128, 2, 512], F32)
        rstd = sb("rstd", [128, 2, 512], F32)
        qo = sb("qo", [128, 3, 512], BF16)
        vx = sb("vx", [128, TS, NH, HD + 1], BF16)
        gqk = sb("gqk", [128, 2], F32)
        for hl in range(4):
            P.op("sp", lambda e, hl=hl: e.dma_start(out=gqk[32 * hl:32 * hl + 32, 0:1], in_=qgain), writes=[tag + "gqk"], dma=True)
            P.op("sp", lambda e, hl=hl: e.dma_start(out=gqk[32 * hl:32 * hl + 32, 1:2], in_=kgain), writes=[tag + "gqk"], dma=True)
        P.op("dve", lambda e: e.tensor_scalar(out=gqk[:, 0:1], in0=gqk[:, 0:1], scalar1=float(HD) ** -0.5, scalar2=None,
                                              op0=ALU.mult), reads=[tag + "gqk"], writes=[tag + "gqk"])
        P.op("dve", lambda e: e.memset(vx[:, :, :, HD:HD + 1], 1.0), writes=[(tag + "vx", ts) for ts in range(TS)])
        P.const.add(tag + "gqk")
        cnt = {"wq": 0, "wv": 0, "s": 0, "q": 0, "b": 0, "ev": 0}

        def load_wq(fo):
            sl = cnt["wq"] % 3
            cnt["wq"] += 1
            P.op("sp", lambda e: e.dma_start(out=wq[:, sl], in_=Wqk[fo]), reads=[(kqk, fo)], writes=[(tag + "wq", sl)], dma=True)
            return sl

        def load_wv(cg):
            sl = cnt["wv"] % 2
            cnt["wv"] += 1
            P.op("sp", lambda e: e.dma_start(out=wv[:, sl], in_=Wv[cg]), reads=[(kv, cg)], writes=[(tag + "wv", sl)], dma=True)
            return sl

        for (t0, nsub) in tiles:
            n = nsub * 128
            tok0 = t0 * 128
            for ts in range(nsub):
                sl, khn = N.run(X, t0 + ts)
                transpose_to(P, ps, C, lambda c, sl=sl: N.hn[:, sl, c * 128:(c + 1) * 128], khn, KC,
                             lambda c0, n_, ts=ts: xnT[:, c0:c0 + n_, ts * 128:(ts + 1) * 128], [(tag + "xnT", ts)], bank=7)
            xk = [(tag + "xnT", ts) for ts in range(nsub)]
            pend = [load_wq(fo) for fo in range(3)]
            for fo in range(32):
                sl = pend.pop(0)
                bA = cnt["b"] % 3
                bB = 3 + cnt["b"] % 2
                cnt["b"] += 1
                ss = cnt["s"] % 2
                cnt["s"] += 1
                qs = cnt["q"] % 3
                cnt["q"] += 1

                def mm(e, sl=sl, bA=bA, n=n):
                    ins = None
                    for kc in range(KC):
                        ins = e.matmul(ps[:, bA, 0:n], lhsT=wq[:, sl, kc, :], rhs=xnT[:, kc, 0:n], start=(kc == 0), stop=(kc == KC - 1))
                    return ins
                P.op("pe", mm, reads=xk + [(tag + "wq", sl)], writes=[("ps", bA)])
                if fo + 3 < 32:
                    pend.append(load_wq(fo + 3))
                P.op("act", lambda e, ss=ss, bA=bA, n=n: e.activation(out=sq[:, ss, 0:n], in_=ps[:, bA, 0:n], func=AF.Square),
                     reads=[("ps", bA)], writes=[(tag + "sq", ss)])
                P.op("pe", lambda e, ss=ss, bB=bB, n=n: e.matmul(ps[:, bB, 0:n], lhsT=C["bd"][:], rhs=sq[:, ss, 0:n],
                                                                  start=True, stop=True),
                     reads=[(tag + "sq", ss), "bd"], writes=[("ps", bB)])
                P.op("act", lambda e, ss=ss, bB=bB, n=n: e.activation(out=lnt[:, ss, 0:n], in_=ps[:, bB, 0:n], func=AF.Ln,
                                                                       bias=C["eps"][:], scale=1.0),
                     reads=[("ps", bB)], writes=[(tag + "lnt", ss)])
                P.op("act", lambda e, ss=ss, n=n: e.activation(out=rstd[:, ss, 0:n], in_=lnt[:, ss, 0:n], func=AF.Exp, scale=-0.5),
                     reads=[(tag + "lnt", ss)], writes=[(tag + "rstd", ss)])
                gi = 0 if fo < 16 else 1
                P.op("dve", lambda e, ss=ss, qs=qs, bA=bA, n=n, gi=gi: e.scalar_tensor_tensor(
                    out=qo[:, qs, 0:n], in0=ps[:, bA, 0:n], scalar=gqk[:, gi:gi + 1], in1=rstd[:, ss, 0:n],
                    op0=ALU.mult, op1=ALU.mult), reads=[("ps", bA), (tag + "rstd", ss), tag + "gqk"], writes=[(tag + "qo", qs)])
                P.op("sp", lambda e, qs=qs, fo=fo, tok0=tok0, n=n: e.dma_start(out=QT[fo, :, tok0:tok0 + n], in_=qo[:, qs, 0:n]),
                     reads=[(tag + "qo", qs)], writes=[(tag + "QT", fo, t0)], dma=True)
            nxt = load_wv(0)
            for cg in range(4):
                sl = nxt
                if cg + 1 < 4:
                    nxt = load_wv(cg + 1)
                for ts in range(nsub):
                    bank = 5 + cnt["ev"] % 2
                    ev = cnt["ev"]
                    cnt["ev"] += 1

                    def mm(e, sl=sl, ts=ts, bank=bank):
                        ins = None
                        for kc in range(KC):
                            ins = e.matmul(ps[:, bank, :], lhsT=xnT[:, kc, ts * 128:(ts + 1) * 128], rhs=wv[:, sl, kc, :],
                                           start=(kc == 0), stop=(kc == KC - 1))
                        return ins
                    P.op("pe", mm, reads=[(tag + "xnT", ts), (tag + "wv", sl)], writes=[("ps", bank)])
                    if ev % 2 == 0:
                        P.op("dve", lambda e, ts=ts, cg=cg, bank=bank: e.tensor_copy(
                            out=vx[:, ts, 16 * cg:16 * cg + 16, 0:HD], in_=ps[:, bank, :].rearrange("p (h d) -> p h d", h=16)),
                            reads=[("ps", bank)], writes=[(tag + "vx", ts)])
                    else:
                        P.op("act", lambda e, ts=ts, cg=cg, bank=bank: e.copy(
                            out=vx[:, ts, 16 * cg:16 * cg + 16, 0:HD], in_=ps[:, bank, :].rearrange("p (h d) -> p h d", h=16)),
                            reads=[("ps", bank)], writes=[(tag + "vx", ts)])
            for ts in range(nsub):
                s_ = t0 + ts
                P.op("sp", lambda e, ts=ts, s_=s_: e.dma_start(out=VX[s_ * 128:(s_ + 1) * 128, :],
                                                                  in_=vx[:, ts].rearrange("p h d -> p (h d)")),
                     reads=[(tag + "vx", ts)], writes=[(tag + "VX", s_)], dma=True)
        P.flush()


def na_attn_phase(P, nc, ps, C, tag, NB, q_lo, q_hi, QT, VX, AO, TB, RVd, qkv_tag, qkv_tiles):
    NTOK = NB * 128
    LOOK = 2
    NS = LOOK + 1
    with contextlib.ExitStack() as st:
        def sb(name, shape, dt):
            return st.enter_context(nc.sbuf_tensor(tag + name, shape, dt))
        Kc = sb("Kc", [128, 2, NTOK], BF16)
        Qc = sb("Qc", [128, 2, NTOK], BF16)
        Vc = sb("Vc", [128, 2, NB, 4, HD + 1], BF16)
        Gt = sb("Gt", [128, 2, 4, 896], F32)
        RVc = sb("RVc", [128, NB, 14], BF16)
        RVx = sb("RVx", [128, 3, 14, 64], BF16)
        Sg = sb("Sg", [128, NS, 896], F32)
        PT = sb("PT", [128, NS, 896], BF16)
        rec = sb("rec", [128, 2, 4], F32)
        aot = sb("aot", [128, 2, 4, HD], BF16)
        P.op("dve", lambda e: e.memset(RVc[:], 0.0), writes=[tag + "RVc"])
        for hl in range(4):
            P.op("sp", lambda e, hl=hl: e.dma_start(out=RVc[32 * hl:32 * hl + 2, :, :], in_=RVd.rearrange("b r c -> r b c")),
                 writes=[tag + "RVc"], dma=True)
        P.const.add(tag + "RVc")

        def load_chunk(c):
            cs = c % 2
            qk_keys = [(qkv_tag + "QT", fo, t0) for fo in (c, 16 + c) for (t0, _) in qkv_tiles]
            P.op("sp", lambda e: e.dma_start(out=Kc[:, cs, :], in_=QT[16 + c]), reads=qk_keys, writes=[(tag + "Kc", cs)], dma=True)
            P.op("sp", lambda e: e.dma_start(out=Qc[:, cs, :], in_=QT[c]), reads=qk_keys, writes=[(tag + "Qc", cs)], dma=True)
            P.op("sp", lambda e: e.dma_start(
                out=Vc[:, cs], in_=VX.rearrange("(j p) (h d) -> p j h d", p=128, d=HD + 1)[:, :, 4 * c:4 * c + 4, :]),
                reads=[(qkv_tag + "VX", s_) for s_ in range(NB)], writes=[(tag + "Vc", cs)], dma=True)
            P.op("sp", lambda e: e.dma_start(out=Gt[:, cs], in_=TB[4 * c:4 * c + 4].rearrange("h p f -> p h f")),
                 writes=[(tag + "Gt", cs)], dma=True)

        items = [(c, i, hl) for c in range(16) for i in range(q_lo, q_hi) for hl in range(4)]
        ngrp_per_c = q_hi - q_lo

        def geom(i):
            jlo, jhi = max(0, i - 3), min(NB - 1, i + 3)
            return jlo, jhi, (jlo - (i - 3)) * 128, (jhi - (i - 3) + 1) * 128

        def stage_s(k):
            c, i, hl = items[k]
            cs = c % 2
            grp = k // 4
            xs = grp % 3
            jlo, jhi, a0, a1 = geom(i)
            if hl == 0:
                if i == q_lo:
                    load_chunk(c)
                P.op("act", lambda e: e.copy(out=RVx[:, xs], in_=RVc[:, i, :].unsqueeze(2).broadcast_to([128, 14, 64])),
                     reads=[tag + "RVc"], writes=[(tag + "RVx", xs)])
            ssl = k % NS
            b0 = 2 * ssl
            pb = 32 * hl

            def mm_s(e):
                Sv = ps[:, b0:b0 + 2, :].rearrange("p a c -> p (a c)")
                rv = RVx[:, xs].rearrange("p a b -> p (a b)")
                ins = None
                for (lo, hi) in ((a0, min(a1, 512)), (max(a0, 512), a1)):
                    if hi > lo:
                        ins = e.matmul(Sv[:, lo:hi], lhsT=C["ind4"][pb:pb + 2, :], rhs=rv[pb:pb + 2, lo:hi],
                                       start=True, stop=False, tile_position=(pb, 0), skip_group_check=True)
                for j in range(jlo, jhi + 1):
                    o = j - (i - 3)
                    ins = e.matmul(Sv[:, o * 128:(o + 1) * 128], lhsT=Kc[pb:pb + 32, cs, j * 128:(j + 1) * 128],
                                   rhs=Qc[pb:pb + 32, cs, i * 128:(i + 1) * 128], start=False, stop=True,
                                   tile_position=(pb, 0), skip_group_check=True)
                return ins
            P.op("pe", mm_s, reads=[(tag + "RVx", xs), (tag + "Kc", cs), (tag + "Qc", cs), "ind4"],
                 writes=[("ps", b0), ("ps", b0 + 1)])
            P.op("dve", lambda e: e.tensor_tensor(
                out=Sg[:, ssl, a0:a1], in0=ps[:, b0:b0 + 2, :].rearrange("p a c -> p (a c)")[:, a0:a1],
                in1=Gt[:, cs, hl, a0:a1], op=ALU.add),
                reads=[("ps", b0), ("ps", b0 + 1), (tag + "Gt", cs)], writes=[(tag + "Sg", ssl)])
            P.op("act", lambda e: e.activation(out=PT[:, ssl, a0:a1], in_=Sg[:, ssl, a0:a1], func=AF.Exp),
                 reads=[(tag + "Sg", ssl)], writes=[(tag + "PT", ssl)])

        def stage_o(k):
            c, i, hl = items[k]
            cs = c % 2
            grp = k // 4
            ob = 6 + grp % 2
            osl = grp % 2
            ssl = k % NS
            jlo, jhi, a0, a1 = geom(i)

            def mm_o(e):
                ins = None
                for j in range(jlo, jhi + 1):
                    o = j - (i - 3)
                    ins = e.matmul(ps[:, ob, hl * (HD + 1):(hl + 1) * (HD + 1)], lhsT=PT[:, ssl, o * 128:(o + 1) * 128],
                                   rhs=Vc[:, cs, j, hl, :], start=(j == jlo), stop=(j == jhi))
                return ins
            P.op("pe", mm_o, reads=[(tag + "PT", ssl), (tag + "Vc", cs)], writes=[("ps", ob)])
            if hl == 3:
                def Ov():
                    return ps[:, ob, 0:4 * (HD + 1)].rearrange("p (h d) -> p h d", h=4)
                P.op("dve", lambda e: e.tensor_scalar(out=rec[:, osl, :], in0=Ov()[:, :, HD], scalar1=1e-30, scalar2=None,
                                                      op0=ALU.add), reads=[("ps", ob)], writes=[(tag + "rec", osl)])
                P.op("dve", lambda e: e.reciprocal(out=rec[:, osl, :], in_=rec[:, osl, :]),
                     reads=[(tag + "rec", osl)], writes=[(tag + "rec", osl)])
                P.op("dve", lambda e: e.tensor_tensor(
                    out=aot[:, osl], in0=Ov()[:, :, 0:HD], in1=rec[:, osl, :].unsqueeze(2).broadcast_to([128, 4, HD]),
                    op=ALU.mult), reads=[("ps", ob), (tag + "rec", osl)], writes=[(tag + "aot", osl)])
                P.op("sp", lambda e: e.dma_start(out=AO[i * 128:(i + 1) * 128, c * 128:(c + 1) * 128],
                                                 in_=aot[:, osl].rearrange("p h d -> p (h d)")),
                     reads=[(tag + "aot", osl)], writes=[(tag + "AO", i, c)], dma=True)

        n = len(items)
        for k in range(n + LOOK):
            if k < n:
                stage_s(k)
            if k - LOOK >= 0:
                stage_o(k - LOOK)
        P.flush()


def na_out_phase(P, nc, ps, C, tag, X, st_lo, st_hi, AO, at_tag, Wo, ko):
    TS = 4
    tiles = split_tiles(st_lo, st_hi, TS)
    with contextlib.ExitStack() as st:
        def sb(name, shape, dt):
            return st.enter_context(nc.sbuf_tensor(tag + name, shape, dt))
        wo = sb("wo", [128, 4, KC, 512], BF16)
        at = sb("at", [128, 2, D], BF16)
        aT = sb("aT", [128, KC, TS * 128], BF16)
        xr = sb("xr", [128, 6, 512], F32)
        xo = None
        for dg in range(4):
            P.op("sp", lambda e, dg=dg: e.dma_start(out=wo[:, dg], in_=Wo[dg]), reads=[(ko, dg)], writes=[(tag + "wo", dg)], dma=True)
        cnt = {"b": 0, "r": 0, "a": 0}
        for (t0, nsub) in tiles:
            for ts in range(nsub):
                s_ = t0 + ts
                sl = cnt["a"] % 2
                cnt["a"] += 1
                P.op("sp", lambda e, sl=sl, s_=s_: e.dma_start(out=at[:, sl, :], in_=AO[s_ * 128:(s_ + 1) * 128, :]),
                     reads=[(at_tag + "AO", s_, c) for c in range(16)], writes=[(tag + "at", sl)], dma=True)
                transpose_to(P, ps, C, lambda c, sl=sl: at[:, sl, c * 128:(c + 1) * 128], (tag + "at", sl), KC,
                             lambda c0, n, ts=ts: aT[:, c0:c0 + n, ts * 128:(ts + 1) * 128], [(tag + "aT", ts)], banks=(6, 7))
            emit_out_proj(P, ps, tag, X, t0, nsub, aT, [(tag + "aT", ts) for ts in range(nsub)], KC,
                          lambda dg, k: wo[:, dg, k, :], lambda dg: [(tag + "wo", dg)], xr, xo, cnt, [0, 1, 2, 3, 4, 5])
        P.flush()


NEG = -30000.0
GRID_W = 64
ROWS = 256


def build_tb(rpb):
    kr = np.arange(2)[:, None, None, None, None]
    kc = np.arange(64)[None, :, None, None, None]
    o = np.arange(7)[None, None, :, None, None]
    qr = np.arange(2)[None, None, None, :, None]
    qc = np.arange(64)[None, None, None, None, :]
    dr = 2 * (o - 3) + kr - qr + 7
    dc = kc - qc + 15
    cs = np.clip(qc - 8, 0, GRID_W - 16)
    cvalid = (kc >= cs) & (kc < cs + 16)
    dr_b, dc_b, cv_b = np.broadcast_arrays(dr, dc, cvalid)
    dc_c = np.clip(dc_b, 0, 30)
    g = rpb[:, dr_b, dc_c]
    g = np.where(cv_b[None], g, np.float32(NEG)).astype(np.float32)
    return np.ascontiguousarray(g.reshape(64, 128, 896))


def build_rv(NB, row0):
    rv = np.full((NB, 2, 7, 2), NEG, np.float32)
    for i in range(NB):
        for o in range(7):
            for kr in range(2):
                for qr in range(2):
                    Rq = row0 + 2 * i + qr
                    Rk = row0 + 2 * (i + o - 3) + kr
                    if 0 <= Rq < ROWS and 0 <= Rk < ROWS:
                        rs = min(max(Rq - 4, 0), ROWS - 8)
                        if rs <= Rk < rs + 8:
                            rv[i, kr, o, qr] = 0.0
    return rv.reshape(NB, 2, 14).astype(ml_dtypes.bfloat16)


def build_consts():
    ident = np.eye(128, dtype=np.float32)
    p = np.arange(128)
    bd = np.where((p[:, None] // 32) == (p[None, :] // 32), np.float32(1.0 / 32), np.float32(0.0))
    ind4 = np.zeros((128, 128), np.float32)
    for hl in range(4):
        for r in range(2):
            ind4[32 * hl + r, :] = (p // 64 == r)
    return dict(ident=ident.astype(ml_dtypes.bfloat16), identf=ident, bd=bd.astype(ml_dtypes.bfloat16),
                ind4=ind4.astype(ml_dtypes.bfloat16))


DEPTH = 4
NB = 41
NTOK = NB * 128
OWN_LO, OWN_HI = 5, 37
SEQ = 16384
BATCH = 2


def build_program():
    nc = bass.Bass("TRN2", target_bir_lowering=False)

    def din(name, shape, dt=F32):
        return nc.dram_tensor(name, list(shape), dt, kind="ExternalInput").ap()

    def dscr(name, shape, dt):
        return nc.dram_tensor(name, list(shape), dt, kind="Internal").ap()

    x_loc = din("x_loc", [NTOK, D])
    I = {}
    for nm in ("ffn1_norm", "mix_norm", "ffn2_norm", "out_norm"):
        I[nm] = din(nm, [DEPTH, D])
    for nm in ("ffn1_w_gate", "ffn1_w_up", "ffn2_w_gate", "ffn2_w_up"):
        I[nm] = din(nm, [DEPTH, D, DFF])
    for nm in ("ffn1_w_down", "ffn2_w_down"):
        I[nm] = din(nm, [DEPTH, DFF, D])
    I["na_w_qkv"] = din("na_w_qkv", [2, D, 3 * D])
    I["na_q_gain"] = din("na_q_gain", [2, HD])
    I["na_k_gain"] = din("na_k_gain", [2, HD])
    I["na_w_o"] = din("na_w_o", [2, D, D])
    I["sgu_w_in"] = din("sgu_w_in", [1, D, 8192])
    I["sgu_v_gain"] = din("sgu_v_gain", [1, 4096])
    I["sgu_w_s"] = din("sgu_w_s", [1, 16, 128, 128])
    I["sgu_b_s"] = din("sgu_b_s", [1, 16, 128])
    I["sgu_w_out"] = din("sgu_w_out", [1, 4096, D])
    I["pool_w"] = din("pool_w", [1, 4, 512, 512])
    I["pool_scale"] = din("pool_scale", [1, D])
    TB = [din("tb0", [NH, 128, 896]), din("tb1", [NH, 128, 896])]
    RVd = din("rv", [NB, 2, 14], BF16)
    valid_d = din("valid", [NTOK, 1])
    invcnt_d = din("invcnt", [4, NTOK])
    cin = {k: din("c_" + k, [128, 128], F32 if k == "identf" else BF16) for k in ("ident", "identf", "bd", "ind4")}
    out = nc.dram_tensor("out", [(OWN_HI - OWN_LO) * 128, D], F32, kind="ExternalOutput").ap()

    X = dscr("X", [NTOK, D], F32)
    QT = dscr("QT", [32, 128, NTOK], BF16)
    VX = dscr("VX", [NTOK, NH * (HD + 1)], BF16)
    AO = dscr("AO", [NTOK, D], BF16)
    HT = dscr("HT", [KC, 128, NTOK + 16], F32)
    W = {}
    for l in range(DEPTH):
        for f in (1, 2):
            W["g%d%d" % (f, l)] = dscr("wg%d%d" % (f, l), [FC, 128, KC, 128], BF16)
            W["u%d%d" % (f, l)] = dscr("wu%d%d" % (f, l), [FC, 128, KC, 128], BF16)
            W["d%d%d" % (f, l)] = dscr("wd%d%d" % (f, l), [4, 128, FC, 512], BF16)
    for j in range(2):
        W["qk%d" % j] = dscr("wqk%d" % j, [32, 128, KC, 128], BF16)
        W["v%d" % j] = dscr("wv%d" % j, [4, 128, KC, 512], BF16)
        W["o%d" % j] = dscr("wo%d" % j, [4, 128, KC, 512], BF16)
    W["sin"] = dscr("wsin", [16, 128, KC, 512], BF16)
    W["sout"] = dscr("wsout", [4, 128, 32, 512], BF16)
    W["pw"] = dscr("wpw", [4, 128, 4, 512], BF16)

    P = Prog(nc)

    def cast_ffn(f, l):
        def go():
            cast_cols(P, I["ffn%d_w_gate" % f][l], W["g%d%d" % (f, l)], "g%d%d" % (f, l), D, DFF, 128)
            cast_cols(P, I["ffn%d_w_up" % f][l], W["u%d%d" % (f, l)], "u%d%d" % (f, l), D, DFF, 128)
            cast_cols(P, I["ffn%d_w_down" % f][l], W["d%d%d" % (f, l)], "d%d%d" % (f, l), DFF, D, 512)
        return go

    def cast_na(j):
        def go():
            cast_cols(P, I["na_w_qkv"][j][:, 0:2 * D], W["qk%d" % j], "qk%d" % j, D, 2 * D, 128)
            cast_cols(P, I["na_w_qkv"][j][:, 2 * D:3 * D], W["v%d" % j], "v%d" % j, D, D, 512)
            cast_cols(P, I["na_w_o"][j], W["o%d" % j], "o%d" % j, D, D, 512)
        return go

    def cast_sgu():
        cast_cols(P, I["sgu_w_in"][0], W["sin"], "sin", D, 8192, 512)
        cast_cols(P, I["sgu_w_out"][0], W["sout"], "sout", 4096, D, 512)

    def cast_pool():
        for g in range(4):
            P.op("pool", lambda e, g=g: e.dma_start(out=W["pw"][g], in_=I["pool_w"][0][g].rearrange("(cc p) d -> p cc d", p=128)),
                 writes=[("pw", g)], dma=True)

    with contextlib.ExitStack() as st:
        ps = st.enter_context(nc.psum_tensor("ps", [128, 8, 512], F32))
        C = {}
        for k in ("ident", "identf", "bd", "ind4"):
            C[k] = st.enter_context(nc.sbuf_tensor(k + "_sb", [128, 128], F32 if k == "identf" else BF16))
            P.op("sp", lambda e, k=k: e.dma_start(out=C[k][:], in_=cin[k]), writes=[k], dma=True)
            P.const.add(k)
        C["eps"] = st.enter_context(nc.sbuf_tensor("eps_sb", [128, 1], F32))
        P.op("dve", lambda e: e.memset(C["eps"][:], EPS), writes=["eps"])
        P.const.add("eps")
        cast_ffn(1, 0)()
        P.flush()

        def vec(nm, l):
            return I[nm][l:l + 1, :]

        def ffn(f, l, lo, hi, Xin, g_pre, nxt_cast):
            if nxt_cast is not None:
                nxt_cast()
            ffn_phase(P, nc, ps, C, "f%d%d" % (f, l), Xin, X, lo, hi, W["g%d%d" % (f, l)], W["u%d%d" % (f, l)],
                      W["d%d%d" % (f, l)], ("g%d%d" % (f, l), "u%d%d" % (f, l), "d%d%d" % (f, l)), g_pre,
                      vec("ffn%d_norm" % f, l))

        def na(j, l, kv_lo, kv_hi, q_lo, q_hi, nxt_cast):
            nxt_cast()
            qt = "q%d" % j
            na_qkv_phase(P, nc, ps, C, qt, X, kv_lo, kv_hi, vec("mix_norm", l), W["qk%d" % j], "qk%d" % j, W["v%d" % j],
                         "v%d" % j, I["na_q_gain"][j:j + 1, :].rearrange("o d -> d o"),
                         I["na_k_gain"][j:j + 1, :].rearrange("o d -> d o"), QT, VX)
            na_attn_phase(P, nc, ps, C, "a%d" % j, NB, q_lo, q_hi, QT, VX, AO, TB[j], RVd, qt, split_tiles(kv_lo, kv_hi, 4))
            na_out_phase(P, nc, ps, C, "o%d" % j, X, q_lo, q_hi, AO, "a%d" % j, W["o%d" % j], "o%d" % j)

        ffn(1, 0, 0, NB, x_loc, None, cast_na(0))
        na(0, 0, 0, NB, 2, 39, cast_ffn(2, 0))
        ffn(2, 0, 2, 39, X, None, cast_ffn(1, 1))
        ffn(1, 1, 2, 39, X, vec("out_norm", 0), cast_sgu)
        cast_ffn(2, 1)()
        sgu_phase(P, nc, ps, C, "sg", X, 2, 39, vec("mix_norm", 1), W["sin"], "sin", W["sout"], "sout",
                  I["sgu_v_gain"], I["sgu_w_s"][0], I["sgu_b_s"][0])
        ffn(2, 1, 2, 39, X, None, cast_ffn(1, 2))
        ffn(1, 2, 2, 39, X, vec("out_norm", 1), cast_pool)
        cast_ffn(2, 2)()
        pool_phase(P, nc, ps, C, "pl", X, HT, 2, 39, vec("mix_norm", 2), W["pw"], "pw", I["pool_scale"], valid_d, invcnt_d, NTOK)
        ffn(2, 2, 3, 39, X, None, cast_ffn(1, 3))
        ffn(1, 3, 3, 39, X, vec("out_norm", 2), cast_na(1))
        na(1, 3, 3, 39, OWN_LO, OWN_HI, cast_ffn(2, 3))
        ffn(2, 3, OWN_LO, OWN_HI, X, None, None)
        final_phase(P, nc, C, "fin", X, out, OWN_LO, OWN_HI, vec("out_norm", 3))
    P.close()
    return nc


def host_inputs(inputs, core):
    b, q = divmod(core, 4)
    row0 = 64 * q - 10
    x = np.asarray(inputs["x"], dtype=np.float32)
    x_loc = np.zeros((NTOK, D), np.float32)
    t_lo, t_hi = row0 * 64, row0 * 64 + NTOK
    a, c = max(t_lo, 0), min(t_hi, SEQ)
    x_loc[a - t_lo:c - t_lo] = x[b, a:c]
    t = np.arange(NTOK) + t_lo
    valid = ((t >= 0) & (t < SEQ)).astype(np.float32).reshape(NTOK, 1)
    invcnt = np.ones((4, NTOK), np.float32)
    for g, w in enumerate((2, 4, 8, 16)):
        lo = np.clip(t - w // 2, 0, SEQ)
        hi = np.clip(t + w // 2, 0, SEQ)
        cnt = np.maximum(hi - lo, 1).astype(np.float32)
        invcnt[g] = np.float32(1.0) / cnt
    m = {"x_loc": x_loc, "valid": valid, "invcnt": invcnt, "rv": build_rv(NB, row0)}
    return m


_NC_CACHE = {}


def _run(inputs, cores):
    if "nc" not in _NC_CACHE:
        _NC_CACHE["nc"] = build_program()
    nc = _NC_CACHE["nc"]
    shared = {}
    for k, v in inputs.items():
        if k != "x":
            shared[k] = np.ascontiguousarray(np.asarray(v, dtype=np.float32))
    rpb = shared.pop("na_rpb")
    shared["tb0"] = build_tb(rpb[0])
    shared["tb1"] = build_tb(rpb[1])
    for k, v in build_consts().items():
        shared["c_" + k] = v
    in_maps = []
    for c in cores:
        m = dict(shared)
        m.update(host_inputs(inputs, c))
        in_maps.append(m)
    res = run_bass_kernel_spmd(nc, in_maps, core_ids=list(range(len(cores))))
    return [np.asarray(r["out"]) for r in res.results]


def kernel(**inputs):
    outs = _run(inputs, list(range(8)))
    full = np.zeros((BATCH, SEQ, D), np.float32)
    for c, o in enumerate(outs):
        b, q = divmod(c, 4)
        full[b, q * 4096:(q + 1) * 4096] = o
    return full
```

```python
import contextlib
import numpy as np
import ml_dtypes
import concourse.bass as bass
import concourse.mybir as mybir
from concourse.bass_utils import run_bass_kernel_spmd

F32 = mybir.dt.float32
BF16 = mybir.dt.bfloat16
AF = mybir.ActivationFunctionType
ALU = mybir.AluOpType
AX = mybir.AxisListType

D = 2048
DFF = 5632
KC = D // 128
FC = DFF // 128
EPS = 1e-6
NSUB_MAX = 7


class _Sem:
    __slots__ = ("h", "cnt", "name")

    def __init__(self, name):
        self.h = None
        self.cnt = 0
        self.name = name


class _Eng:
    def __init__(self, name, ndma=0):
        self.name = name
        self.ops = []
        self.sem = _Sem("c_" + name)
        self.waited = {}
        self.dma_sems = [_Sem("d_%s%d" % (name, i)) for i in range(ndma)]
        self.rr = 0


class Prog:
    def __init__(self, nc, ndma=10):
        self.nc = nc
        self.E = {
            "sp": _Eng("sp", 32),
            "act": _Eng("act", 2),
            "pool": _Eng("pool", 12),
            "dve": _Eng("dve"),
            "pe": _Eng("pe"),
        }
        self.lastw = {}
        self.readers = {}
        self.const = set()
        self.stack = contextlib.ExitStack()
        for s in self.all_sems():
            s.h = self.stack.enter_context(nc.semaphore(s.name))

    def all_sems(self):
        out = []
        for e in self.E.values():
            out.append(e.sem)
            out.extend(e.dma_sems)
        return out

    def op(self, eng, fn, reads=(), writes=(), dma=False):
        e = self.E[eng]
        deps = []
        for k in reads:
            t = self.lastw.get(k)
            if t is not None:
                deps.append(t)
        for k in writes:
            t = self.lastw.get(k)
            if t is not None:
                deps.append(t)
            r = self.readers.get(k)
            if r:
                deps.extend(r.values())
        if dma:
            ds = e.dma_sems[e.rr]
            e.rr = (e.rr + 1) % len(e.dma_sems)
            if ds.cnt > 0:
                deps.append((ds, ds.cnt))
            ds.cnt += 16
            tok = (ds, ds.cnt)
        else:
            e.sem.cnt += 1
            tok = (e.sem, e.sem.cnt)
        newmax = {}
        for (so, v) in deps:
            if e.waited.get(so, 0) < v and newmax.get(so, 0) < v:
                newmax[so] = v
        waits = list(newmax.items())
        for so, v in waits:
            e.waited[so] = v
        e.ops.append((fn, waits, tok, 16 if dma else 1))
        for k in reads:
            if k in self.const:
                continue
            self.readers.setdefault(k, {})[tok[0]] = tok
        for k in writes:
            self.lastw[k] = tok
            self.readers[k] = {}
        return tok

    def wait_all(self, eng):
        e = self.E[eng]
        waits = []
        for s in self.all_sems():
            if s.cnt > 0 and e.waited.get(s, 0) < s.cnt:
                waits.append((s, s.cnt))
                e.waited[s] = s.cnt
        e.ops.append((None, waits, None, 0))

    def flush(self):
        self.wait_all("sp")
        with self.nc.Block() as block:
            def run(e):
                ops = e.ops

                def body(eng):
                    for fn, waits, tok, amt in ops:
                        for so, v in waits:
                            eng.wait_ge(so.h, v)
                        if fn is None:
                            continue
                        ins = fn(eng)
                        if tok is not None:
                            ins.then_inc(tok[0].h, amt)
                return body

            reg = {"sp": block.sync, "act": block.scalar, "pool": block.gpsimd,
                   "dve": block.vector, "pe": block.tensor}
            for n, e in self.E.items():
                if e.ops:
                    reg[n](run(e))
        for e in self.E.values():
            e.ops = []

    def close(self):
        self.stack.close()


def split_tiles(lo, hi, mx):
    n = hi - lo
    nt = -(-n // mx)
    base, rem = divmod(n, nt)
    out = []
    s = lo
    for i in range(nt):
        k = base + (1 if i < rem else 0)
        out.append((s, k))
        s += k
    return out


def cast_cols(P, src, dst, key, K, ncols, cw):
    s = src.rearrange("(kc p) f -> p kc f", p=128)
    ng = ncols // cw
    per = max(1, 4096 // (K // 128 * 128) * 1)
    for g in range(ng):
        P.op("pool", lambda e, g=g: e.dma_start(out=dst[g], in_=s[:, :, g * cw:(g + 1) * cw]),
             writes=[(key, g)], dma=True)


def ffn_phase(P, nc, ps, C, tag, Xin, Xout, st_lo, st_hi, Wg, Wu, Wd, wkeys, g_pre, g_ffn):
    tiles = split_tiles(st_lo, st_hi, NSUB_MAX)
    NTM = NSUB_MAX * 128
    kg, ku, kd = wkeys
    xin_name = Xin.tensor.name
    xout_name = Xout.tensor.name
    with contextlib.ExitStack() as st:
        def sb(name, shape, dt):
            return st.enter_context(nc.sbuf_tensor(tag + name, shape, dt))
        xnT = sb("xnT", [128, KC, NTM], BF16)
        hT = sb("hT", [128, FC, NTM], BF16)
        wgu = sb("wgu", [128, 3, 2, KC, 128], BF16)
        wd = sb("wd", [128, 3, 4, 512], BF16)
        xt = sb("xt", [128, 2, D], F32)
        hn = sb("hn", [128, 2, D], BF16)
        gbf = sb("gbf", [128, D], F32)
        gbp = sb("gbp", [128, D], F32) if g_pre is not None else None
        sil = sb("sil", [128, 2, 512], BF16)
        xr = sb("xr", [128, NSUB_MAX, 512], F32)
        stat = sb("stat", [128, 2, 8], F32)

        P.op("sp", lambda e: e.dma_start(out=gbf[:], in_=g_ffn.partition_broadcast(128)),
             writes=[tag + "gbf"], dma=True)
        if g_pre is not None:
            P.op("sp", lambda e: e.dma_start(out=gbp[:], in_=g_pre.partition_broadcast(128)),
                 writes=[tag + "gbp"], dma=True)
        P.const.add(tag + "gbf")
        P.const.add(tag + "gbp")

        cnt = {"x": 0, "wgu": 0, "wd": 0, "r": 0, "sil": 0}

        def prologue_steps(t0, nsub):
            slots = {}
            fronts, pes = [], []

            def mk_front(ts):
                def front():
                    s_ = t0 + ts
                    sl = cnt["x"] % 2
                    cnt["x"] += 1
                    slots[ts] = sl
                    rows = slice(s_ * 128, (s_ + 1) * 128)
                    kx, khn, kst = (tag + "xt", sl), (tag + "hn", sl), (tag + "stat", sl)
                    P.op("sp", lambda e: e.dma_start(out=xt[:, sl, :], in_=Xin[rows, :]),
                         reads=[(xin_name, s_, j) for j in range(4)], writes=[kx], dma=True)
                    if g_pre is not None:
                        P.op("act", lambda e: e.activation(out=hn[:, sl, :], in_=xt[:, sl, :], func=AF.Square,
                                                           accum_out=stat[:, sl, 0:1]),
                             reads=[kx], writes=[khn, kst])
                        P.op("act", lambda e: e.activation(out=stat[:, sl, 1:2], in_=stat[:, sl, 0:1], func=AF.Sqrt,
                                                           bias=C["eps"][:], scale=1.0 / D),
                             reads=[kst], writes=[kst])
                        P.op("dve", lambda e: e.reciprocal(out=stat[:, sl, 2:3], in_=stat[:, sl, 1:2]),
                             reads=[kst], writes=[kst])
                        P.op("dve", lambda e: e.scalar_tensor_tensor(
                            out=xt[:, sl, :], in0=xt[:, sl, :], scalar=stat[:, sl, 2:3], in1=gbp[:],
                            op0=ALU.mult, op1=ALU.mult), reads=[kx, kst, tag + "gbp"], writes=[kx])
                        P.op("sp", lambda e: e.dma_start(out=Xout[rows, :], in_=xt[:, sl, :]),
                             reads=[kx], writes=[(xout_name, s_, j) for j in range(4)], dma=True)
                    P.op("act", lambda e: e.activation(out=hn[:, sl, :], in_=xt[:, sl, :], func=AF.Square,
                                                       accum_out=stat[:, sl, 3:4]),
                         reads=[kx], writes=[khn, kst])
                    P.op("act", lambda e: e.activation(out=stat[:, sl, 4:5], in_=stat[:, sl, 3:4], func=AF.Sqrt,
                                                       bias=C["eps"][:], scale=1.0 / D),
                         reads=[kst], writes=[kst])
                    P.op("dve", lambda e: e.reciprocal(out=stat[:, sl, 5:6], in_=stat[:, sl, 4:5]),
                         reads=[kst], writes=[kst])
                    P.op("dve", lambda e: e.scalar_tensor_tensor(
                        out=hn[:, sl, :], in0=xt[:, sl, :], scalar=stat[:, sl, 5:6], in1=gbf[:],
                        op0=ALU.mult, op1=ALU.mult), reads=[kx, kst, tag + "gbf"], writes=[khn])
                return front

            def mk_pe(ts):
                def pe():
                    sl = slots[ts]
                    khn = (tag + "hn", sl)
                    for half in range(2):
                        def tr(e, half=half):
                            pv = ps[:, 7, :].bitcast(BF16)
                            ins = None
                            for j in range(8):
                                kc = half * 8 + j
                                ins = e.transpose(out=pv[:, j * 128:(j + 1) * 128], in_=hn[:, sl, kc * 128:(kc + 1) * 128],
                                                  identity=C["ident"][:])
                            return ins
                        P.op("pe", tr, reads=[khn, "ident"], writes=[("ps", 7)])
                        if half == 0:
                            P.op("dve", lambda e, half=half: e.tensor_copy(
                                out=xnT[:, half * 8:(half + 1) * 8, ts * 128:(ts + 1) * 128],
                                in_=ps[:, 7, :].bitcast(BF16).rearrange("p (j t) -> p j t", j=8)),
                                reads=[("ps", 7)], writes=[(tag + "xnT", ts)])
                        else:
                            P.op("act", lambda e, half=half: e.copy(
                                out=xnT[:, half * 8:(half + 1) * 8, ts * 128:(ts + 1) * 128],
                                in_=ps[:, 7, :].bitcast(BF16).rearrange("p (j t) -> p j t", j=8)),
                                reads=[("ps", 7)], writes=[(tag + "xnT", ts)])
                return pe

            steps = []
            for k in range(nsub):
                steps.append(mk_front(k))
                if k >= 1:
                    steps.append(mk_pe(k - 1))
            steps.append(mk_pe(nsub - 1))
            return steps

        def load_wgu(fc):
            sl = cnt["wgu"] % 3
            cnt["wgu"] += 1
            P.op("sp", lambda e: e.dma_start(out=wgu[:, sl, 0], in_=Wg[fc]), reads=[(kg, fc)],
                 writes=[(tag + "wg", sl)], dma=True)
            P.op("sp", lambda e: e.dma_start(out=wgu[:, sl, 1], in_=Wu[fc]), reads=[(ku, fc)],
                 writes=[(tag + "wu", sl)], dma=True)
            return sl

        def load_wd(dg, fg):
            sl = cnt["wd"] % 3
            cnt["wd"] += 1
            P.op("sp", lambda e: e.dma_start(out=wd[:, sl], in_=Wd[dg, :, fg * 4:(fg + 1) * 4, :]),
                 reads=[(kd, dg)], writes=[(tag + "wd", sl)], dma=True)
            return sl

        def gateup(nsub):
            NT = nsub * 128
            halves = [(0, min(512, NT))] + ([(512, NT)] if NT > 512 else [])
            pend = [load_wgu(fc) for fc in range(min(3, FC))]
            for fc in range(FC):
                sl = pend.pop(0)
                par = fc % 2
                for gu in range(2):
                    for hi, (a, b) in enumerate(halves):
                        bank = 4 * par + 2 * gu + hi
                        def mm(e, sl=sl, gu=gu, a=a, b=b, bank=bank):
                            ins = None
                            for kc in range(KC):
                                ins = e.matmul(ps[:, bank, 0:b - a], lhsT=wgu[:, sl, gu, kc, :], rhs=xnT[:, kc, a:b],
                                               start=(kc == 0), stop=(kc == KC - 1))
                            return ins
                        P.op("pe", mm, reads=[(tag + ("wg" if gu == 0 else "wu"), sl)] +
                             [(tag + "xnT", t) for t in range(a // 128, b // 128)], writes=[("ps", bank)])
                for hi, (a, b) in enumerate(halves):
                    ss = cnt["sil"] % 2
                    cnt["sil"] += 1
                    bg, bu = 4 * par + hi, 4 * par + 2 + hi
                    P.op("act", lambda e, ss=ss, a=a, b=b, bg=bg: e.activation(
                        out=sil[:, ss, 0:b - a], in_=ps[:, bg, 0:b - a], func=AF.Silu),
                        reads=[("ps", bg)], writes=[(tag + "sil", ss)])
                    P.op("dve", lambda e, ss=ss, a=a, b=b, bu=bu, fc=fc: e.tensor_tensor(
                        out=hT[:, fc, a:b], in0=ps[:, bu, 0:b - a], in1=sil[:, ss, 0:b - a], op=ALU.mult),
                        reads=[("ps", bu), (tag + "sil", ss)], writes=[(tag + "hT", fc)])
                if fc + 3 < FC:
                    pend.append(load_wgu(fc + 3))

        def down(t0, nsub, steps):
            Xres = Xout if g_pre is not None else Xin
            xres_name = Xres.tensor.name
            NFG = FC // 4
            seq = [(dg, fg) for dg in range(4) for fg in range(NFG)]

            def load_xr(dg):
                cols = slice(dg * 512, (dg + 1) * 512)
                for ts in range(nsub):
                    s_ = t0 + ts
                    rows = slice(s_ * 128, (s_ + 1) * 128)
                    P.op("sp", lambda e, ts=ts, rows=rows, cols=cols: e.dma_start(out=xr[:, ts, :], in_=Xres[rows, cols]),
                         reads=[(xres_name, s_, dg)], writes=[(tag + "xr", ts)], dma=True)

            pend = [load_wd(*seq[i]) for i in range(3)]
            load_xr(0)
            for i, (dg, fg) in enumerate(seq):
                sl = pend.pop(0)
                hkeys = [(tag + "hT", fg * 4 + fl) for fl in range(4)]
                if fg == 0:
                    for ts in range(nsub):
                        def mm1(e, sl=sl, fg=fg, ts=ts):
                            ins = None
                            for fl in range(4):
                                fc = fg * 4 + fl
                                ins = e.matmul(ps[:, ts, :], lhsT=hT[:, fc, ts * 128:(ts + 1) * 128], rhs=wd[:, sl, fl, :],
                                               start=(fc == 0), stop=(fc == FC - 1))
                            return ins
                        P.op("pe", mm1, reads=[(tag + "wd", sl)] + hkeys, writes=[("ps", ts)])
                else:
                    def mm(e, sl=sl, fg=fg):
                        ins = None
                        for fl in range(4):
                            fc = fg * 4 + fl
                            for ts in range(nsub):
                                ins = e.matmul(ps[:, ts, :], lhsT=hT[:, fc, ts * 128:(ts + 1) * 128], rhs=wd[:, sl, fl, :],
                                               start=(fc == 0), stop=(fc == FC - 1))
                        return ins
                    P.op("pe", mm, reads=[(tag + "wd", sl)] + hkeys, writes=[("ps", ts) for ts in range(nsub)])
                if i + 3 < len(seq):
                    pend.append(load_wd(*seq[i + 3]))
                if fg == NFG - 1:
                    cols = slice(dg * 512, (dg + 1) * 512)
                    for ts in range(nsub):
                        s_ = t0 + ts
                        rows = slice(s_ * 128, (s_ + 1) * 128)
                        P.op("dve", lambda e, ts=ts: e.scalar_tensor_tensor(
                            out=xr[:, ts, :], in0=ps[:, ts, :], scalar=0.5, in1=xr[:, ts, :], op0=ALU.mult, op1=ALU.add),
                            reads=[("ps", ts), (tag + "xr", ts)], writes=[(tag + "xr", ts)])
                        P.op("sp", lambda e, ts=ts, rows=rows, cols=cols: e.dma_start(out=Xout[rows, cols], in_=xr[:, ts, :]),
                             reads=[(tag + "xr", ts)], writes=[(xout_name, s_, dg)], dma=True)
                    if dg + 1 < 4:
                        load_xr(dg + 1)
                if dg >= 2 and steps:
                    steps.pop(0)()
            while steps:
                steps.pop(0)()

        for ti, (t0, nsub) in enumerate(tiles):
            if ti == 0:
                for s in prologue_steps(t0, nsub):
                    s()
            gateup(nsub)
            nxt = tiles[ti + 1] if ti + 1 < len(tiles) else None
            down(t0, nsub, prologue_steps(*nxt) if nxt is not None else [])
        P.flush()


class NormCtx:
    def __init__(self, P, nc, st, C, tag, gvec, out_dt, nslot=2):
        self.P, self.C, self.tag = P, C, tag
        self.xt = st.enter_context(nc.sbuf_tensor(tag + "nxt", [128, nslot, D], F32))
        self.hn = st.enter_context(nc.sbuf_tensor(tag + "nhn", [128, nslot, D], out_dt))
        self.junk = st.enter_context(nc.sbuf_tensor(tag + "njunk", [128, D], BF16))
        self.gb = st.enter_context(nc.sbuf_tensor(tag + "ngb", [128, D], F32))
        self.stat = st.enter_context(nc.sbuf_tensor(tag + "nstat", [128, nslot, 4], F32))
        self.n = 0
        self.nslot = nslot
        gb = self.gb
        P.op("sp", lambda e: e.dma_start(out=gb[:], in_=gvec.partition_broadcast(128)), writes=[tag + "ngb"], dma=True)
        P.const.add(tag + "ngb")

    def run(self, X, s_, valid=None):
        P, C, tag = self.P, self.C, self.tag
        sl = self.n % self.nslot
        self.n += 1
        xt, hn, junk, gb, stat = self.xt, self.hn, self.junk, self.gb, self.stat
        xname = X.tensor.name
        rows = slice(s_ * 128, (s_ + 1) * 128)
        kx, khn, kst = (tag + "nxt", sl), (tag + "nhn", sl), (tag + "nstat", sl)
        P.op("sp", lambda e: e.dma_start(out=xt[:, sl, :], in_=X[rows, :]),
             reads=[(xname, s_, j) for j in range(4)], writes=[kx], dma=True)
        P.op("act", lambda e: e.activation(out=junk[:], in_=xt[:, sl, :], func=AF.Square, accum_out=stat[:, sl, 0:1]),
             reads=[kx], writes=[tag + "njunk", kst])
        P.op("act", lambda e: e.activation(out=stat[:, sl, 1:2], in_=stat[:, sl, 0:1], func=AF.Sqrt,
                                           bias=C["eps"][:], scale=1.0 / D), reads=[kst], writes=[kst])
        P.op("dve", lambda e: e.reciprocal(out=stat[:, sl, 2:3], in_=stat[:, sl, 1:2]), reads=[kst], writes=[kst])
        if valid is not None:
            vap, vkey = valid
            P.op("dve", lambda e: e.tensor_tensor(out=stat[:, sl, 2:3], in0=stat[:, sl, 2:3], in1=vap, op=ALU.mult),
                 reads=[kst, vkey], writes=[kst])
        P.op("dve", lambda e: e.scalar_tensor_tensor(out=hn[:, sl, :], in0=xt[:, sl, :], scalar=stat[:, sl, 2:3],
                                                     in1=gb[:], op0=ALU.mult, op1=ALU.mult),
             reads=[kx, kst, tag + "ngb"], writes=[khn])
        return sl, khn


def transpose_to(P, ps, C, src_ap_fn, src_key, nchunk, dst_fn, dst_keys, bank=7, fp32=False):
    per = 4 if fp32 else 8
    ident = C["identf"] if fp32 else C["ident"]
    ikey = "identf" if fp32 else "ident"
    r = 0
    for c0 in range(0, nchunk, per):
        n = min(per, nchunk - c0)

        def tr(e, c0=c0, n=n):
            pv = ps[:, bank, :] if fp32 else ps[:, bank, :].bitcast(BF16)
            ins = None
            for j in range(n):
                ins = e.transpose(out=pv[:, j * 128:(j + 1) * 128], in_=src_ap_fn(c0 + j), identity=ident[:])
            return ins
        P.op("pe", tr, reads=[src_key, ikey], writes=[("ps", bank)])

        def ev(e, c0=c0, n=n, r=r):
            pv = ps[:, bank, :] if fp32 else ps[:, bank, :].bitcast(BF16)
            src = pv[:, 0:n * 128].rearrange("p (j t) -> p j t", j=n)
            if r % 2 == 0:
                return e.tensor_copy(out=dst_fn(c0, n), in_=src)
            return e.copy(out=dst_fn(c0, n), in_=src)
        P.op("dve" if r % 2 == 0 else "act", ev, reads=[("ps", bank)], writes=dst_keys)
        r += 1


def final_phase(P, nc, C, tag, X, out, st_lo, st_hi, gvec):
    with contextlib.ExitStack() as st:
        N = NormCtx(P, nc, st, C, tag, gvec, F32, nslot=3)
        for s_ in range(st_lo, st_hi):
            sl, khn = N.run(X, s_)
            o = s_ - st_lo
            P.op("sp", lambda e, sl=sl, o=o: e.dma_start(out=out[o * 128:(o + 1) * 128, :], in_=N.hn[:, sl, :]),
                 reads=[khn], writes=[("out", o)], dma=True)
        P.flush()


def pool_phase(P, nc, ps, C, tag, X, HT, st_lo, st_hi, gvec, Wp, wkey, scale_vec, valid_d, invcnt_d, NTOK):
    with contextlib.ExitStack() as st:
        N = NormCtx(P, nc, st, C, tag + "A", gvec, F32, nslot=2)
        hts = st.enter_context(nc.sbuf_tensor(tag + "hts", [128, 2, KC, 128], F32))
        vt = st.enter_context(nc.sbuf_tensor(tag + "vt", [128, 2, 1], F32))
        zt = st.enter_context(nc.sbuf_tensor(tag + "zt", [128, KC, 8], F32))
        P.op("dve", lambda e: e.memset(zt[:], 0.0), writes=[tag + "zt"])
        P.op("sp", lambda e: e.dma_start(out=HT[:, :, st_lo * 128:st_lo * 128 + 8].rearrange("c p t -> p c t"), in_=zt[:]),
             reads=[tag + "zt"], writes=[(tag + "HTpad", 0)], dma=True)
        P.op("sp", lambda e: e.dma_start(out=HT[:, :, st_hi * 128 + 8:st_hi * 128 + 16].rearrange("c p t -> p c t"), in_=zt[:]),
             reads=[tag + "zt"], writes=[(tag + "HTpad", 1)], dma=True)
        for i, s_ in enumerate(range(st_lo, st_hi)):
            hs = i % 2
            P.op("sp", lambda e, hs=hs, s_=s_: e.dma_start(out=vt[:, hs, :], in_=valid_d[s_ * 128:(s_ + 1) * 128, :]),
                 writes=[(tag + "vt", hs)], dma=True)
            sl, khn = N.run(X, s_, valid=(vt[:, hs, :], (tag + "vt", hs)))
            transpose_to(P, ps, C, lambda c, sl=sl: N.hn[:, sl, c * 128:(c + 1) * 128], khn, KC,
                         lambda c0, n, hs=hs: hts[:, hs, c0:c0 + n, :], [(tag + "hts", hs)], bank=7, fp32=True)
            P.op("sp", lambda e, hs=hs, s_=s_: e.dma_start(
                out=HT[:, :, 8 + s_ * 128:8 + (s_ + 1) * 128].rearrange("c p t -> p c t"), in_=hts[:, hs]),
                reads=[(tag + "hts", hs)], writes=[(tag + "HT", s_)], dma=True)
        P.flush()
    TS = 4
    tiles = split_tiles(st_lo, st_hi, TS)
    xname = X.tensor.name
    with contextlib.ExitStack() as st:
        def sb(name, shape, dt):
            return st.enter_context(nc.sbuf_tensor(tag + name, shape, dt))
        W = TS * 128
        hg = sb("hg", [128, 2, 4, W + 16], F32)
        ca = sb("ca", [128, 4, W + 16], F32)
        cb = sb("cb", [128, 4, W + 16], F32)
        ic = sb("ic", [128, 2, W], F32)
        pT = sb("pT", [128, 2, 4, W], BF16)
        wp = sb("wp", [128, 4, 4, 512], BF16)
        sc = sb("sc", [128, D], F32)
        xr = sb("xr", [128, 3, 512], F32)
        xo = sb("xo", [128, 3, 512], F32)
        tm = sb("tm", [128, 2, 512], F32)
        P.op("sp", lambda e: e.dma_start(out=sc[:], in_=scale_vec.partition_broadcast(128)), writes=[tag + "sc"], dma=True)
        for g in range(4):
            P.op("sp", lambda e, g=g: e.dma_start(out=wp[:, g], in_=Wp[g]), reads=[(wkey, g)], writes=[tag + "wp"], dma=True)
        P.const.add(tag + "sc")
        k = 0
        r = 0
        for (t0, nsub) in tiles:
            n = nsub * 128
            tok0 = t0 * 128
            for g in range(4):
                w = 2 << g
                sl = k % 2
                k += 1
                khg, kic, kpT = (tag + "hg", sl), (tag + "ic", sl), (tag + "pT", sl)
                P.op("sp", lambda e, sl=sl, g=g, tok0=tok0, n=n: e.dma_start(
                    out=hg[:, sl, :, 0:n + 16], in_=HT[4 * g:4 * g + 4, :, tok0:tok0 + n + 16].rearrange("c p t -> p c t")),
                    reads=[(tag + "HT", s_) for s_ in range(max(st_lo, t0 - 1), min(st_hi, t0 + nsub + 1))] +
                    [(tag + "HTpad", 0), (tag + "HTpad", 1)], writes=[khg], dma=True)
                P.op("sp", lambda e, sl=sl, g=g, tok0=tok0, n=n: e.dma_start(
                    out=ic[:, sl, 0:n], in_=invcnt_d[g:g + 1, tok0:tok0 + n].partition_broadcast(128)),
                    writes=[kic], dma=True)
                L = n + 16
                cur, curk = (lambda a, b, sl=sl: hg[:, sl, :, a:b]), khg
                bufs = [(ca, tag + "ca"), (cb, tag + "cb")]
                step = 1
                for lv in range(g + 1):
                    dst, dk = bufs[lv % 2]
                    Ln = L - (2 * step - 1)
                    P.op("dve",
                         lambda e, cur=cur, dst=dst, step=step, Ln=Ln: e.tensor_tensor(
                             out=dst[:, :, 0:Ln], in0=cur(0, Ln), in1=cur(step, step + Ln), op=ALU.add),
                         reads=[curk], writes=[dk])
                    cur, curk = (lambda a, b, dst=dst: dst[:, :, a:b]), dk
                    step *= 2
                off = 8 - w // 2
                dst, dk = bufs[(g + 1) % 2]
                P.op("dve", lambda e, cur=cur, dst=dst, off=off, n=n, sl=sl: e.tensor_tensor(
                    out=dst[:, :, 0:n], in0=cur(off, off + n),
                    in1=ic[:, sl, 0:n].unsqueeze(1).broadcast_to([128, 4, n]), op=ALU.mult),
                    reads=[curk, kic], writes=[dk])
                P.op("dve", lambda e, dst=dst, n=n, sl=sl: e.tensor_tensor(
                    out=pT[:, sl, :, 0:n], in0=dst[:, :, 0:n], in1=hg[:, sl, :, 8:8 + n], op=ALU.subtract),
                    reads=[dk, khg], writes=[kpT])
                for ts in range(nsub):
                    s_ = t0 + ts
                    bank = (r % 6)
                    rs = r % 3
                    tsl = r % 2
                    r += 1

                    def mm(e, sl=sl, ts=ts, g=g, bank=bank):
                        ins = None
                        for cc in range(4):
                            ins = e.matmul(ps[:, bank, :], lhsT=pT[:, sl, cc, ts * 128:(ts + 1) * 128], rhs=wp[:, g, cc, :],
                                           start=(cc == 0), stop=(cc == 3))
                        return ins
                    P.op("pe", mm, reads=[kpT, tag + "wp"], writes=[("ps", bank)])
                    rows = slice(s_ * 128, (s_ + 1) * 128)
                    cols = slice(g * 512, (g + 1) * 512)
                    P.op("sp", lambda e, rs=rs, rows=rows, cols=cols: e.dma_start(out=xr[:, rs, :], in_=X[rows, cols]),
                         reads=[(xname, s_, g)], writes=[(tag + "xr", rs)], dma=True)
                    P.op("dve", lambda e, tsl=tsl, bank=bank, cols=cols: e.tensor_tensor(
                        out=tm[:, tsl, :], in0=ps[:, bank, :], in1=sc[:, cols], op=ALU.mult),
                        reads=[("ps", bank), tag + "sc"], writes=[(tag + "tm", tsl)])
                    P.op("dve", lambda e, tsl=tsl, rs=rs: e.tensor_tensor(
                        out=xo[:, rs, :], in0=tm[:, tsl, :], in1=xr[:, rs, :], op=ALU.add),
                        reads=[(tag + "tm", tsl), (tag + "xr", rs)], writes=[(tag + "xo", rs)])
                    P.op("sp", lambda e, rs=rs, rows=rows, cols=cols: e.dma_start(out=X[rows, cols], in_=xo[:, rs, :]),
                         reads=[(tag + "xo", rs)], writes=[(xname, s_, g)], dma=True)
        P.flush()


def emit_out_proj(P, ps, tag, X, t0, nsub, aT, aT_keys, nk, w_ap_fn, w_keys_fn, xr, xo, cnt, banks):
    xname = X.tensor.name
    for dg in range(4):
        cols = slice(dg * 512, (dg + 1) * 512)
        for ts in range(nsub):
            s_ = t0 + ts
            bank = banks[cnt["b"] % len(banks)]
            cnt["b"] += 1
            rs = cnt["r"] % 3
            cnt["r"] += 1

            def mm(e, ts=ts, dg=dg, bank=bank):
                ins = None
                for k in range(nk):
                    ins = e.matmul(ps[:, bank, :], lhsT=aT[:, k, ts * 128:(ts + 1) * 128], rhs=w_ap_fn(dg, k),
                                   start=(k == 0), stop=(k == nk - 1))
                return ins
            P.op("pe", mm, reads=list(aT_keys) + list(w_keys_fn(dg)), writes=[("ps", bank)])
            rows = slice(s_ * 128, (s_ + 1) * 128)
            P.op("sp", lambda e, rs=rs, rows=rows, cols=cols: e.dma_start(out=xr[:, rs, :], in_=X[rows, cols]),
                 reads=[(xname, s_, dg)], writes=[(tag + "xr", rs)], dma=True)
            P.op("dve", lambda e, rs=rs, bank=bank: e.tensor_tensor(out=xo[:, rs, :], in0=ps[:, bank, :], in1=xr[:, rs, :],
                                                                     op=ALU.add),
                 reads=[("ps", bank), (tag + "xr", rs)], writes=[(tag + "xo", rs)])
            P.op("sp", lambda e, rs=rs, rows=rows, cols=cols: e.dma_start(out=X[rows, cols], in_=xo[:, rs, :]),
                 reads=[(tag + "xo", rs)], writes=[(xname, s_, dg)], dma=True)


def sgu_phase(P, nc, ps, C, tag, X, st_lo, st_hi, gvec, Win, kin, Wout, kout, vgain, w_s, b_s):
    TS = 3
    DS = 4096
    tiles = split_tiles(st_lo, st_hi, TS)
    with contextlib.ExitStack() as st:
        def sb(name, shape, dt):
            return st.enter_context(nc.sbuf_tensor(tag + name, shape, dt))
        N = NormCtx(P, nc, st, C, tag, gvec, BF16, nslot=2)
        xnT = sb("xnT", [128, KC, TS * 128], BF16)
        wsl = sb("wsl", [128, 2, 16, 512], BF16)
        U = sb("U", [128, TS, DS], BF16)
        V = sb("V", [128, TS, DS], BF16)
        vg = sb("vg", [128, DS], F32)
        G = sb("G", [128, DS], BF16)
        GT = sb("GT", [128, 32, TS * 128], BF16)
        wsT = sb("wsT", [128, 16, 128], BF16)
        bs = sb("bs", [128, 16], F32)
        xr = sb("xr", [128, 3, 512], F32)
        xo = sb("xo", [128, 3, 512], F32)
        vst = sb("vst", [128, 4], F32)
        P.op("sp", lambda e: e.dma_start(out=vg[:], in_=vgain.partition_broadcast(128)), writes=[tag + "vg"], dma=True)
        P.op("sp", lambda e: e.dma_start(out=bs[:], in_=b_s.rearrange("g i -> i g"), allow_slow_non_contiguous=True),
             writes=[tag + "bs"], dma=True)
        wstage = N.xt
        P.op("sp", lambda e: e.dma_start(out=wstage[:, 0, :].rearrange("p (g j) -> p g j", g=16),
                                         in_=w_s.rearrange("g i j -> i g j")), writes=[(tag + "nxt", 0)], dma=True)
        transpose_to(P, ps, C, lambda c: wstage[:, 0, c * 128:(c + 1) * 128], (tag + "nxt", 0), 16,
                     lambda c0, n: wsT[:, c0:c0 + n, :], [tag + "wsT"], bank=7, fp32=True)
        for k in ("vg", "bs", "wsT"):
            P.const.add(tag + k)
        cnt = {"w": 0, "b": 0, "r": 0}

        def loadw(src, keys):
            sl = cnt["w"] % 2
            cnt["w"] += 1
            P.op("sp", lambda e: e.dma_start(out=wsl[:, sl], in_=src), reads=keys, writes=[(tag + "wsl", sl)], dma=True)
            return sl

        for (t0, nsub) in tiles:
            for ts in range(nsub):
                sl, khn = N.run(X, t0 + ts)
                transpose_to(P, ps, C, lambda c, sl=sl: N.hn[:, sl, c * 128:(c + 1) * 128], khn, KC,
                             lambda c0, n, ts=ts: xnT[:, c0:c0 + n, ts * 128:(ts + 1) * 128], [(tag + "xnT", ts)], bank=7)
            nxt = loadw(Win[0], [(kin, 0)])
            for cg in range(16):
                sl = nxt
                if cg + 1 < 16:
                    nxt = loadw(Win[cg + 1], [(kin, cg + 1)])
                for ts in range(nsub):
                    bank = cnt["b"] % 6
                    cnt["b"] += 1

                    def mm(e, sl=sl, ts=ts, bank=bank):
                        ins = None
                        for kc in range(KC):
                            ins = e.matmul(ps[:, bank, :], lhsT=xnT[:, kc, ts * 128:(ts + 1) * 128], rhs=wsl[:, sl, kc, :],
                                           start=(kc == 0), stop=(kc == KC - 1))
                        return ins
                    P.op("pe", mm, reads=[(tag + "xnT", ts), (tag + "wsl", sl)], writes=[("ps", bank)])
                    dst = U if cg < 8 else V
                    dk = (tag + ("U" if cg < 8 else "V"), ts, cg % 8)
                    c0 = (cg % 8) * 512
                    P.op("act", lambda e, dst=dst, ts=ts, c0=c0, bank=bank: e.activation(
                        out=dst[:, ts, c0:c0 + 512], in_=ps[:, bank, :], func=AF.Gelu), reads=[("ps", bank)], writes=[dk])
            for ts in range(nsub):
                vk = [(tag + "V", ts, j) for j in range(8)]
                P.op("act", lambda e, ts=ts: e.activation(out=G[:], in_=V[:, ts, :], func=AF.Square, accum_out=vst[:, 0:1]),
                     reads=vk, writes=[tag + "G", tag + "vst"])
                P.op("act", lambda e: e.activation(out=vst[:, 1:2], in_=vst[:, 0:1], func=AF.Sqrt, bias=C["eps"][:],
                                                   scale=1.0 / DS), reads=[tag + "vst"], writes=[tag + "vst"])
                P.op("dve", lambda e: e.reciprocal(out=vst[:, 2:3], in_=vst[:, 1:2]), reads=[tag + "vst"], writes=[tag + "vst"])
                P.op("dve", lambda e, ts=ts: e.scalar_tensor_tensor(out=V[:, ts, :], in0=V[:, ts, :], scalar=vst[:, 2:3],
                                                                    in1=vg[:], op0=ALU.mult, op1=ALU.mult),
                     reads=vk + [tag + "vst", tag + "vg"], writes=vk)
            for ts in range(nsub):
                vk = [(tag + "V", ts, j) for j in range(8)]
                uk = [(tag + "U", ts, j) for j in range(8)]
                for gp in range(8):
                    bank = cnt["b"] % 6
                    cnt["b"] += 1

                    def mm(e, ts=ts, gp=gp, bank=bank):
                        ins = None
                        for q in range(2):
                            g = 2 * gp + q
                            ins = e.matmul(ps[:, bank, q * 256:(q + 1) * 256], lhsT=wsT[:, g, :],
                                           rhs=V[:, ts, g * 256:(g + 1) * 256], start=True, stop=True)
                        return ins
                    P.op("pe", mm, reads=vk + [tag + "wsT"], writes=[("ps", bank)])
                    for q in range(2):
                        g = 2 * gp + q
                        P.op("dve", lambda e, ts=ts, g=g, q=q, bank=bank: e.scalar_tensor_tensor(
                            out=G[:, g * 256:(g + 1) * 256], in0=ps[:, bank, q * 256:(q + 1) * 256], scalar=bs[:, g:g + 1],
                            in1=U[:, ts, g * 256:(g + 1) * 256], op0=ALU.add, op1=ALU.mult),
                            reads=[("ps", bank), tag + "bs"] + uk, writes=[tag + "G"])
                transpose_to(P, ps, C, lambda c: G[:, c * 128:(c + 1) * 128], tag + "G", 32,
                             lambda c0, n, ts=ts: GT[:, c0:c0 + n, ts * 128:(ts + 1) * 128], [(tag + "GT", ts)], bank=7)
            xname = X.tensor.name
            nxt = loadw(Wout[0, :, 0:16, :], [(kout, 0)])
            for dg in range(4):
                cols = slice(dg * 512, (dg + 1) * 512)
                for half in range(2):
                    sl = nxt
                    nh = dg * 2 + half + 1
                    if nh < 8:
                        nxt = loadw(Wout[nh // 2, :, (nh % 2) * 16:(nh % 2) * 16 + 16, :], [(kout, nh // 2)])
                    for ts in range(nsub):
                        bank = ts

                        def mm(e, sl=sl, ts=ts, half=half, bank=bank):
                            ins = None
                            for k in range(16):
                                fc = half * 16 + k
                                ins = e.matmul(ps[:, bank, :], lhsT=GT[:, fc, ts * 128:(ts + 1) * 128], rhs=wsl[:, sl, k, :],
                                               start=(fc == 0), stop=(fc == 31))
                            return ins
                        P.op("pe", mm, reads=[(tag + "GT", ts), (tag + "wsl", sl)], writes=[("ps", bank)])
                for ts in range(nsub):
                    s_ = t0 + ts
                    rs = cnt["r"] % 3
                    cnt["r"] += 1
                    rows = slice(s_ * 128, (s_ + 1) * 128)
                    P.op("sp", lambda e, rs=rs, rows=rows, cols=cols: e.dma_start(out=xr[:, rs, :], in_=X[rows, cols]),
                         reads=[(xname, s_, dg)], writes=[(tag + "xr", rs)], dma=True)
                    P.op("dve", lambda e, rs=rs, ts=ts: e.tensor_tensor(out=xo[:, rs, :], in0=ps[:, ts, :], in1=xr[:, rs, :],
                                                                         op=ALU.add),
                         reads=[("ps", ts), (tag + "xr", rs)], writes=[(tag + "xo", rs)])
                    P.op("sp", lambda e, rs=rs, rows=rows, cols=cols: e.dma_start(out=X[rows, cols], in_=xo[:, rs, :]),
                         reads=[(tag + "xo", rs)], writes=[(xname, s_, dg)], dma=True)
        P.flush()


NH = 64
HD = 32


def na_qkv_phase(P, nc, ps, C, tag, X, st_lo, st_hi, gvec, Wqk, kqk, Wv, kv, qgain, kgain, QT, VX):
    TS = 4
    tiles = split_tiles(st_lo, st_hi, TS)
    with contextlib.ExitStack() as st:
        def sb(name, shape, dt):
            return st.enter_context(nc.sbuf_tensor(tag + name, shape, dt))
        N = NormCtx(P, nc, st, C, tag, gvec, BF16, nslot=2)
        xnT = sb("xnT", [128, KC, TS * 128], BF16)
        wq = sb("wq", [128, 3, KC, 128], BF16)
        wv = sb("wv", [128, 2, KC, 512], BF16)
        sq = sb("sq", [128, 2, 512], BF16)
        lnt = sb("lnt", [128, 2, 512], F32)
        rstd = sb("rstd", [128, 2, 512], F32)
        qo = sb("qo", [128, 3, 512], BF16)
        vx = sb("vx", [128, TS, NH, HD + 1], BF16)
        gqk = sb("gqk", [128, 2], F32)
        for hl in range(4):
            P.op("sp", lambda e, hl=hl: e.dma_start(out=gqk[32 * hl:32 * hl + 32, 0:1], in_=qgain), writes=[tag + "gqk"], dma=True)
            P.op("sp", lambda e, hl=hl: e.dma_start(out=gqk[32 * hl:32 * hl + 32, 1:2], in_=kgain), writes=[tag + "gqk"], dma=True)
        P.op("dve", lambda e: e.tensor_scalar(out=gqk[:, 0:1], in0=gqk[:, 0:1], scalar1=float(HD) ** -0.5, scalar2=None,
                                              op0=ALU.mult), reads=[tag + "gqk"], writes=[tag + "gqk"])
        P.op("dve", lambda e: e.memset(vx[:, :, :, HD:HD + 1], 1.0), writes=[(tag + "vx", ts) for ts in range(TS)])
        P.const.add(tag + "gqk")
        cnt = {"wq": 0, "wv": 0, "s": 0, "q": 0, "b": 0, "ev": 0}

        def load_wq(fo):
            sl = cnt["wq"] % 3
            cnt["wq"] += 1
            P.op("sp", lambda e: e.dma_start(out=wq[:, sl], in_=Wqk[fo]), reads=[(kqk, fo)], writes=[(tag + "wq", sl)], dma=True)
            return sl

        def load_wv(cg):
            sl = cnt["wv"] % 2
            cnt["wv"] += 1
            P.op("sp", lambda e: e.dma_start(out=wv[:, sl], in_=Wv[cg]), reads=[(kv, cg)], writes=[(tag + "wv", sl)], dma=True)
            return sl

        for (t0, nsub) in tiles:
            n = nsub * 128
            tok0 = t0 * 128
            for ts in range(nsub):
                sl, khn = N.run(X, t0 + ts)
                transpose_to(P, ps, C, lambda c, sl=sl: N.hn[:, sl, c * 128:(c + 1) * 128], khn, KC,
                             lambda c0, n_, ts=ts: xnT[:, c0:c0 + n_, ts * 128:(ts + 1) * 128], [(tag + "xnT", ts)], bank=7)
            xk = [(tag + "xnT", ts) for ts in range(nsub)]
            pend = [load_wq(fo) for fo in range(3)]
            for fo in range(32):
                sl = pend.pop(0)
                bA = cnt["b"] % 3
                bB = 3 + cnt["b"] % 2
                cnt["b"] += 1
                ss = cnt["s"] % 2
                cnt["s"] += 1
                qs = cnt["q"] % 3
                cnt["q"] += 1

                def mm(e, sl=sl, bA=bA, n=n):
                    ins = None
                    for kc in range(KC):
                        ins = e.matmul(ps[:, bA, 0:n], lhsT=wq[:, sl, kc, :], rhs=xnT[:, kc, 0:n], start=(kc == 0), stop=(kc == KC - 1))
                    return ins
                P.op("pe", mm, reads=xk + [(tag + "wq", sl)], writes=[("ps", bA)])
                if fo + 3 < 32:
                    pend.append(load_wq(fo + 3))
                P.op("act", lambda e, ss=ss, bA=bA, n=n: e.activation(out=sq[:, ss, 0:n], in_=ps[:, bA, 0:n], func=AF.Square),
                     reads=[("ps", bA)], writes=[(tag + "sq", ss)])
                P.op("pe", lambda e, ss=ss, bB=bB, n=n: e.matmul(ps[:, bB, 0:n], lhsT=C["bd"][:], rhs=sq[:, ss, 0:n],
                                                                  start=True, stop=True),
                     reads=[(tag + "sq", ss), "bd"], writes=[("ps", bB)])
                P.op("act", lambda e, ss=ss, bB=bB, n=n: e.activation(out=lnt[:, ss, 0:n], in_=ps[:, bB, 0:n], func=AF.Ln,
                                                                       bias=C["eps"][:], scale=1.0),
                     reads=[("ps", bB)], writes=[(tag + "lnt", ss)])
                P.op("act", lambda e, ss=ss, n=n: e.activation(out=rstd[:, ss, 0:n], in_=lnt[:, ss, 0:n], func=AF.Exp, scale=-0.5),
                     reads=[(tag + "lnt", ss)], writes=[(tag + "rstd", ss)])
                gi = 0 if fo < 16 else 1
                P.op("dve", lambda e, ss=ss, qs=qs, bA=bA, n=n, gi=gi: e.scalar_tensor_tensor(
                    out=qo[:, qs, 0:n], in0=ps[:, bA, 0:n], scalar=gqk[:, gi:gi + 1], in1=rstd[:, ss, 0:n],
                    op0=ALU.mult, op1=ALU.mult), reads=[("ps", bA), (tag + "rstd", ss), tag + "gqk"], writes=[(tag + "qo", qs)])
                P.op("sp", lambda e, qs=qs, fo=fo, tok0=tok0, n=n: e.dma_start(out=QT[fo, :, tok0:tok0 + n], in_=qo[:, qs, 0:n]),
                     reads=[(tag + "qo", qs)], writes=[(tag + "QT", fo, t0)], dma=True)
            nxt = load_wv(0)
            for cg in range(4):
                sl = nxt
                if cg + 1 < 4:
                    nxt = load_wv(cg + 1)
                for ts in range(nsub):
                    bank = 5 + cnt["ev"] % 2
                    ev = cnt["ev"]
                    cnt["ev"] += 1

                    def mm(e, sl=sl, ts=ts, bank=bank):
                        ins = None
                        for kc in range(KC):
                            ins = e.matmul(ps[:, bank, :], lhsT=xnT[:, kc, ts * 128:(ts + 1) * 128], rhs=wv[:, sl, kc, :],
                                           start=(kc == 0), stop=(kc == KC - 1))
                        return ins
                    P.op("pe", mm, reads=[(tag + "xnT", ts), (tag + "wv", sl)], writes=[("ps", bank)])
                    if ev % 2 == 0:
                        P.op("dve", lambda e, ts=ts, cg=cg, bank=bank: e.tensor_copy(
                            out=vx[:, ts, 16 * cg:16 * cg + 16, 0:HD], in_=ps[:, bank, :].rearrange("p (h d) -> p h d", h=16)),
                            reads=[("ps", bank)], writes=[(tag + "vx", ts)])
                    else:
                        P.op("act", lambda e, ts=ts, cg=cg, bank=bank: e.copy(
                            out=vx[:, ts, 16 * cg:16 * cg + 16, 0:HD], in_=ps[:, bank, :].rearrange("p (h d) -> p h d", h=16)),
                            reads=[("ps", bank)], writes=[(tag + "vx", ts)])
            for ts in range(nsub):
                s_ = t0 + ts
                P.op("sp", lambda e, ts=ts, s_=s_: e.dma_start(out=VX[s_ * 128:(s_ + 1) * 128, :],
                                                                  in_=vx[:, ts].rearrange("p h d -> p (h d)")),
                     reads=[(tag + "vx", ts)], writes=[(tag + "VX", s_)], dma=True)
        P.flush()


def na_attn_phase(P, nc, ps, C, tag, NB, q_lo, q_hi, QT, VX, AO, TB, RVd, qkv_tag, qkv_tiles):
    NTOK = NB * 128
    INT_LO, INT_HI = 7, 35
    NS = 3
    SB = (0, 2, 6)
    DEFER = 2
    with contextlib.ExitStack() as st:
        def sb(name, shape, dt):
            return st.enter_context(nc.sbuf_tensor(tag + name, shape, dt))
        Kc = sb("Kc", [128, 2, NTOK], BF16)
        Qc = sb("Qc", [128, 2, NTOK], BF16)
        Vc = sb("Vc", [128, 2, NB, 4, HD + 1], BF16)
        Gt = sb("Gt", [128, 2, 4, 896], F32)
        RVc = sb("RVc", [128, NB, 14], BF16)
        RVa = sb("RVa", [128, NB, 14, 64], BF16)
        Sg = sb("Sg", [128, NS, 896], F32)
        PT = sb("PT", [128, NS, 896], BF16)
        rec = sb("rec", [128, 2, 4], F32)
        aot = sb("aot", [128, 2, 4, HD], BF16)
        P.op("dve", lambda e: e.memset(RVc[:], 0.0), writes=[tag + "RVc"])
        for hl in range(4):
            P.op("sp", lambda e, hl=hl: e.dma_start(out=RVc[32 * hl:32 * hl + 2, :, :], in_=RVd.rearrange("b r c -> r b c")),
                 writes=[tag + "RVc"], dma=True)
        for i in range(q_lo, q_hi):
            P.op("act", lambda e, i=i: e.copy(
                out=RVa[:, i], in_=RVc[:, i, :].unsqueeze(2).broadcast_to([128, 14, 64])),
                reads=[tag + "RVc"], writes=[tag + "RVa"])
        P.const.add(tag + "RVc")
        P.const.add(tag + "RVa")
        cnt = {"s": 0, "x": 0, "o": 0}
        queue = []

        def loads(c):
            cs = c % 2
            qk_keys = [(qkv_tag + "QT", fo, t0) for fo in (c, 16 + c) for (t0, _) in qkv_tiles]
            P.op("sp", lambda e: e.dma_start(out=Kc[:, cs, :], in_=QT[16 + c]), reads=qk_keys,
                 writes=[(tag + "Kc", cs)], dma=True)
            P.op("sp", lambda e: e.dma_start(out=Qc[:, cs, :], in_=QT[c]), reads=qk_keys,
                 writes=[(tag + "Qc", cs)], dma=True)
            P.op("sp", lambda e: e.dma_start(
                out=Vc[:, cs], in_=VX.rearrange("(j p) (h d) -> p j h d", p=128, d=HD + 1)[:, :, 4 * c:4 * c + 4, :]),
                reads=[(qkv_tag + "VX", s_) for s_ in range(NB)], writes=[(tag + "Vc", cs)], dma=True)
            P.op("sp", lambda e: e.dma_start(out=Gt[:, cs], in_=TB[4 * c:4 * c + 4].rearrange("h p f -> p h f")),
                 writes=[(tag + "Gt", cs)], dma=True)

        def s_stage(c, cs, i, hl, interior, xs, olist, a0, a1):
            ssl = cnt["s"] % NS
            cnt["s"] += 1
            b0 = SB[ssl]
            pb = 32 * hl
            def mm_s(e):
                Sv = ps[:, b0:b0 + 2, :].rearrange("p a c -> p (a c)")
                rv = RVa[:, i].rearrange("p a b -> p (a b)")
                ins = None
                for (lo, hi) in ((a0, min(a1, 512)), (max(a0, 512), a1)):
                    if hi > lo:
                        ins = e.matmul(Sv[:, lo:hi], lhsT=C["ind4"][pb:pb + 2, :], rhs=rv[pb:pb + 2, lo:hi],
                                       start=True, stop=False, tile_position=(pb, 0), skip_group_check=True)
                for o in olist:
                    j = i - 3 + o
                    ins = e.matmul(Sv[:, o * 128:(o + 1) * 128], lhsT=Kc[pb:pb + 32, cs, j * 128:(j + 1) * 128],
                                   rhs=Qc[pb:pb + 32, cs, i * 128:(i + 1) * 128], start=False, stop=True,
                                   tile_position=(pb, 0), skip_group_check=True)
                return ins
            P.op("pe", mm_s, reads=[tag + "RVa", (tag + "Kc", cs), (tag + "Qc", cs), "ind4"],
                 writes=[("ps", b0), ("ps", b0 + 1)])
            P.op("dve", lambda e: e.tensor_tensor(
                out=Sg[:, ssl, a0:a1], in0=ps[:, b0:b0 + 2, :].rearrange("p a c -> p (a c)")[:, a0:a1],
                in1=Gt[:, cs, hl, a0:a1], op=ALU.add),
                reads=[("ps", b0), ("ps", b0 + 1), (tag + "Gt", cs)], writes=[(tag + "Sg", ssl)])
            P.op("act", lambda e: e.activation(out=PT[:, ssl, a0:a1], in_=Sg[:, ssl, a0:a1], func=AF.Exp),
                 reads=[(tag + "Sg", ssl)], writes=[(tag + "PT", ssl)])
            return ssl

        def o_stage(c, cs, i, hl, ssl, olist, ob, osl):
            def mm_o(e):
                ins = None
                for o in olist:
                    j = i - 3 + o
                    ins = e.matmul(ps[:, ob, hl * (HD + 1):(hl + 1) * (HD + 1)], lhsT=PT[:, ssl, o * 128:(o + 1) * 128],
                                   rhs=Vc[:, cs, j, hl, :], start=(o == olist[0]), stop=(o == olist[-1]))
                return ins
            P.op("pe", mm_o, reads=[(tag + "PT", ssl), (tag + "Vc", cs)], writes=[("ps", ob)])
            if hl != 3:
                return
            Ov_fn = lambda: ps[:, ob, 0:4 * (HD + 1)].rearrange("p (h d) -> p h d", h=4)
            P.op("dve", lambda e: e.tensor_scalar(
                out=rec[:, osl, :], in0=Ov_fn()[:, :, HD], scalar1=1e-30, scalar2=None, op0=ALU.add),
                reads=[("ps", ob)], writes=[(tag + "rec", osl)])
            P.op("dve", lambda e: e.reciprocal(out=rec[:, osl, :], in_=rec[:, osl, :]),
                 reads=[(tag + "rec", osl)], writes=[(tag + "rec", osl)])
            P.op("dve", lambda e: e.tensor_tensor(
                out=aot[:, osl], in0=Ov_fn()[:, :, 0:HD], in1=rec[:, osl, :].unsqueeze(2).broadcast_to([128, 4, HD]),
                op=ALU.mult), reads=[("ps", ob), (tag + "rec", osl)], writes=[(tag + "aot", osl)])
            P.op("sp", lambda e: e.dma_start(
                out=AO[i * 128:(i + 1) * 128, c * 128:(c + 1) * 128], in_=aot[:, osl].rearrange("p h d -> p (h d)")),
                reads=[(tag + "aot", osl)], writes=[(tag + "AO", i, c)], dma=True)

        loads(0)
        for c in range(16):
            cs = c % 2
            for i in range(q_lo, q_hi):
                interior = INT_LO <= i < INT_HI
                if interior:
                    olist = [1, 2, 3, 4, 5]
                else:
                    jlo, jhi = max(0, i - 3), min(NB - 1, i + 3)
                    olist = list(range(jlo - (i - 3), jhi - (i - 3) + 1))
                xs = None
                a0, a1 = olist[0] * 128, (olist[-1] + 1) * 128
                ob = 4 + cnt["o"] % 2
                osl = cnt["o"] % 2
                cnt["o"] += 1
                for hl in range(4):
                    ssl = s_stage(c, cs, i, hl, interior, xs, olist, a0, a1)
                    queue.append((c, cs, i, hl, ssl, olist, ob, osl))
                    if len(queue) > DEFER:
                        o_stage(*queue.pop(0))
                if i == q_lo and c + 1 < 16:
                    loads(c + 1)
        while queue:
            o_stage(*queue.pop(0))
        P.flush()


def na_out_phase(P, nc, ps, C, tag, X, st_lo, st_hi, AO, at_tag, Wo, ko):
    TS = 4
    tiles = split_tiles(st_lo, st_hi, TS)
    with contextlib.ExitStack() as st:
        def sb(name, shape, dt):
            return st.enter_context(nc.sbuf_tensor(tag + name, shape, dt))
        wo = sb("wo", [128, 4, KC, 512], BF16)
        at = sb("at", [128, 2, D], BF16)
        aT = sb("aT", [128, KC, TS * 128], BF16)
        xr = sb("xr", [128, 3, 512], F32)
        xo = sb("xo", [128, 3, 512], F32)
        for dg in range(4):
            P.op("sp", lambda e, dg=dg: e.dma_start(out=wo[:, dg], in_=Wo[dg]), reads=[(ko, dg)], writes=[(tag + "wo", dg)], dma=True)
        cnt = {"b": 0, "r": 0, "a": 0}
        for (t0, nsub) in tiles:
            for ts in range(nsub):
                s_ = t0 + ts
                sl = cnt["a"] % 2
                cnt["a"] += 1
                P.op("sp", lambda e, sl=sl, s_=s_: e.dma_start(out=at[:, sl, :], in_=AO[s_ * 128:(s_ + 1) * 128, :]),
                     reads=[(at_tag + "AO", s_, c) for c in range(16)], writes=[(tag + "at", sl)], dma=True)
                transpose_to(P, ps, C, lambda c, sl=sl: at[:, sl, c * 128:(c + 1) * 128], (tag + "at", sl), KC,
                             lambda c0, n, ts=ts: aT[:, c0:c0 + n, ts * 128:(ts + 1) * 128], [(tag + "aT", ts)], bank=7)
            emit_out_proj(P, ps, tag, X, t0, nsub, aT, [(tag + "aT", ts) for ts in range(nsub)], KC,
                          lambda dg, k: wo[:, dg, k, :], lambda dg: [(tag + "wo", dg)], xr, xo, cnt, [0, 1, 2, 3, 4, 5])
        P.flush()


NEG = -30000.0
GRID_W = 64
ROWS = 256


def build_tb(rpb):
    kr = np.arange(2)[:, None, None, None, None]
    kc = np.arange(64)[None, :, None, None, None]
    o = np.arange(7)[None, None, :, None, None]
    qr = np.arange(2)[None, None, None, :, None]
    qc = np.arange(64)[None, None, None, None, :]
    dr = 2 * (o - 3) + kr - qr + 7
    dc = kc - qc + 15
    cs = np.clip(qc - 8, 0, GRID_W - 16)
    cvalid = (kc >= cs) & (kc < cs + 16)
    dr_b, dc_b, cv_b = np.broadcast_arrays(dr, dc, cvalid)
    dc_c = np.clip(dc_b, 0, 30)
    g = rpb[:, dr_b, dc_c]
    g = np.where(cv_b[None], g, np.float32(NEG)).astype(np.float32)
    return np.ascontiguousarray(g.reshape(64, 128, 896))


def build_rv(NB, row0):
    rv = np.full((NB, 2, 7, 2), NEG, np.float32)
    for i in range(NB):
        for o in range(7):
            for kr in range(2):
                for qr in range(2):
                    Rq = row0 + 2 * i + qr
                    Rk = row0 + 2 * (i + o - 3) + kr
                    if 0 <= Rq < ROWS and 0 <= Rk < ROWS:
                        rs = min(max(Rq - 4, 0), ROWS - 8)
                        if rs <= Rk < rs + 8:
                            rv[i, kr, o, qr] = 0.0
    return rv.reshape(NB, 2, 14).astype(ml_dtypes.bfloat16)


def build_consts():
    ident = np.eye(128, dtype=np.float32)
    p = np.arange(128)
    bd = np.where((p[:, None] // 32) == (p[None, :] // 32), np.float32(1.0 / 32), np.float32(0.0))
    ind4 = np.zeros((128, 128), np.float32)
    for hl in range(4):
        for r in range(2):
            ind4[32 * hl + r, :] = (p // 64 == r)
    return dict(ident=ident.astype(ml_dtypes.bfloat16), identf=ident, bd=bd.astype(ml_dtypes.bfloat16),
                ind4=ind4.astype(ml_dtypes.bfloat16))


DEPTH = 4
NB = 41
NTOK = NB * 128
OWN_LO, OWN_HI = 5, 37
SEQ = 16384
BATCH = 2


_DBG = [None]


def build_program():
    nc = bass.Bass("TRN2", target_bir_lowering=False)

    def din(name, shape, dt=F32):
        return nc.dram_tensor(name, list(shape), dt, kind="ExternalInput").ap()

    def dscr(name, shape, dt):
        return nc.dram_tensor(name, list(shape), dt, kind="Internal").ap()

    x_loc = din("x_loc", [NTOK, D])
    I = {}
    for nm in ("ffn1_norm", "mix_norm", "ffn2_norm", "out_norm"):
        I[nm] = din(nm, [DEPTH, D])
    for nm in ("ffn1_w_gate", "ffn1_w_up", "ffn2_w_gate", "ffn2_w_up"):
        I[nm] = din(nm, [DEPTH, D, DFF])
    for nm in ("ffn1_w_down", "ffn2_w_down"):
        I[nm] = din(nm, [DEPTH, DFF, D])
    I["na_w_qkv"] = din("na_w_qkv", [2, D, 3 * D])
    I["na_q_gain"] = din("na_q_gain", [2, HD])
    I["na_k_gain"] = din("na_k_gain", [2, HD])
    I["na_w_o"] = din("na_w_o", [2, D, D])
    I["sgu_w_in"] = din("sgu_w_in", [1, D, 8192])
    I["sgu_v_gain"] = din("sgu_v_gain", [1, 4096])
    I["sgu_w_s"] = din("sgu_w_s", [1, 16, 128, 128])
    I["sgu_b_s"] = din("sgu_b_s", [1, 16, 128])
    I["sgu_w_out"] = din("sgu_w_out", [1, 4096, D])
    I["pool_w"] = din("pool_w", [1, 4, 512, 512])
    I["pool_scale"] = din("pool_scale", [1, D])
    TB = [din("tb0", [NH, 128, 896]), din("tb1", [NH, 128, 896])]
    RVd = din("rv", [NB, 2, 14], BF16)
    valid_d = din("valid", [NTOK, 1])
    invcnt_d = din("invcnt", [4, NTOK])
    cin = {k: din("c_" + k, [128, 128], F32 if k == "identf" else BF16) for k in ("ident", "identf", "bd", "ind4")}
    out = nc.dram_tensor("out", [(OWN_HI - OWN_LO) * 128, D], F32, kind="ExternalOutput").ap()

    X = dscr("X", [NTOK, D], F32)
    QT = dscr("QT", [32, 128, NTOK], BF16)
    VX = dscr("VX", [NTOK, NH * (HD + 1)], BF16)
    AO = dscr("AO", [NTOK, D], BF16)
    HT = dscr("HT", [KC, 128, NTOK + 16], F32)
    W = {}
    for l in range(DEPTH):
        for f in (1, 2):
            W["g%d%d" % (f, l)] = dscr("wg%d%d" % (f, l), [FC, 128, KC, 128], BF16)
            W["u%d%d" % (f, l)] = dscr("wu%d%d" % (f, l), [FC, 128, KC, 128], BF16)
            W["d%d%d" % (f, l)] = dscr("wd%d%d" % (f, l), [4, 128, FC, 512], BF16)
    for j in range(2):
        W["qk%d" % j] = dscr("wqk%d" % j, [32, 128, KC, 128], BF16)
        W["v%d" % j] = dscr("wv%d" % j, [4, 128, KC, 512], BF16)
        W["o%d" % j] = dscr("wo%d" % j, [4, 128, KC, 512], BF16)
    W["sin"] = dscr("wsin", [16, 128, KC, 512], BF16)
    W["sout"] = dscr("wsout", [4, 128, 32, 512], BF16)
    W["pw"] = dscr("wpw", [4, 128, 4, 512], BF16)

    P = Prog(nc)

    def cast_ffn(f, l):
        def go():
            cast_cols(P, I["ffn%d_w_gate" % f][l], W["g%d%d" % (f, l)], "g%d%d" % (f, l), D, DFF, 128)
            cast_cols(P, I["ffn%d_w_up" % f][l], W["u%d%d" % (f, l)], "u%d%d" % (f, l), D, DFF, 128)
            cast_cols(P, I["ffn%d_w_down" % f][l], W["d%d%d" % (f, l)], "d%d%d" % (f, l), DFF, D, 512)
        return go

    def cast_na(j):
        def go():
            cast_cols(P, I["na_w_qkv"][j][:, 0:2 * D], W["qk%d" % j], "qk%d" % j, D, 2 * D, 128)
            cast_cols(P, I["na_w_qkv"][j][:, 2 * D:3 * D], W["v%d" % j], "v%d" % j, D, D, 512)
            cast_cols(P, I["na_w_o"][j], W["o%d" % j], "o%d" % j, D, D, 512)
        return go

    def cast_sgu():
        cast_cols(P, I["sgu_w_in"][0], W["sin"], "sin", D, 8192, 512)
        cast_cols(P, I["sgu_w_out"][0], W["sout"], "sout", 4096, D, 512)

    def cast_pool():
        for g in range(4):
            P.op("pool", lambda e, g=g: e.dma_start(out=W["pw"][g], in_=I["pool_w"][0][g].rearrange("(cc p) d -> p cc d", p=128)),
                 writes=[("pw", g)], dma=True)

    with contextlib.ExitStack() as st:
        ps = st.enter_context(nc.psum_tensor("ps", [128, 8, 512], F32))
        C = {}
        for k in ("ident", "identf", "bd", "ind4"):
            C[k] = st.enter_context(nc.sbuf_tensor(k + "_sb", [128, 128], F32 if k == "identf" else BF16))
            P.op("sp", lambda e, k=k: e.dma_start(out=C[k][:], in_=cin[k]), writes=[k], dma=True)
            P.const.add(k)
        C["eps"] = st.enter_context(nc.sbuf_tensor("eps_sb", [128, 1], F32))
        P.op("dve", lambda e: e.memset(C["eps"][:], EPS), writes=["eps"])
        P.const.add("eps")
        cast_ffn(1, 0)()
        P.flush()

        def vec(nm, l):
            return I[nm][l:l + 1, :]

        def ffn(f, l, lo, hi, Xin, g_pre, nxt_cast):
            if nxt_cast is not None:
                nxt_cast()
            ffn_phase(P, nc, ps, C, "f%d%d" % (f, l), Xin, X, lo, hi, W["g%d%d" % (f, l)], W["u%d%d" % (f, l)],
                      W["d%d%d" % (f, l)], ("g%d%d" % (f, l), "u%d%d" % (f, l), "d%d%d" % (f, l)), g_pre,
                      vec("ffn%d_norm" % f, l))

        def na(j, l, kv_lo, kv_hi, q_lo, q_hi, nxt_cast):
            nxt_cast()
            qt = "q%d" % j
            na_qkv_phase(P, nc, ps, C, qt, X, kv_lo, kv_hi, vec("mix_norm", l), W["qk%d" % j], "qk%d" % j, W["v%d" % j],
                         "v%d" % j, I["na_q_gain"][j:j + 1, :].rearrange("o d -> d o"),
                         I["na_k_gain"][j:j + 1, :].rearrange("o d -> d o"), QT, VX)
            na_attn_phase(P, nc, ps, C, "a%d" % j, NB, q_lo, q_hi, QT, VX, AO, TB[j], RVd, qt, split_tiles(kv_lo, kv_hi, 4))
            na_out_phase(P, nc, ps, C, "o%d" % j, X, q_lo, q_hi, AO, "a%d" % j, W["o%d" % j], "o%d" % j)

        ffn(1, 0, 0, NB, x_loc, None, cast_na(0))
        if _DBG[0] == "ffn":
            final_phase(P, nc, C, "fin", X, out, OWN_LO, OWN_HI, vec("out_norm", 3))
            P.close()
            return nc
        na(0, 0, 0, NB, 2, 39, cast_ffn(2, 0))
        if _DBG[0] == "na":
            final_phase(P, nc, C, "fin", X, out, OWN_LO, OWN_HI, vec("out_norm", 3))
            P.close()
            return nc
        ffn(2, 0, 2, 39, X, None, cast_ffn(1, 1))
        ffn(1, 1, 2, 39, X, vec("out_norm", 0), cast_sgu)
        cast_ffn(2, 1)()
        sgu_phase(P, nc, ps, C, "sg", X, 2, 39, vec("mix_norm", 1), W["sin"], "sin", W["sout"], "sout",
                  I["sgu_v_gain"], I["sgu_w_s"][0], I["sgu_b_s"][0])
        ffn(2, 1, 2, 39, X, None, cast_ffn(1, 2))
        ffn(1, 2, 2, 39, X, vec("out_norm", 1), cast_pool)
        cast_ffn(2, 2)()
        pool_phase(P, nc, ps, C, "pl", X, HT, 2, 39, vec("mix_norm", 2), W["pw"], "pw", I["pool_scale"], valid_d, invcnt_d, NTOK)
        ffn(2, 2, 3, 39, X, None, cast_ffn(1, 3))
        ffn(1, 3, 3, 39, X, vec("out_norm", 2), cast_na(1))
        na(1, 3, 3, 39, OWN_LO, OWN_HI, cast_ffn(2, 3))
        ffn(2, 3, OWN_LO, OWN_HI, X, None, None)
        final_phase(P, nc, C, "fin", X, out, OWN_LO, OWN_HI, vec("out_norm", 3))
    P.close()
    return nc


def host_inputs(inputs, core):
    b, q = divmod(core, 4)
    row0 = 64 * q - 10
    x = np.asarray(inputs["x"], dtype=np.float32)
    x_loc = np.zeros((NTOK, D), np.float32)
    t_lo, t_hi = row0 * 64, row0 * 64 + NTOK
    a, c = max(t_lo, 0), min(t_hi, SEQ)
    x_loc[a - t_lo:c - t_lo] = x[b, a:c]
    t = np.arange(NTOK) + t_lo
    valid = ((t >= 0) & (t < SEQ)).astype(np.float32).reshape(NTOK, 1)
    invcnt = np.ones((4, NTOK), np.float32)
    for g, w in enumerate((2, 4, 8, 16)):
        lo = np.clip(t - w // 2, 0, SEQ)
        hi = np.clip(t + w // 2, 0, SEQ)
        cnt = np.maximum(hi - lo, 1).astype(np.float32)
        invcnt[g] = np.float32(1.0) / cnt
    m = {"x_loc": x_loc, "valid": valid, "invcnt": invcnt, "rv": build_rv(NB, row0)}
    return m


_NC_CACHE = {}


def _run(inputs, cores):
    if "nc" not in _NC_CACHE:
        _NC_CACHE["nc"] = build_program()
    nc = _NC_CACHE["nc"]
    shared = {}
    for k, v in inputs.items():
        if k != "x":
            shared[k] = np.ascontiguousarray(np.asarray(v, dtype=np.float32))
    rpb = shared.pop("na_rpb")
    shared["tb0"] = build_tb(rpb[0])
    shared["tb1"] = build_tb(rpb[1])
    for k, v in build_consts().items():
        shared["c_" + k] = v
    in_maps = []
    for c in cores:
        m = dict(shared)
        m.update(host_inputs(inputs, c))
        in_maps.append(m)
    res = run_bass_kernel_spmd(nc, in_maps, core_ids=list(range(len(cores))))
    return [np.asarray(r["out"]) for r in res.results]


def kernel(**inputs):
    outs = _run(inputs, list(range(8)))
    full = np.zeros((BATCH, SEQ, D), np.float32)
    for c, o in enumerate(outs):
        b, q = divmod(c, 4)
        full[b, q * 4096:(q + 1) * 4096] = o
    return full
```

```python
import contextlib
import numpy as np
import ml_dtypes
import concourse.bass as bass
import concourse.mybir as mybir
from concourse.bass_utils import run_bass_kernel_spmd

F32 = mybir.dt.float32
BF16 = mybir.dt.bfloat16
AF = mybir.ActivationFunctionType
ALU = mybir.AluOpType
AX = mybir.AxisListType

D = 2048
DFF = 5632
KC = D // 128
FC = DFF // 128
EPS = 1e-6
NSUB_MAX = 7


class _Sem:
    __slots__ = ("h", "cnt", "name")

    def __init__(self, name):
        self.h = None
        self.cnt = 0
        self.name = name


class _Eng:
    def __init__(self, name, ndma=0):
        self.name = name
        self.ops = []
        self.sem = _Sem("c_" + name)
        self.waited = {}
        self.dma_sems = [_Sem("d_%s%d" % (name, i)) for i in range(ndma)]
        self.rr = 0


class Prog:
    def __init__(self, nc, ndma=10):
        self.nc = nc
        self.E = {
            "sp": _Eng("sp", 32),
            "act": _Eng("act", 2),
            "pool": _Eng("pool", 12),
            "dve": _Eng("dve"),
            "pe": _Eng("pe"),
        }
        self.lastw = {}
        self.readers = {}
        self.const = set()
        self.stack = contextlib.ExitStack()
        for s in self.all_sems():
            s.h = self.stack.enter_context(nc.semaphore(s.name))

    def all_sems(self):
        out = []
        for e in self.E.values():
            out.append(e.sem)
            out.extend(e.dma_sems)
        return out

    def op(self, eng, fn, reads=(), writes=(), dma=False):
        e = self.E[eng]
        deps = []
        for k in reads:
            t = self.lastw.get(k)
            if t is not None:
                deps.append(t)
        for k in writes:
            t = self.lastw.get(k)
            if t is not None:
                deps.append(t)
            r = self.readers.get(k)
            if r:
                deps.extend(r.values())
        if dma:
            ds = e.dma_sems[e.rr]
            e.rr = (e.rr + 1) % len(e.dma_sems)
            if ds.cnt > 0:
                deps.append((ds, ds.cnt))
            ds.cnt += 16
            tok = (ds, ds.cnt)
        else:
            e.sem.cnt += 1
            tok = (e.sem, e.sem.cnt)
        newmax = {}
        for (so, v) in deps:
            if e.waited.get(so, 0) < v and newmax.get(so, 0) < v:
                newmax[so] = v
        waits = list(newmax.items())
        for so, v in waits:
            e.waited[so] = v
        e.ops.append((fn, waits, tok, 16 if dma else 1))
        for k in reads:
            if k in self.const:
                continue
            self.readers.setdefault(k, {})[tok[0]] = tok
        for k in writes:
            self.lastw[k] = tok
            self.readers[k] = {}
        return tok

    def wait_all(self, eng):
        e = self.E[eng]
        waits = []
        for s in self.all_sems():
            if s.cnt > 0 and e.waited.get(s, 0) < s.cnt:
                waits.append((s, s.cnt))
                e.waited[s] = s.cnt
        e.ops.append((None, waits, None, 0))

    def flush(self):
        self.wait_all("sp")
        with self.nc.Block() as block:
            def run(e):
                ops = e.ops

                def body(eng):
                    for fn, waits, tok, amt in ops:
                        for so, v in waits:
                            eng.wait_ge(so.h, v)
                        if fn is None:
                            continue
                        ins = fn(eng)
                        if tok is not None:
                            ins.then_inc(tok[0].h, amt)
                return body

            reg = {"sp": block.sync, "act": block.scalar, "pool": block.gpsimd,
                   "dve": block.vector, "pe": block.tensor}
            for n, e in self.E.items():
                if e.ops:
                    reg[n](run(e))
        for e in self.E.values():
            e.ops = []

    def close(self):
        self.stack.close()


def split_tiles(lo, hi, mx):
    n = hi - lo
    nt = -(-n // mx)
    base, rem = divmod(n, nt)
    out = []
    s = lo
    for i in range(nt):
        k = base + (1 if i < rem else 0)
        out.append((s, k))
        s += k
    return out


def cast_cols(P, src, dst, key, K, ncols, cw):
    s = src.rearrange("(kc p) f -> p kc f", p=128)
    ng = ncols // cw
    per = max(1, 4096 // (K // 128 * 128) * 1)
    for g in range(ng):
        P.op("pool", lambda e, g=g: e.dma_start(out=dst[g], in_=s[:, :, g * cw:(g + 1) * cw]),
             writes=[(key, g)], dma=True)


def ffn_phase(P, nc, ps, C, tag, Xin, Xout, st_lo, st_hi, Wg, Wu, Wd, wkeys, g_pre, g_ffn):
    tiles = split_tiles(st_lo, st_hi, NSUB_MAX)
    NTM = NSUB_MAX * 128
    kg, ku, kd = wkeys
    xin_name = Xin.tensor.name
    xout_name = Xout.tensor.name
    with contextlib.ExitStack() as st:
        def sb(name, shape, dt):
            return st.enter_context(nc.sbuf_tensor(tag + name, shape, dt))
        xnT = sb("xnT", [128, KC, NTM], BF16)
        hT = sb("hT", [128, FC, NTM], BF16)
        wgu = sb("wgu", [128, 3, 2, KC, 128], BF16)
        wd = sb("wd", [128, 3, 4, 512], BF16)
        xt = sb("xt", [128, 2, D], F32)
        hn = sb("hn", [128, 2, D], BF16)
        gbf = sb("gbf", [128, D], F32)
        gbp = sb("gbp", [128, D], F32) if g_pre is not None else None
        sil = sb("sil", [128, 2, 512], BF16)
        xr = sb("xr", [128, NSUB_MAX, 512], F32)
        stat = sb("stat", [128, 2, 8], F32)

        P.op("sp", lambda e: e.dma_start(out=gbf[:], in_=g_ffn.partition_broadcast(128)),
             writes=[tag + "gbf"], dma=True)
        if g_pre is not None:
            P.op("sp", lambda e: e.dma_start(out=gbp[:], in_=g_pre.partition_broadcast(128)),
                 writes=[tag + "gbp"], dma=True)
        P.const.add(tag + "gbf")
        P.const.add(tag + "gbp")

        cnt = {"x": 0, "wgu": 0, "wd": 0, "r": 0, "sil": 0}

        def prologue_steps(t0, nsub):
            slots = {}
            fronts, pes = [], []

            def mk_front(ts):
                def front():
                    s_ = t0 + ts
                    sl = cnt["x"] % 2
                    cnt["x"] += 1
                    slots[ts] = sl
                    rows = slice(s_ * 128, (s_ + 1) * 128)
                    kx, khn, kst = (tag + "xt", sl), (tag + "hn", sl), (tag + "stat", sl)
                    P.op("sp", lambda e: e.dma_start(out=xt[:, sl, :], in_=Xin[rows, :]),
                         reads=[(xin_name, s_, j) for j in range(4)], writes=[kx], dma=True)
                    if g_pre is not None:
                        P.op("act", lambda e: e.activation(out=hn[:, sl, :], in_=xt[:, sl, :], func=AF.Square,
                                                           accum_out=stat[:, sl, 0:1]),
                             reads=[kx], writes=[khn, kst])
                        P.op("act", lambda e: e.activation(out=stat[:, sl, 1:2], in_=stat[:, sl, 0:1], func=AF.Sqrt,
                                                           bias=C["eps"][:], scale=1.0 / D),
                             reads=[kst], writes=[kst])
                        P.op("dve", lambda e: e.reciprocal(out=stat[:, sl, 2:3], in_=stat[:, sl, 1:2]),
                             reads=[kst], writes=[kst])
                        P.op("dve", lambda e: e.scalar_tensor_tensor(
                            out=xt[:, sl, :], in0=xt[:, sl, :], scalar=stat[:, sl, 2:3], in1=gbp[:],
                            op0=ALU.mult, op1=ALU.mult), reads=[kx, kst, tag + "gbp"], writes=[kx])
                        P.op("sp", lambda e: e.dma_start(out=Xout[rows, :], in_=xt[:, sl, :]),
                             reads=[kx], writes=[(xout_name, s_, j) for j in range(4)], dma=True)
                    P.op("act", lambda e: e.activation(out=hn[:, sl, :], in_=xt[:, sl, :], func=AF.Square,
                                                       accum_out=stat[:, sl, 3:4]),
                         reads=[kx], writes=[khn, kst])
                    P.op("act", lambda e: e.activation(out=stat[:, sl, 4:5], in_=stat[:, sl, 3:4], func=AF.Sqrt,
                                                       bias=C["eps"][:], scale=1.0 / D),
                         reads=[kst], writes=[kst])
                    P.op("dve", lambda e: e.reciprocal(out=stat[:, sl, 5:6], in_=stat[:, sl, 4:5]),
                         reads=[kst], writes=[kst])
                    P.op("dve", lambda e: e.scalar_tensor_tensor(
                        out=hn[:, sl, :], in0=xt[:, sl, :], scalar=stat[:, sl, 5:6], in1=gbf[:],
                        op0=ALU.mult, op1=ALU.mult), reads=[kx, kst, tag + "gbf"], writes=[khn])
                return front

            def mk_pe(ts):
                def pe():
                    sl = slots[ts]
                    khn = (tag + "hn", sl)
                    for half in range(2):
                        def tr(e, half=half):
                            pv = ps[:, 7, :].bitcast(BF16)
                            ins = None
                            for j in range(8):
                                kc = half * 8 + j
                                ins = e.transpose(out=pv[:, j * 128:(j + 1) * 128], in_=hn[:, sl, kc * 128:(kc + 1) * 128],
                                                  identity=C["ident"][:])
                            return ins
                        P.op("pe", tr, reads=[khn, "ident"], writes=[("ps", 7)])
                        if half == 0:
                            P.op("dve", lambda e, half=half: e.tensor_copy(
                                out=xnT[:, half * 8:(half + 1) * 8, ts * 128:(ts + 1) * 128],
                                in_=ps[:, 7, :].bitcast(BF16).rearrange("p (j t) -> p j t", j=8)),
                                reads=[("ps", 7)], writes=[(tag + "xnT", ts)])
                        else:
                            P.op("act", lambda e, half=half: e.copy(
                                out=xnT[:, half * 8:(half + 1) * 8, ts * 128:(ts + 1) * 128],
                                in_=ps[:, 7, :].bitcast(BF16).rearrange("p (j t) -> p j t", j=8)),
                                reads=[("ps", 7)], writes=[(tag + "xnT", ts)])
                return pe

            steps = []
            for k in range(nsub):
                steps.append(mk_front(k))
                if k >= 1:
                    steps.append(mk_pe(k - 1))
            steps.append(mk_pe(nsub - 1))
            return steps

        def load_wgu(fc):
            sl = cnt["wgu"] % 3
            cnt["wgu"] += 1
            P.op("sp", lambda e: e.dma_start(out=wgu[:, sl, 0], in_=Wg[fc]), reads=[(kg, fc)],
                 writes=[(tag + "wg", sl)], dma=True)
            P.op("sp", lambda e: e.dma_start(out=wgu[:, sl, 1], in_=Wu[fc]), reads=[(ku, fc)],
                 writes=[(tag + "wu", sl)], dma=True)
            return sl

        def load_wd(dg, fg):
            sl = cnt["wd"] % 3
            cnt["wd"] += 1
            P.op("sp", lambda e: e.dma_start(out=wd[:, sl], in_=Wd[dg, :, fg * 4:(fg + 1) * 4, :]),
                 reads=[(kd, dg)], writes=[(tag + "wd", sl)], dma=True)
            return sl

        def gateup(nsub):
            NT = nsub * 128
            halves = [(0, min(512, NT))] + ([(512, NT)] if NT > 512 else [])
            pend = [load_wgu(fc) for fc in range(min(3, FC))]
            for fc in range(FC):
                sl = pend.pop(0)
                par = fc % 2
                for gu in range(2):
                    for hi, (a, b) in enumerate(halves):
                        bank = 4 * par + 2 * gu + hi
                        def mm(e, sl=sl, gu=gu, a=a, b=b, bank=bank):
                            ins = None
                            for kc in range(KC):
                                ins = e.matmul(ps[:, bank, 0:b - a], lhsT=wgu[:, sl, gu, kc, :], rhs=xnT[:, kc, a:b],
                                               start=(kc == 0), stop=(kc == KC - 1))
                            return ins
                        P.op("pe", mm, reads=[(tag + ("wg" if gu == 0 else "wu"), sl)] +
                             [(tag + "xnT", t) for t in range(a // 128, b // 128)], writes=[("ps", bank)])
                for hi, (a, b) in enumerate(halves):
                    ss = cnt["sil"] % 2
                    cnt["sil"] += 1
                    bg, bu = 4 * par + hi, 4 * par + 2 + hi
                    P.op("act", lambda e, ss=ss, a=a, b=b, bg=bg: e.activation(
                        out=sil[:, ss, 0:b - a], in_=ps[:, bg, 0:b - a], func=AF.Silu),
                        reads=[("ps", bg)], writes=[(tag + "sil", ss)])
                    P.op("dve", lambda e, ss=ss, a=a, b=b, bu=bu, fc=fc: e.tensor_tensor(
                        out=hT[:, fc, a:b], in0=ps[:, bu, 0:b - a], in1=sil[:, ss, 0:b - a], op=ALU.mult),
                        reads=[("ps", bu), (tag + "sil", ss)], writes=[(tag + "hT", fc)])
                if fc + 3 < FC:
                    pend.append(load_wgu(fc + 3))

        def down(t0, nsub, steps):
            Xres = Xout if g_pre is not None else Xin
            xres_name = Xres.tensor.name
            NFG = FC // 4
            seq = [(dg, fg) for dg in range(4) for fg in range(NFG)]

            def load_xr(dg):
                cols = slice(dg * 512, (dg + 1) * 512)
                for ts in range(nsub):
                    s_ = t0 + ts
                    rows = slice(s_ * 128, (s_ + 1) * 128)
                    P.op("sp", lambda e, ts=ts, rows=rows, cols=cols: e.dma_start(out=xr[:, ts, :], in_=Xres[rows, cols]),
                         reads=[(xres_name, s_, dg)], writes=[(tag + "xr", ts)], dma=True)

            pend = [load_wd(*seq[i]) for i in range(3)]
            load_xr(0)
            for i, (dg, fg) in enumerate(seq):
                sl = pend.pop(0)
                hkeys = [(tag + "hT", fg * 4 + fl) for fl in range(4)]
                if fg == 0:
                    for ts in range(nsub):
                        def mm1(e, sl=sl, fg=fg, ts=ts):
                            ins = None
                            for fl in range(4):
                                fc = fg * 4 + fl
                                ins = e.matmul(ps[:, ts, :], lhsT=hT[:, fc, ts * 128:(ts + 1) * 128], rhs=wd[:, sl, fl, :],
                                               start=(fc == 0), stop=(fc == FC - 1))
                            return ins
                        P.op("pe", mm1, reads=[(tag + "wd", sl)] + hkeys, writes=[("ps", ts)])
                else:
                    def mm(e, sl=sl, fg=fg):
                        ins = None
                        for fl in range(4):
                            fc = fg * 4 + fl
                            for ts in range(nsub):
                                ins = e.matmul(ps[:, ts, :], lhsT=hT[:, fc, ts * 128:(ts + 1) * 128], rhs=wd[:, sl, fl, :],
                                               start=(fc == 0), stop=(fc == FC - 1))
                        return ins
                    P.op("pe", mm, reads=[(tag + "wd", sl)] + hkeys, writes=[("ps", ts) for ts in range(nsub)])
                if i + 3 < len(seq):
                    pend.append(load_wd(*seq[i + 3]))
                if fg == NFG - 1:
                    cols = slice(dg * 512, (dg + 1) * 512)
                    for ts in range(nsub):
                        s_ = t0 + ts
                        rows = slice(s_ * 128, (s_ + 1) * 128)
                        P.op("dve", lambda e, ts=ts: e.scalar_tensor_tensor(
                            out=xr[:, ts, :], in0=ps[:, ts, :], scalar=0.5, in1=xr[:, ts, :], op0=ALU.mult, op1=ALU.add),
                            reads=[("ps", ts), (tag + "xr", ts)], writes=[(tag + "xr", ts)])
                        P.op("sp", lambda e, ts=ts, rows=rows, cols=cols: e.dma_start(out=Xout[rows, cols], in_=xr[:, ts, :]),
                             reads=[(tag + "xr", ts)], writes=[(xout_name, s_, dg)], dma=True)
                    if dg + 1 < 4:
                        load_xr(dg + 1)
                if dg >= 2 and steps:
                    steps.pop(0)()
            while steps:
                steps.pop(0)()

        for ti, (t0, nsub) in enumerate(tiles):
            if ti == 0:
                for s in prologue_steps(t0, nsub):
                    s()
            gateup(nsub)
            nxt = tiles[ti + 1] if ti + 1 < len(tiles) else None
            down(t0, nsub, prologue_steps(*nxt) if nxt is not None else [])
        P.flush()


class NormCtx:
    def __init__(self, P, nc, st, C, tag, gvec, out_dt, nslot=2):
        self.P, self.C, self.tag = P, C, tag
        self.xt = st.enter_context(nc.sbuf_tensor(tag + "nxt", [128, nslot, D], F32))
        self.hn = st.enter_context(nc.sbuf_tensor(tag + "nhn", [128, nslot, D], out_dt))
        self.junk = st.enter_context(nc.sbuf_tensor(tag + "njunk", [128, D], BF16))
        self.gb = st.enter_context(nc.sbuf_tensor(tag + "ngb", [128, D], F32))
        self.stat = st.enter_context(nc.sbuf_tensor(tag + "nstat", [128, nslot, 4], F32))
        self.n = 0
        self.nslot = nslot
        gb = self.gb
        P.op("sp", lambda e: e.dma_start(out=gb[:], in_=gvec.partition_broadcast(128)), writes=[tag + "ngb"], dma=True)
        P.const.add(tag + "ngb")

    def load(self, X, s_):
        P, tag = self.P, self.tag
        sl = self.n % self.nslot
        self.n += 1
        xt = self.xt
        rows = slice(s_ * 128, (s_ + 1) * 128)
        P.op("sp", lambda e: e.dma_start(out=xt[:, sl, :], in_=X[rows, :]),
             reads=[(X.tensor.name, s_, j) for j in range(4)], writes=[(tag + "nxt", sl)], dma=True)
        return sl

    def compute(self, sl, valid=None):
        P, C, tag = self.P, self.C, self.tag
        xt, hn, junk, gb, stat = self.xt, self.hn, self.junk, self.gb, self.stat
        kx, khn, kst = (tag + "nxt", sl), (tag + "nhn", sl), (tag + "nstat", sl)
        P.op("act", lambda e: e.activation(out=junk[:], in_=xt[:, sl, :], func=AF.Square, accum_out=stat[:, sl, 0:1]),
             reads=[kx], writes=[tag + "njunk", kst])
        P.op("act", lambda e: e.activation(out=stat[:, sl, 1:2], in_=stat[:, sl, 0:1], func=AF.Sqrt,
                                           bias=C["eps"][:], scale=1.0 / D), reads=[kst], writes=[kst])
        P.op("dve", lambda e: e.reciprocal(out=stat[:, sl, 2:3], in_=stat[:, sl, 1:2]), reads=[kst], writes=[kst])
        if valid is not None:
            vap, vkey = valid
            P.op("dve", lambda e: e.tensor_tensor(out=stat[:, sl, 2:3], in0=stat[:, sl, 2:3], in1=vap, op=ALU.mult),
                 reads=[kst, vkey], writes=[kst])
        P.op("dve", lambda e: e.scalar_tensor_tensor(out=hn[:, sl, :], in0=xt[:, sl, :], scalar=stat[:, sl, 2:3],
                                                     in1=gb[:], op0=ALU.mult, op1=ALU.mult),
             reads=[kx, kst, tag + "ngb"], writes=[khn])
        return sl, khn

    def run(self, X, s_, valid=None):
        return self.compute(self.load(X, s_), valid)


def transpose_to(P, ps, C, src_ap_fn, src_key, nchunk, dst_fn, dst_keys, bank=7, fp32=False):
    per = 4 if fp32 else 8
    ident = C["identf"] if fp32 else C["ident"]
    ikey = "identf" if fp32 else "ident"
    r = 0
    for c0 in range(0, nchunk, per):
        n = min(per, nchunk - c0)

        def tr(e, c0=c0, n=n):
            pv = ps[:, bank, :] if fp32 else ps[:, bank, :].bitcast(BF16)
            ins = None
            for j in range(n):
                ins = e.transpose(out=pv[:, j * 128:(j + 1) * 128], in_=src_ap_fn(c0 + j), identity=ident[:])
            return ins
        P.op("pe", tr, reads=[src_key, ikey], writes=[("ps", bank)])

        def ev(e, c0=c0, n=n, r=r):
            pv = ps[:, bank, :] if fp32 else ps[:, bank, :].bitcast(BF16)
            src = pv[:, 0:n * 128].rearrange("p (j t) -> p j t", j=n)
            if r % 2 == 0:
                return e.tensor_copy(out=dst_fn(c0, n), in_=src)
            return e.copy(out=dst_fn(c0, n), in_=src)
        P.op("dve" if r % 2 == 0 else "act", ev, reads=[("ps", bank)], writes=dst_keys)
        r += 1


def final_phase(P, nc, C, tag, X, out, st_lo, st_hi, gvec):
    with contextlib.ExitStack() as st:
        N = NormCtx(P, nc, st, C, tag, gvec, F32, nslot=3)
        blocks = list(range(st_lo, st_hi))
        pre = {0: N.load(X, blocks[0])}
        for bi, s_ in enumerate(blocks):
            if bi + 1 < len(blocks):
                pre[bi + 1] = N.load(X, blocks[bi + 1])
            sl, khn = N.compute(pre.pop(bi))
            o = s_ - st_lo
            P.op("sp", lambda e, sl=sl, o=o: e.dma_start(out=out[o * 128:(o + 1) * 128, :], in_=N.hn[:, sl, :]),
                 reads=[khn], writes=[("out", o)], dma=True)
        P.flush()


def pool_phase(P, nc, ps, C, tag, X, HT, st_lo, st_hi, gvec, Wp, wkey, scale_vec, valid_d, invcnt_d, NTOK):
    with contextlib.ExitStack() as st:
        N = NormCtx(P, nc, st, C, tag + "A", gvec, F32, nslot=2)
        hts = st.enter_context(nc.sbuf_tensor(tag + "hts", [128, 2, KC, 128], F32))
        vt = st.enter_context(nc.sbuf_tensor(tag + "vt", [128, 2, 1], F32))
        zt = st.enter_context(nc.sbuf_tensor(tag + "zt", [128, KC, 8], F32))
        P.op("dve", lambda e: e.memset(zt[:], 0.0), writes=[tag + "zt"])
        P.op("sp", lambda e: e.dma_start(out=HT[:, :, st_lo * 128:st_lo * 128 + 8].rearrange("c p t -> p c t"), in_=zt[:]),
             reads=[tag + "zt"], writes=[(tag + "HTpad", 0)], dma=True)
        P.op("sp", lambda e: e.dma_start(out=HT[:, :, st_hi * 128 + 8:st_hi * 128 + 16].rearrange("c p t -> p c t"), in_=zt[:]),
             reads=[tag + "zt"], writes=[(tag + "HTpad", 1)], dma=True)
        blocks = list(range(st_lo, st_hi))
        pre = {}

        def issue(bi):
            hs, s_ = bi % 2, blocks[bi]
            P.op("sp", lambda e: e.dma_start(out=vt[:, hs, :], in_=valid_d[s_ * 128:(s_ + 1) * 128, :]),
                 writes=[(tag + "vt", hs)], dma=True)
            pre[bi] = N.load(X, s_)
        issue(0)
        for i, s_ in enumerate(blocks):
            hs = i % 2
            if i + 1 < len(blocks):
                issue(i + 1)
            sl, khn = N.compute(pre.pop(i), valid=(vt[:, hs, :], (tag + "vt", hs)))
            transpose_to(P, ps, C, lambda c, sl=sl: N.hn[:, sl, c * 128:(c + 1) * 128], khn, KC,
                         lambda c0, n, hs=hs: hts[:, hs, c0:c0 + n, :], [(tag + "hts", hs)], bank=7, fp32=True)
            P.op("sp", lambda e, hs=hs, s_=s_: e.dma_start(
                out=HT[:, :, 8 + s_ * 128:8 + (s_ + 1) * 128].rearrange("c p t -> p c t"), in_=hts[:, hs]),
                reads=[(tag + "hts", hs)], writes=[(tag + "HT", s_)], dma=True)
        P.flush()
    TS = 4
    tiles = split_tiles(st_lo, st_hi, TS)
    xname = X.tensor.name
    with contextlib.ExitStack() as st:
        def sb(name, shape, dt):
            return st.enter_context(nc.sbuf_tensor(tag + name, shape, dt))
        W = TS * 128
        hg = sb("hg", [128, 2, 4, W + 16], F32)
        ca = sb("ca", [128, 4, W + 16], F32)
        cb = sb("cb", [128, 4, W + 16], F32)
        ic = sb("ic", [128, 2, W], F32)
        pT = sb("pT", [128, 2, 4, W], BF16)
        wp = sb("wp", [128, 4, 4, 512], BF16)
        sc = sb("sc", [128, D], F32)
        xr = sb("xr", [128, 3, 512], F32)
        xo = sb("xo", [128, 3, 512], F32)
        tm = sb("tm", [128, 2, 512], F32)
        P.op("sp", lambda e: e.dma_start(out=sc[:], in_=scale_vec.partition_broadcast(128)), writes=[tag + "sc"], dma=True)
        for g in range(4):
            P.op("sp", lambda e, g=g: e.dma_start(out=wp[:, g], in_=Wp[g]), reads=[(wkey, g)], writes=[tag + "wp"], dma=True)
        P.const.add(tag + "sc")
        k = 0
        r = 0
        items = [(t0, nsub, g) for (t0, nsub) in tiles for g in range(4)]

        def issue_b(kk):
            t0, nsub, g = items[kk]
            sl, n, tok0 = kk % 2, nsub * 128, t0 * 128
            P.op("sp", lambda e: e.dma_start(
                out=hg[:, sl, :, 0:n + 16], in_=HT[4 * g:4 * g + 4, :, tok0:tok0 + n + 16].rearrange("c p t -> p c t")),
                reads=[(tag + "HT", s_) for s_ in range(max(st_lo, t0 - 1), min(st_hi, t0 + nsub + 1))] +
                [(tag + "HTpad", 0), (tag + "HTpad", 1)], writes=[(tag + "hg", sl)], dma=True)
            P.op("sp", lambda e: e.dma_start(
                out=ic[:, sl, 0:n], in_=invcnt_d[g:g + 1, tok0:tok0 + n].partition_broadcast(128)),
                writes=[(tag + "ic", sl)], dma=True)
        issue_b(0)
        for (t0, nsub) in tiles:
            n = nsub * 128
            tok0 = t0 * 128
            for g in range(4):
                w = 2 << g
                sl = k % 2
                k += 1
                if k < len(items):
                    issue_b(k)
                khg, kic, kpT = (tag + "hg", sl), (tag + "ic", sl), (tag + "pT", sl)
                L = n + 16
                cur, curk = (lambda a, b, sl=sl: hg[:, sl, :, a:b]), khg
                bufs = [(ca, tag + "ca"), (cb, tag + "cb")]
                step = 1
                for lv in range(g + 1):
                    dst, dk = bufs[lv % 2]
                    Ln = L - (2 * step - 1)
                    P.op("dve",
                         lambda e, cur=cur, dst=dst, step=step, Ln=Ln: e.tensor_tensor(
                             out=dst[:, :, 0:Ln], in0=cur(0, Ln), in1=cur(step, step + Ln), op=ALU.add),
                         reads=[curk], writes=[dk])
                    cur, curk = (lambda a, b, dst=dst: dst[:, :, a:b]), dk
                    step *= 2
                off = 8 - w // 2
                dst, dk = bufs[(g + 1) % 2]
                P.op("dve", lambda e, cur=cur, dst=dst, off=off, n=n, sl=sl: e.tensor_tensor(
                    out=dst[:, :, 0:n], in0=cur(off, off + n),
                    in1=ic[:, sl, 0:n].unsqueeze(1).broadcast_to([128, 4, n]), op=ALU.mult),
                    reads=[curk, kic], writes=[dk])
                P.op("dve", lambda e, dst=dst, n=n, sl=sl: e.tensor_tensor(
                    out=pT[:, sl, :, 0:n], in0=dst[:, :, 0:n], in1=hg[:, sl, :, 8:8 + n], op=ALU.subtract),
                    reads=[dk, khg], writes=[kpT])
                for ts in range(nsub):
                    s_ = t0 + ts
                    bank = (r % 6)
                    rs = r % 3
                    tsl = r % 2
                    r += 1

                    def mm(e, sl=sl, ts=ts, g=g, bank=bank):
                        ins = None
                        for cc in range(4):
                            ins = e.matmul(ps[:, bank, :], lhsT=pT[:, sl, cc, ts * 128:(ts + 1) * 128], rhs=wp[:, g, cc, :],
                                           start=(cc == 0), stop=(cc == 3))
                        return ins
                    P.op("pe", mm, reads=[kpT, tag + "wp"], writes=[("ps", bank)])
                    rows = slice(s_ * 128, (s_ + 1) * 128)
                    cols = slice(g * 512, (g + 1) * 512)
                    P.op("sp", lambda e, rs=rs, rows=rows, cols=cols: e.dma_start(out=xr[:, rs, :], in_=X[rows, cols]),
                         reads=[(xname, s_, g)], writes=[(tag + "xr", rs)], dma=True)
                    P.op("dve", lambda e, tsl=tsl, bank=bank, cols=cols: e.tensor_tensor(
                        out=tm[:, tsl, :], in0=ps[:, bank, :], in1=sc[:, cols], op=ALU.mult),
                        reads=[("ps", bank), tag + "sc"], writes=[(tag + "tm", tsl)])
                    P.op("dve", lambda e, tsl=tsl, rs=rs: e.tensor_tensor(
                        out=xo[:, rs, :], in0=tm[:, tsl, :], in1=xr[:, rs, :], op=ALU.add),
                        reads=[(tag + "tm", tsl), (tag + "xr", rs)], writes=[(tag + "xo", rs)])
                    P.op("sp", lambda e, rs=rs, rows=rows, cols=cols: e.dma_start(out=X[rows, cols], in_=xo[:, rs, :]),
                         reads=[(tag + "xo", rs)], writes=[(xname, s_, g)], dma=True)
        P.flush()


def emit_out_proj(P, ps, tag, X, t0, nsub, aT, aT_keys, nk, w_ap_fn, w_keys_fn, xr, xo, cnt, banks):
    xname = X.tensor.name
    for dg in range(4):
        cols = slice(dg * 512, (dg + 1) * 512)
        for ts in range(nsub):
            s_ = t0 + ts
            bank = banks[cnt["b"] % len(banks)]
            cnt["b"] += 1
            rs = cnt["r"] % 3
            cnt["r"] += 1

            def mm(e, ts=ts, dg=dg, bank=bank):
                ins = None
                for k in range(nk):
                    ins = e.matmul(ps[:, bank, :], lhsT=aT[:, k, ts * 128:(ts + 1) * 128], rhs=w_ap_fn(dg, k),
                                   start=(k == 0), stop=(k == nk - 1))
                return ins
            P.op("pe", mm, reads=list(aT_keys) + list(w_keys_fn(dg)), writes=[("ps", bank)])
            rows = slice(s_ * 128, (s_ + 1) * 128)
            P.op("sp", lambda e, rs=rs, rows=rows, cols=cols: e.dma_start(out=xr[:, rs, :], in_=X[rows, cols]),
                 reads=[(xname, s_, dg)], writes=[(tag + "xr", rs)], dma=True)
            P.op("dve", lambda e, rs=rs, bank=bank: e.tensor_tensor(out=xo[:, rs, :], in0=ps[:, bank, :], in1=xr[:, rs, :],
                                                                     op=ALU.add),
                 reads=[("ps", bank), (tag + "xr", rs)], writes=[(tag + "xo", rs)])
            P.op("sp", lambda e, rs=rs, rows=rows, cols=cols: e.dma_start(out=X[rows, cols], in_=xo[:, rs, :]),
                 reads=[(tag + "xo", rs)], writes=[(xname, s_, dg)], dma=True)


def sgu_phase(P, nc, ps, C, tag, X, st_lo, st_hi, gvec, Win, kin, Wout, kout, vgain, w_s, b_s):
    TS = 3
    DS = 4096
    tiles = split_tiles(st_lo, st_hi, TS)
    with contextlib.ExitStack() as st:
        def sb(name, shape, dt):
            return st.enter_context(nc.sbuf_tensor(tag + name, shape, dt))
        N = NormCtx(P, nc, st, C, tag, gvec, BF16, nslot=2)
        xnT = sb("xnT", [128, KC, TS * 128], BF16)
        wsl = sb("wsl", [128, 2, 16, 512], BF16)
        U = sb("U", [128, TS, DS], BF16)
        V = sb("V", [128, TS, DS], BF16)
        vg = sb("vg", [128, DS], F32)
        G = sb("G", [128, DS], BF16)
        GT = sb("GT", [128, 32, TS * 128], BF16)
        wsT = sb("wsT", [128, 16, 128], BF16)
        bs = sb("bs", [128, 16], F32)
        xr = sb("xr", [128, 3, 512], F32)
        xo = sb("xo", [128, 3, 512], F32)
        vst = sb("vst", [128, 4], F32)
        P.op("sp", lambda e: e.dma_start(out=vg[:], in_=vgain.partition_broadcast(128)), writes=[tag + "vg"], dma=True)
        P.op("sp", lambda e: e.dma_start(out=bs[:], in_=b_s.rearrange("g i -> i g"), allow_slow_non_contiguous=True),
             writes=[tag + "bs"], dma=True)
        wstage = N.xt
        P.op("sp", lambda e: e.dma_start(out=wstage[:, 0, :].rearrange("p (g j) -> p g j", g=16),
                                         in_=w_s.rearrange("g i j -> i g j")), writes=[(tag + "nxt", 0)], dma=True)
        transpose_to(P, ps, C, lambda c: wstage[:, 0, c * 128:(c + 1) * 128], (tag + "nxt", 0), 16,
                     lambda c0, n: wsT[:, c0:c0 + n, :], [tag + "wsT"], bank=7, fp32=True)
        for k in ("vg", "bs", "wsT"):
            P.const.add(tag + k)
        cnt = {"w": 0, "b": 0, "r": 0}

        def loadw(src, keys):
            sl = cnt["w"] % 2
            cnt["w"] += 1
            P.op("sp", lambda e: e.dma_start(out=wsl[:, sl], in_=src), reads=keys, writes=[(tag + "wsl", sl)], dma=True)
            return sl

        for (t0, nsub) in tiles:
            for ts in range(nsub):
                sl, khn = N.run(X, t0 + ts)
                transpose_to(P, ps, C, lambda c, sl=sl: N.hn[:, sl, c * 128:(c + 1) * 128], khn, KC,
                             lambda c0, n, ts=ts: xnT[:, c0:c0 + n, ts * 128:(ts + 1) * 128], [(tag + "xnT", ts)], bank=7)
            nxt = loadw(Win[0], [(kin, 0)])
            for cg in range(16):
                sl = nxt
                if cg + 1 < 16:
                    nxt = loadw(Win[cg + 1], [(kin, cg + 1)])
                for ts in range(nsub):
                    bank = cnt["b"] % 6
                    cnt["b"] += 1

                    def mm(e, sl=sl, ts=ts, bank=bank):
                        ins = None
                        for kc in range(KC):
                            ins = e.matmul(ps[:, bank, :], lhsT=xnT[:, kc, ts * 128:(ts + 1) * 128], rhs=wsl[:, sl, kc, :],
                                           start=(kc == 0), stop=(kc == KC - 1))
                        return ins
                    P.op("pe", mm, reads=[(tag + "xnT", ts), (tag + "wsl", sl)], writes=[("ps", bank)])
                    dst = U if cg < 8 else V
                    dk = (tag + ("U" if cg < 8 else "V"), ts, cg % 8)
                    c0 = (cg % 8) * 512
                    P.op("act", lambda e, dst=dst, ts=ts, c0=c0, bank=bank: e.activation(
                        out=dst[:, ts, c0:c0 + 512], in_=ps[:, bank, :], func=AF.Gelu), reads=[("ps", bank)], writes=[dk])
            for ts in range(nsub):
                vk = [(tag + "V", ts, j) for j in range(8)]
                P.op("act", lambda e, ts=ts: e.activation(out=G[:], in_=V[:, ts, :], func=AF.Square, accum_out=vst[:, 0:1]),
                     reads=vk, writes=[tag + "G", tag + "vst"])
                P.op("act", lambda e: e.activation(out=vst[:, 1:2], in_=vst[:, 0:1], func=AF.Sqrt, bias=C["eps"][:],
                                                   scale=1.0 / DS), reads=[tag + "vst"], writes=[tag + "vst"])
                P.op("dve", lambda e: e.reciprocal(out=vst[:, 2:3], in_=vst[:, 1:2]), reads=[tag + "vst"], writes=[tag + "vst"])
                P.op("dve", lambda e, ts=ts: e.scalar_tensor_tensor(out=V[:, ts, :], in0=V[:, ts, :], scalar=vst[:, 2:3],
                                                                    in1=vg[:], op0=ALU.mult, op1=ALU.mult),
                     reads=vk + [tag + "vst", tag + "vg"], writes=vk)
            for ts in range(nsub):
                vk = [(tag + "V", ts, j) for j in range(8)]
                uk = [(tag + "U", ts, j) for j in range(8)]
                for gp in range(8):
                    bank = cnt["b"] % 6
                    cnt["b"] += 1

                    def mm(e, ts=ts, gp=gp, bank=bank):
                        ins = None
                        for q in range(2):
                            g = 2 * gp + q
                            ins = e.matmul(ps[:, bank, q * 256:(q + 1) * 256], lhsT=wsT[:, g, :],
                                           rhs=V[:, ts, g * 256:(g + 1) * 256], start=True, stop=True)
                        return ins
                    P.op("pe", mm, reads=vk + [tag + "wsT"], writes=[("ps", bank)])
                    for q in range(2):
                        g = 2 * gp + q
                        P.op("dve", lambda e, ts=ts, g=g, q=q, bank=bank: e.scalar_tensor_tensor(
                            out=G[:, g * 256:(g + 1) * 256], in0=ps[:, bank, q * 256:(q + 1) * 256], scalar=bs[:, g:g + 1],
                            in1=U[:, ts, g * 256:(g + 1) * 256], op0=ALU.add, op1=ALU.mult),
                            reads=[("ps", bank), tag + "bs"] + uk, writes=[tag + "G"])
                transpose_to(P, ps, C, lambda c: G[:, c * 128:(c + 1) * 128], tag + "G", 32,
                             lambda c0, n, ts=ts: GT[:, c0:c0 + n, ts * 128:(ts + 1) * 128], [(tag + "GT", ts)], bank=7)
            xname = X.tensor.name
            nxt = loadw(Wout[0, :, 0:16, :], [(kout, 0)])
            for dg in range(4):
                cols = slice(dg * 512, (dg + 1) * 512)
                for half in range(2):
                    sl = nxt
                    nh = dg * 2 + half + 1
                    if nh < 8:
                        nxt = loadw(Wout[nh // 2, :, (nh % 2) * 16:(nh % 2) * 16 + 16, :], [(kout, nh // 2)])
                    for ts in range(nsub):
                        bank = ts

                        def mm(e, sl=sl, ts=ts, half=half, bank=bank):
                            ins = None
                            for k in range(16):
                                fc = half * 16 + k
                                ins = e.matmul(ps[:, bank, :], lhsT=GT[:, fc, ts * 128:(ts + 1) * 128], rhs=wsl[:, sl, k, :],
                                               start=(fc == 0), stop=(fc == 31))
                            return ins
                        P.op("pe", mm, reads=[(tag + "GT", ts), (tag + "wsl", sl)], writes=[("ps", bank)])
                for ts in range(nsub):
                    s_ = t0 + ts
                    rs = cnt["r"] % 3
                    cnt["r"] += 1
                    rows = slice(s_ * 128, (s_ + 1) * 128)
                    P.op("sp", lambda e, rs=rs, rows=rows, cols=cols: e.dma_start(out=xr[:, rs, :], in_=X[rows, cols]),
                         reads=[(xname, s_, dg)], writes=[(tag + "xr", rs)], dma=True)
                    P.op("dve", lambda e, rs=rs, ts=ts: e.tensor_tensor(out=xo[:, rs, :], in0=ps[:, ts, :], in1=xr[:, rs, :],
                                                                         op=ALU.add),
                         reads=[("ps", ts), (tag + "xr", rs)], writes=[(tag + "xo", rs)])
                    P.op("sp", lambda e, rs=rs, rows=rows, cols=cols: e.dma_start(out=X[rows, cols], in_=xo[:, rs, :]),
                         reads=[(tag + "xo", rs)], writes=[(xname, s_, dg)], dma=True)
        P.flush()


NH = 64
HD = 32


def na_qkv_phase(P, nc, ps, C, tag, X, st_lo, st_hi, gvec, Wqk, kqk, Wv, kv, qgain, kgain, QT, VX):
    TS = 4
    tiles = split_tiles(st_lo, st_hi, TS)
    with contextlib.ExitStack() as st:
        def sb(name, shape, dt):
            return st.enter_context(nc.sbuf_tensor(tag + name, shape, dt))
        N = NormCtx(P, nc, st, C, tag, gvec, BF16, nslot=2)
        xnT = sb("xnT", [128, KC, TS * 128], BF16)
        wq = sb("wq", [128, 3, KC, 128], BF16)
        wv = sb("wv", [128, 2, KC, 512], BF16)
        sq = sb("sq", [128, 2, 512], BF16)
        lnt = sb("lnt", [128, 2, 512], F32)
        rstd = sb("rstd", [128, 2, 512], F32)
        qo = sb("qo", [128, 3, 512], BF16)
        vx = sb("vx", [128, TS, NH, HD + 1], BF16)
        gqk = sb("gqk", [128, 2], F32)
        for hl in range(4):
            P.op("sp", lambda e, hl=hl: e.dma_start(out=gqk[32 * hl:32 * hl + 32, 0:1], in_=qgain), writes=[tag + "gqk"], dma=True)
            P.op("sp", lambda e, hl=hl: e.dma_start(out=gqk[32 * hl:32 * hl + 32, 1:2], in_=kgain), writes=[tag + "gqk"], dma=True)
        P.op("dve", lambda e: e.tensor_scalar(out=gqk[:, 0:1], in0=gqk[:, 0:1], scalar1=float(HD) ** -0.5, scalar2=None,
                                              op0=ALU.mult), reads=[tag + "gqk"], writes=[tag + "gqk"])
        P.op("dve", lambda e: e.memset(vx[:, :, :, HD:HD + 1], 1.0), writes=[(tag + "vx", ts) for ts in range(TS)])
        P.const.add(tag + "gqk")
        cnt = {"wq": 0, "wv": 0, "s": 0, "q": 0, "b": 0, "ev": 0}

        def load_wq(fo):
            sl = cnt["wq"] % 3
            cnt["wq"] += 1
            P.op("sp", lambda e: e.dma_start(out=wq[:, sl], in_=Wqk[fo]), reads=[(kqk, fo)], writes=[(tag + "wq", sl)], dma=True)
            return sl

        def load_wv(cg):
            sl = cnt["wv"] % 2
            cnt["wv"] += 1
            P.op("sp", lambda e: e.dma_start(out=wv[:, sl], in_=Wv[cg]), reads=[(kv, cg)], writes=[(tag + "wv", sl)], dma=True)
            return sl

        for (t0, nsub) in tiles:
            n = nsub * 128
            tok0 = t0 * 128
            for ts in range(nsub):
                sl, khn = N.run(X, t0 + ts)
                transpose_to(P, ps, C, lambda c, sl=sl: N.hn[:, sl, c * 128:(c + 1) * 128], khn, KC,
                             lambda c0, n_, ts=ts: xnT[:, c0:c0 + n_, ts * 128:(ts + 1) * 128], [(tag + "xnT", ts)], bank=7)
            xk = [(tag + "xnT", ts) for ts in range(nsub)]
            pend = [load_wq(fo) for fo in range(3)]

            def stage1(fo, n, tok0):
                sl = pend.pop(0)
                bA = cnt["b"] % 3
                bB = 3 + cnt["b"] % 2
                cnt["b"] += 1
                ss = cnt["s"] % 2
                cnt["s"] += 1
                qs = cnt["q"] % 3
                cnt["q"] += 1

                def mm(e):
                    ins = None
                    for kc in range(KC):
                        ins = e.matmul(ps[:, bA, 0:n], lhsT=wq[:, sl, kc, :], rhs=xnT[:, kc, 0:n], start=(kc == 0), stop=(kc == KC - 1))
                    return ins
                P.op("pe", mm, reads=xk + [(tag + "wq", sl)], writes=[("ps", bA)])
                if fo + 3 < 32:
                    pend.append(load_wq(fo + 3))
                P.op("act", lambda e: e.activation(out=sq[:, ss, 0:n], in_=ps[:, bA, 0:n], func=AF.Square),
                     reads=[("ps", bA)], writes=[(tag + "sq", ss)])
                return (fo, bA, bB, ss, qs, n, tok0)

            def stage2(fo, bA, bB, ss, qs, n, tok0):
                P.op("pe", lambda e: e.matmul(ps[:, bB, 0:n], lhsT=C["bd"][:], rhs=sq[:, ss, 0:n], start=True, stop=True),
                     reads=[(tag + "sq", ss), "bd"], writes=[("ps", bB)])
                P.op("act", lambda e: e.activation(out=lnt[:, ss, 0:n], in_=ps[:, bB, 0:n], func=AF.Ln,
                                                   bias=C["eps"][:], scale=1.0),
                     reads=[("ps", bB)], writes=[(tag + "lnt", ss)])
                P.op("act", lambda e: e.activation(out=rstd[:, ss, 0:n], in_=lnt[:, ss, 0:n], func=AF.Exp, scale=-0.5),
                     reads=[(tag + "lnt", ss)], writes=[(tag + "rstd", ss)])
                gi = 0 if fo < 16 else 1
                P.op("dve", lambda e: e.scalar_tensor_tensor(
                    out=qo[:, qs, 0:n], in0=ps[:, bA, 0:n], scalar=gqk[:, gi:gi + 1], in1=rstd[:, ss, 0:n],
                    op0=ALU.mult, op1=ALU.mult), reads=[("ps", bA), (tag + "rstd", ss), tag + "gqk"], writes=[(tag + "qo", qs)])
                P.op("sp", lambda e: e.dma_start(out=QT[fo, :, tok0:tok0 + n], in_=qo[:, qs, 0:n]),
                     reads=[(tag + "qo", qs)], writes=[(tag + "QT", fo, t0)], dma=True)

            prev = None
            for fo in range(32):
                cur = stage1(fo, n, tok0)
                if prev is not None:
                    stage2(*prev)
                prev = cur
            stage2(*prev)
            nxt = load_wv(0)
            for cg in range(4):
                sl = nxt
                if cg + 1 < 4:
                    nxt = load_wv(cg + 1)
                for ts in range(nsub):
                    bank = 5 + cnt["ev"] % 2
                    ev = cnt["ev"]
                    cnt["ev"] += 1

                    def mm(e, sl=sl, ts=ts, bank=bank):
                        ins = None
                        for kc in range(KC):
                            ins = e.matmul(ps[:, bank, :], lhsT=xnT[:, kc, ts * 128:(ts + 1) * 128], rhs=wv[:, sl, kc, :],
                                           start=(kc == 0), stop=(kc == KC - 1))
                        return ins
                    P.op("pe", mm, reads=[(tag + "xnT", ts), (tag + "wv", sl)], writes=[("ps", bank)])
                    if ev % 2 == 0:
                        P.op("dve", lambda e, ts=ts, cg=cg, bank=bank: e.tensor_copy(
                            out=vx[:, ts, 16 * cg:16 * cg + 16, 0:HD], in_=ps[:, bank, :].rearrange("p (h d) -> p h d", h=16)),
                            reads=[("ps", bank)], writes=[(tag + "vx", ts)])
                    else:
                        P.op("act", lambda e, ts=ts, cg=cg, bank=bank: e.copy(
                            out=vx[:, ts, 16 * cg:16 * cg + 16, 0:HD], in_=ps[:, bank, :].rearrange("p (h d) -> p h d", h=16)),
                            reads=[("ps", bank)], writes=[(tag + "vx", ts)])
            for ts in range(nsub):
                s_ = t0 + ts
                P.op("sp", lambda e, ts=ts, s_=s_: e.dma_start(out=VX[s_ * 128:(s_ + 1) * 128, :],
                                                                  in_=vx[:, ts].rearrange("p h d -> p (h d)")),
                     reads=[(tag + "vx", ts)], writes=[(tag + "VX", s_)], dma=True)
        P.flush()


def na_attn_phase(P, nc, ps, C, tag, NB, q_lo, q_hi, QT, VX, AO, TB, TBI, RVd, qkv_tag, qkv_tiles):
    NTOK = NB * 128
    INT_LO, INT_HI = 7, 35
    NS = 3
    SB = (0, 2, 6)
    DEFER = 2
    with contextlib.ExitStack() as st:
        def sb(name, shape, dt):
            return st.enter_context(nc.sbuf_tensor(tag + name, shape, dt))
        Kc = sb("Kc", [128, 2, NTOK], BF16)
        Qc = sb("Qc", [128, 2, NTOK], BF16)
        Vc = sb("Vc", [128, 2, NB, 4, HD + 1], BF16)
        Gt = sb("Gt", [128, 2, 4, 896], F32)
        RVc = sb("RVc", [128, NB, 14], BF16)
        bnd = [i for i in range(q_lo, q_hi) if not (INT_LO <= i < INT_HI)]
        bidx = {i: k for k, i in enumerate(bnd)}
        RVa = sb("RVa", [128, len(bnd), 14, 64], BF16)
        Gti = sb("Gti", [128, 2, 4, 640], F32)
        Sg = sb("Sg", [128, NS, 896], F32)
        PT = sb("PT", [128, NS, 896], BF16)
        rec = sb("rec", [128, 2, 4], F32)
        aot = sb("aot", [128, 2, 4, HD], BF16)
        P.op("dve", lambda e: e.memset(RVc[:], 0.0), writes=[tag + "RVc"])
        for hl in range(4):
            P.op("sp", lambda e, hl=hl: e.dma_start(out=RVc[32 * hl:32 * hl + 2, :, :], in_=RVd.rearrange("b r c -> r b c")),
                 writes=[tag + "RVc"], dma=True)
        for i in bnd:
            P.op("act", lambda e, i=i: e.copy(
                out=RVa[:, bidx[i]], in_=RVc[:, i, :].unsqueeze(2).broadcast_to([128, 14, 64])),
                reads=[tag + "RVc"], writes=[tag + "RVa"])
        P.const.add(tag + "RVc")
        P.const.add(tag + "RVa")
        cnt = {"s": 0, "x": 0, "o": 0}
        queue = []

        def loads(c):
            cs = c % 2
            qk_keys = [(qkv_tag + "QT", fo, t0) for fo in (c, 16 + c) for (t0, _) in qkv_tiles]
            P.op("sp", lambda e: e.dma_start(out=Kc[:, cs, :], in_=QT[16 + c]), reads=qk_keys,
                 writes=[(tag + "Kc", cs)], dma=True)
            P.op("sp", lambda e: e.dma_start(out=Qc[:, cs, :], in_=QT[c]), reads=qk_keys,
                 writes=[(tag + "Qc", cs)], dma=True)
            P.op("sp", lambda e: e.dma_start(
                out=Vc[:, cs], in_=VX.rearrange("(j p) (h d) -> p j h d", p=128, d=HD + 1)[:, :, 4 * c:4 * c + 4, :]),
                reads=[(qkv_tag + "VX", s_) for s_ in range(NB)], writes=[(tag + "Vc", cs)], dma=True)
            P.op("sp", lambda e: e.dma_start(out=Gt[:, cs], in_=TB[4 * c:4 * c + 4].rearrange("h p f -> p h f")),
                 writes=[(tag + "Gt", cs)], dma=True)
            P.op("sp", lambda e: e.dma_start(out=Gti[:, cs], in_=TBI[4 * c:4 * c + 4].rearrange("h p f -> p h f")),
                 writes=[(tag + "Gti", cs)], dma=True)

        def s_stage(c, cs, i, hl, interior, xs, olist, a0, a1):
            ssl = cnt["s"] % NS
            cnt["s"] += 1
            b0 = SB[ssl]
            pb = 32 * hl
            if interior:
                def mm_s(e):
                    Sv = ps[:, b0:b0 + 2, :].rearrange("p a c -> p (a c)")
                    ins = None
                    for o in olist:
                        j = i - 3 + o
                        ins = e.matmul(Sv[:, o * 128:(o + 1) * 128], lhsT=Kc[pb:pb + 32, cs, j * 128:(j + 1) * 128],
                                       rhs=Qc[pb:pb + 32, cs, i * 128:(i + 1) * 128], start=True, stop=True,
                                       tile_position=(pb, 0), skip_group_check=True)
                    return ins
                P.op("pe", mm_s, reads=[(tag + "Kc", cs), (tag + "Qc", cs)], writes=[("ps", b0), ("ps", b0 + 1)])
                P.op("dve", lambda e: e.tensor_tensor(
                    out=Sg[:, ssl, a0:a1], in0=ps[:, b0:b0 + 2, :].rearrange("p a c -> p (a c)")[:, a0:a1],
                    in1=Gti[:, cs, hl, :], op=ALU.add),
                    reads=[("ps", b0), ("ps", b0 + 1), (tag + "Gti", cs)], writes=[(tag + "Sg", ssl)])
            else:
                def mm_s(e):
                    Sv = ps[:, b0:b0 + 2, :].rearrange("p a c -> p (a c)")
                    rv = RVa[:, bidx[i]].rearrange("p a b -> p (a b)")
                    ins = None
                    for (lo, hi) in ((a0, min(a1, 512)), (max(a0, 512), a1)):
                        if hi > lo:
                            ins = e.matmul(Sv[:, lo:hi], lhsT=C["ind4"][pb:pb + 2, :], rhs=rv[pb:pb + 2, lo:hi],
                                           start=True, stop=False, tile_position=(pb, 0), skip_group_check=True)
                    for o in olist:
                        j = i - 3 + o
                        ins = e.matmul(Sv[:, o * 128:(o + 1) * 128], lhsT=Kc[pb:pb + 32, cs, j * 128:(j + 1) * 128],
                                       rhs=Qc[pb:pb + 32, cs, i * 128:(i + 1) * 128], start=False, stop=True,
                                       tile_position=(pb, 0), skip_group_check=True)
                    return ins
                P.op("pe", mm_s, reads=[tag + "RVa", (tag + "Kc", cs), (tag + "Qc", cs), "ind4"],
                     writes=[("ps", b0), ("ps", b0 + 1)])
                P.op("dve", lambda e: e.tensor_tensor(
                    out=Sg[:, ssl, a0:a1], in0=ps[:, b0:b0 + 2, :].rearrange("p a c -> p (a c)")[:, a0:a1],
                    in1=Gt[:, cs, hl, a0:a1], op=ALU.add),
                    reads=[("ps", b0), ("ps", b0 + 1), (tag + "Gt", cs)], writes=[(tag + "Sg", ssl)])
            P.op("act", lambda e: e.activation(out=PT[:, ssl, a0:a1], in_=Sg[:, ssl, a0:a1], func=AF.Exp),
                 reads=[(tag + "Sg", ssl)], writes=[(tag + "PT", ssl)])
            return ssl

        def o_stage(c, cs, i, hl, ssl, olist, ob, osl):
            def mm_o(e):
                ins = None
                for o in olist:
                    j = i - 3 + o
                    ins = e.matmul(ps[:, ob, hl * (HD + 1):(hl + 1) * (HD + 1)], lhsT=PT[:, ssl, o * 128:(o + 1) * 128],
                                   rhs=Vc[:, cs, j, hl, :], start=(o == olist[0]), stop=(o == olist[-1]))
                return ins
            P.op("pe", mm_o, reads=[(tag + "PT", ssl), (tag + "Vc", cs)], writes=[("ps", ob)])
            if hl != 3:
                return
            Ov_fn = lambda: ps[:, ob, 0:4 * (HD + 1)].rearrange("p (h d) -> p h d", h=4)
            P.op("dve", lambda e: e.tensor_scalar(
                out=rec[:, osl, :], in0=Ov_fn()[:, :, HD], scalar1=1e-30, scalar2=None, op0=ALU.add),
                reads=[("ps", ob)], writes=[(tag + "rec", osl)])
            P.op("dve", lambda e: e.reciprocal(out=rec[:, osl, :], in_=rec[:, osl, :]),
                 reads=[(tag + "rec", osl)], writes=[(tag + "rec", osl)])
            P.op("dve", lambda e: e.tensor_tensor(
                out=aot[:, osl], in0=Ov_fn()[:, :, 0:HD], in1=rec[:, osl, :].unsqueeze(2).broadcast_to([128, 4, HD]),
                op=ALU.mult), reads=[("ps", ob), (tag + "rec", osl)], writes=[(tag + "aot", osl)])
            P.op("sp", lambda e: e.dma_start(
                out=AO[i * 128:(i + 1) * 128, c * 128:(c + 1) * 128], in_=aot[:, osl].rearrange("p h d -> p (h d)")),
                reads=[(tag + "aot", osl)], writes=[(tag + "AO", i, c)], dma=True)

        loads(0)
        for c in range(16):
            cs = c % 2
            for i in range(q_lo, q_hi):
                interior = INT_LO <= i < INT_HI
                if interior:
                    olist = [1, 2, 3, 4, 5]
                else:
                    jlo, jhi = max(0, i - 3), min(NB - 1, i + 3)
                    olist = list(range(jlo - (i - 3), jhi - (i - 3) + 1))
                xs = None
                a0, a1 = olist[0] * 128, (olist[-1] + 1) * 128
                ob = 4 + cnt["o"] % 2
                osl = cnt["o"] % 2
                cnt["o"] += 1
                for hl in range(4):
                    ssl = s_stage(c, cs, i, hl, interior, xs, olist, a0, a1)
                    queue.append((c, cs, i, hl, ssl, olist, ob, osl))
                    if len(queue) > DEFER:
                        o_stage(*queue.pop(0))
                if i == q_lo and c + 1 < 16:
                    loads(c + 1)
        while queue:
            o_stage(*queue.pop(0))
        P.flush()


def na_out_phase(P, nc, ps, C, tag, X, st_lo, st_hi, AO, at_tag, Wo, ko):
    TS = 4
    tiles = split_tiles(st_lo, st_hi, TS)
    with contextlib.ExitStack() as st:
        def sb(name, shape, dt):
            return st.enter_context(nc.sbuf_tensor(tag + name, shape, dt))
        wo = sb("wo", [128, 4, KC, 512], BF16)
        at = sb("at", [128, 2, D], BF16)
        aT = sb("aT", [128, KC, TS * 128], BF16)
        xr = sb("xr", [128, 3, 512], F32)
        xo = sb("xo", [128, 3, 512], F32)
        for dg in range(4):
            P.op("sp", lambda e, dg=dg: e.dma_start(out=wo[:, dg], in_=Wo[dg]), reads=[(ko, dg)], writes=[(tag + "wo", dg)], dma=True)
        cnt = {"b": 0, "r": 0, "a": 0}
        for (t0, nsub) in tiles:
            for ts in range(nsub):
                s_ = t0 + ts
                sl = cnt["a"] % 2
                cnt["a"] += 1
                P.op("sp", lambda e, sl=sl, s_=s_: e.dma_start(out=at[:, sl, :], in_=AO[s_ * 128:(s_ + 1) * 128, :]),
                     reads=[(at_tag + "AO", s_, c) for c in range(16)], writes=[(tag + "at", sl)], dma=True)
                transpose_to(P, ps, C, lambda c, sl=sl: at[:, sl, c * 128:(c + 1) * 128], (tag + "at", sl), KC,
                             lambda c0, n, ts=ts: aT[:, c0:c0 + n, ts * 128:(ts + 1) * 128], [(tag + "aT", ts)], bank=7)
            emit_out_proj(P, ps, tag, X, t0, nsub, aT, [(tag + "aT", ts) for ts in range(nsub)], KC,
                          lambda dg, k: wo[:, dg, k, :], lambda dg: [(tag + "wo", dg)], xr, xo, cnt, [0, 1, 2, 3, 4, 5])
        P.flush()


NEG = -30000.0
GRID_W = 64
ROWS = 256


def build_tb(rpb, interior=False):
    kr = np.arange(2)[:, None, None, None, None]
    kc = np.arange(64)[None, :, None, None, None]
    o = np.arange(7)[None, None, :, None, None]
    qr = np.arange(2)[None, None, None, :, None]
    qc = np.arange(64)[None, None, None, None, :]
    dr = 2 * (o - 3) + kr - qr + 7
    dc = kc - qc + 15
    cs = np.clip(qc - 8, 0, GRID_W - 16)
    cvalid = (kc >= cs) & (kc < cs + 16)
    dr_b, dc_b, cv_b = np.broadcast_arrays(dr, dc, cvalid)
    dc_c = np.clip(dc_b, 0, 30)
    g = rpb[:, dr_b, dc_c]
    if interior:
        dd = 2 * (o - 3) + kr - qr
        rv_b = np.broadcast_arrays((dd >= -4) & (dd <= 3), cv_b)[0]
        g = np.where((cv_b & rv_b)[None], g, np.float32(NEG)).astype(np.float32)
        return np.ascontiguousarray(g[:, :, :, 1:6].reshape(64, 128, 640))
    g = np.where(cv_b[None], g, np.float32(NEG)).astype(np.float32)
    return np.ascontiguousarray(g.reshape(64, 128, 896))


def build_rv(NB, row0):
    rv = np.full((NB, 2, 7, 2), NEG, np.float32)
    for i in range(NB):
        for o in range(7):
            for kr in range(2):
                for qr in range(2):
                    Rq = row0 + 2 * i + qr
                    Rk = row0 + 2 * (i + o - 3) + kr
                    if 0 <= Rq < ROWS and 0 <= Rk < ROWS:
                        rs = min(max(Rq - 4, 0), ROWS - 8)
                        if rs <= Rk < rs + 8:
                            rv[i, kr, o, qr] = 0.0
    return rv.reshape(NB, 2, 14).astype(ml_dtypes.bfloat16)


def build_consts():
    ident = np.eye(128, dtype=np.float32)
    p = np.arange(128)
    bd = np.where((p[:, None] // 32) == (p[None, :] // 32), np.float32(1.0 / 32), np.float32(0.0))
    ind4 = np.zeros((128, 128), np.float32)
    for hl in range(4):
        for r in range(2):
            ind4[32 * hl + r, :] = (p // 64 == r)
    return dict(ident=ident.astype(ml_dtypes.bfloat16), identf=ident, bd=bd.astype(ml_dtypes.bfloat16),
                ind4=ind4.astype(ml_dtypes.bfloat16))


DEPTH = 4
NB = 41
NTOK = NB * 128
OWN_LO, OWN_HI = 5, 37
SEQ = 16384
BATCH = 2


_DBG = [None]


def build_program():
    nc = bass.Bass("TRN2", target_bir_lowering=False)

    def din(name, shape, dt=F32):
        return nc.dram_tensor(name, list(shape), dt, kind="ExternalInput").ap()

    def dscr(name, shape, dt):
        return nc.dram_tensor(name, list(shape), dt, kind="Internal").ap()

    x_loc = din("x_loc", [NTOK, D])
    I = {}
    for nm in ("ffn1_norm", "mix_norm", "ffn2_norm", "out_norm"):
        I[nm] = din(nm, [DEPTH, D])
    for nm in ("ffn1_w_gate", "ffn1_w_up", "ffn2_w_gate", "ffn2_w_up"):
        I[nm] = din(nm, [DEPTH, D, DFF])
    for nm in ("ffn1_w_down", "ffn2_w_down"):
        I[nm] = din(nm, [DEPTH, DFF, D])
    I["na_w_qkv"] = din("na_w_qkv", [2, D, 3 * D])
    I["na_q_gain"] = din("na_q_gain", [2, HD])
    I["na_k_gain"] = din("na_k_gain", [2, HD])
    I["na_w_o"] = din("na_w_o", [2, D, D])
    I["sgu_w_in"] = din("sgu_w_in", [1, D, 8192])
    I["sgu_v_gain"] = din("sgu_v_gain", [1, 4096])
    I["sgu_w_s"] = din("sgu_w_s", [1, 16, 128, 128])
    I["sgu_b_s"] = din("sgu_b_s", [1, 16, 128])
    I["sgu_w_out"] = din("sgu_w_out", [1, 4096, D])
    I["pool_w"] = din("pool_w", [1, 4, 512, 512])
    I["pool_scale"] = din("pool_scale", [1, D])
    TB = [din("tb0", [NH, 128, 896]), din("tb1", [NH, 128, 896])]
    TBI = [din("tb0i", [NH, 128, 640]), din("tb1i", [NH, 128, 640])]
    RVd = din("rv", [NB, 2, 14], BF16)
    valid_d = din("valid", [NTOK, 1])
    invcnt_d = din("invcnt", [4, NTOK])
    cin = {k: din("c_" + k, [128, 128], F32 if k == "identf" else BF16) for k in ("ident", "identf", "bd", "ind4")}
    out = nc.dram_tensor("out", [(OWN_HI - OWN_LO) * 128, D], F32, kind="ExternalOutput").ap()

    X = dscr("X", [NTOK, D], F32)
    QT = dscr("QT", [32, 128, NTOK], BF16)
    VX = dscr("VX", [NTOK, NH * (HD + 1)], BF16)
    AO = dscr("AO", [NTOK, D], BF16)
    HT = dscr("HT", [KC, 128, NTOK + 16], F32)
    W = {}
    for l in range(DEPTH):
        for f in (1, 2):
            W["g%d%d" % (f, l)] = dscr("wg%d%d" % (f, l), [FC, 128, KC, 128], BF16)
            W["u%d%d" % (f, l)] = dscr("wu%d%d" % (f, l), [FC, 128, KC, 128], BF16)
            W["d%d%d" % (f, l)] = dscr("wd%d%d" % (f, l), [4, 128, FC, 512], BF16)
    for j in range(2):
        W["qk%d" % j] = dscr("wqk%d" % j, [32, 128, KC, 128], BF16)
        W["v%d" % j] = dscr("wv%d" % j, [4, 128, KC, 512], BF16)
        W["o%d" % j] = dscr("wo%d" % j, [4, 128, KC, 512], BF16)
    W["sin"] = dscr("wsin", [16, 128, KC, 512], BF16)
    W["sout"] = dscr("wsout", [4, 128, 32, 512], BF16)
    W["pw"] = dscr("wpw", [4, 128, 4, 512], BF16)

    P = Prog(nc)

    def cast_ffn(f, l):
        def go():
            cast_cols(P, I["ffn%d_w_gate" % f][l], W["g%d%d" % (f, l)], "g%d%d" % (f, l), D, DFF, 128)
            cast_cols(P, I["ffn%d_w_up" % f][l], W["u%d%d" % (f, l)], "u%d%d" % (f, l), D, DFF, 128)
            cast_cols(P, I["ffn%d_w_down" % f][l], W["d%d%d" % (f, l)], "d%d%d" % (f, l), DFF, D, 512)
        return go

    def cast_na(j):
        def go():
            cast_cols(P, I["na_w_qkv"][j][:, 0:2 * D], W["qk%d" % j], "qk%d" % j, D, 2 * D, 128)
            cast_cols(P, I["na_w_qkv"][j][:, 2 * D:3 * D], W["v%d" % j], "v%d" % j, D, D, 512)
            cast_cols(P, I["na_w_o"][j], W["o%d" % j], "o%d" % j, D, D, 512)
        return go

    def cast_sgu():
        cast_cols(P, I["sgu_w_in"][0], W["sin"], "sin", D, 8192, 512)
        cast_cols(P, I["sgu_w_out"][0], W["sout"], "sout", 4096, D, 512)

    def cast_pool():
        for g in range(4):
            P.op("pool", lambda e, g=g: e.dma_start(out=W["pw"][g], in_=I["pool_w"][0][g].rearrange("(cc p) d -> p cc d", p=128)),
                 writes=[("pw", g)], dma=True)

    with contextlib.ExitStack() as st:
        ps = st.enter_context(nc.psum_tensor("ps", [128, 8, 512], F32))
        C = {}
        for k in ("ident", "identf", "bd", "ind4"):
            C[k] = st.enter_context(nc.sbuf_tensor(k + "_sb", [128, 128], F32 if k == "identf" else BF16))
            P.op("sp", lambda e, k=k: e.dma_start(out=C[k][:], in_=cin[k]), writes=[k], dma=True)
            P.const.add(k)
        C["eps"] = st.enter_context(nc.sbuf_tensor("eps_sb", [128, 1], F32))
        P.op("dve", lambda e: e.memset(C["eps"][:], EPS), writes=["eps"])
        P.const.add("eps")
        cast_ffn(1, 0)()
        P.flush()

        def vec(nm, l):
            return I[nm][l:l + 1, :]

        def ffn(f, l, lo, hi, Xin, g_pre, nxt_cast):
            if nxt_cast is not None:
                nxt_cast()
            ffn_phase(P, nc, ps, C, "f%d%d" % (f, l), Xin, X, lo, hi, W["g%d%d" % (f, l)], W["u%d%d" % (f, l)],
                      W["d%d%d" % (f, l)], ("g%d%d" % (f, l), "u%d%d" % (f, l), "d%d%d" % (f, l)), g_pre,
                      vec("ffn%d_norm" % f, l))

        def na(j, l, kv_lo, kv_hi, q_lo, q_hi, nxt_cast):
            nxt_cast()
            qt = "q%d" % j
            na_qkv_phase(P, nc, ps, C, qt, X, kv_lo, kv_hi, vec("mix_norm", l), W["qk%d" % j], "qk%d" % j, W["v%d" % j],
                         "v%d" % j, I["na_q_gain"][j:j + 1, :].rearrange("o d -> d o"),
                         I["na_k_gain"][j:j + 1, :].rearrange("o d -> d o"), QT, VX)
            na_attn_phase(P, nc, ps, C, "a%d" % j, NB, q_lo, q_hi, QT, VX, AO, TB[j], TBI[j], RVd, qt, split_tiles(kv_lo, kv_hi, 4))
            na_out_phase(P, nc, ps, C, "o%d" % j, X, q_lo, q_hi, AO, "a%d" % j, W["o%d" % j], "o%d" % j)

        ffn(1, 0, 0, NB, x_loc, None, cast_na(0))
        if _DBG[0] == "ffn":
            final_phase(P, nc, C, "fin", X, out, OWN_LO, OWN_HI, vec("out_norm", 3))
            P.close()
            return nc
        na(0, 0, 0, NB, 2, 39, cast_ffn(2, 0))
        if _DBG[0] == "na":
            final_phase(P, nc, C, "fin", X, out, OWN_LO, OWN_HI, vec("out_norm", 3))
            P.close()
            return nc
        ffn(2, 0, 2, 39, X, None, cast_ffn(1, 1))
        ffn(1, 1, 2, 39, X, vec("out_norm", 0), cast_sgu)
        cast_ffn(2, 1)()
        sgu_phase(P, nc, ps, C, "sg", X, 2, 39, vec("mix_norm", 1), W["sin"], "sin", W["sout"], "sout",
                  I["sgu_v_gain"], I["sgu_w_s"][0], I["sgu_b_s"][0])
        ffn(2, 1, 2, 39, X, None, cast_ffn(1, 2))
        ffn(1, 2, 2, 39, X, vec("out_norm", 1), cast_pool)
        cast_ffn(2, 2)()
        pool_phase(P, nc, ps, C, "pl", X, HT, 2, 39, vec("mix_norm", 2), W["pw"], "pw", I["pool_scale"], valid_d, invcnt_d, NTOK)
        ffn(2, 2, 3, 39, X, None, cast_ffn(1, 3))
        ffn(1, 3, 3, 39, X, vec("out_norm", 2), cast_na(1))
        na(1, 3, 3, 39, OWN_LO, OWN_HI, cast_ffn(2, 3))
        ffn(2, 3, OWN_LO, OWN_HI, X, None, None)
        final_phase(P, nc, C, "fin", X, out, OWN_LO, OWN_HI, vec("out_norm", 3))
    P.close()
    return nc


def host_inputs(inputs, core):
    b, q = divmod(core, 4)
    row0 = 64 * q - 10
    x = np.asarray(inputs["x"], dtype=np.float32)
    x_loc = np.zeros((NTOK, D), np.float32)
    t_lo, t_hi = row0 * 64, row0 * 64 + NTOK
    a, c = max(t_lo, 0), min(t_hi, SEQ)
    x_loc[a - t_lo:c - t_lo] = x[b, a:c]
    t = np.arange(NTOK) + t_lo
    valid = ((t >= 0) & (t < SEQ)).astype(np.float32).reshape(NTOK, 1)
    invcnt = np.ones((4, NTOK), np.float32)
    for g, w in enumerate((2, 4, 8, 16)):
        lo = np.clip(t - w // 2, 0, SEQ)
        hi = np.clip(t + w // 2, 0, SEQ)
        cnt = np.maximum(hi - lo, 1).astype(np.float32)
        invcnt[g] = np.float32(1.0) / cnt
    m = {"x_loc": x_loc, "valid": valid, "invcnt": invcnt, "rv": build_rv(NB, row0)}
    return m


_NC_CACHE = {}


def _run(inputs, cores):
    if "nc" not in _NC_CACHE:
        _NC_CACHE["nc"] = build_program()
    nc = _NC_CACHE["nc"]
    shared = {}
    for k, v in inputs.items():
        if k != "x":
            shared[k] = np.ascontiguousarray(np.asarray(v, dtype=np.float32))
    rpb = shared.pop("na_rpb")
    shared["tb0"] = build_tb(rpb[0])
    shared["tb1"] = build_tb(rpb[1])
    shared["tb0i"] = build_tb(rpb[0], True)
    shared["tb1i"] = build_tb(rpb[1], True)
    for k, v in build_consts().items():
        shared["c_" + k] = v
    in_maps = []
    for c in cores:
        m = dict(shared)
        m.update(host_inputs(inputs, c))
        in_maps.append(m)
    res = run_bass_kernel_spmd(nc, in_maps, core_ids=list(range(len(cores))))
    return [np.asarray(r["out"]) for r in res.results]


def kernel(**inputs):
    outs = _run(inputs, list(range(8)))
    full = np.zeros((BATCH, SEQ, D), np.float32)
    for c, o in enumerate(outs):
        b, q = divmod(c, 4)
        full[b, q * 4096:(q + 1) * 4096] = o
    return full
```
